# Optimizing a Trainium2 kernel written in Bass

```python
import jax, jax.numpy as jnp
from jax import lax
import numpy as np

D_MODEL = 1024
BATCH = 8
SEQ = 2048
DEPTH = 4
DEC_BATCH = 2
DEC_SEQ = 8192
PAST_LEN = 128

GRID_W = 64
N_MIXERS = 3
N_LAYERS_A = (DEPTH + 2) // 3
N_LAYERS_B = (DEPTH + 1) // 3
N_LAYERS_C = DEPTH // 3
Q_BLOCK = 128
ROPE_THETA = 10000.0
EPS = 1e-6

MLA_HEADS = 8
MLA_Q_LORA = 384
MLA_KV_LORA = 256
MLA_NOPE = 128
MLA_ROPE = 64
MLA_V = 128

GQA_Q_HEADS = 8
GQA_KV_HEADS = 2
GQA_HEAD_DIM = 128

NA_HEADS = 16
NA_HEAD_DIM = 64
NA_WIN_ROWS_MAX = 8
NA_WIN_COLS = 16
NA_REL_ROWS = 2 * NA_WIN_ROWS_MAX - 1
NA_REL_COLS = 2 * NA_WIN_COLS - 1

D_FF = 4096
CONV_WIDTH = 3
PLE_DIM = 256

kernel_name = "hybrid_mla_gqa_natten_encoder"


def rms_norm(x, g):
    xf = x.astype(jnp.float32)
    y = xf * lax.rsqrt(jnp.mean(xf * xf, axis=-1, keepdims=True) + EPS)
    return (y * g.astype(jnp.float32)).astype(x.dtype)


def rms_norm_plain(x):
    xf = x.astype(jnp.float32)
    return (xf * lax.rsqrt(jnp.mean(xf * xf, axis=-1, keepdims=True) + EPS)).astype(x.dtype)


def rope_tables(pos, dim):
    inv = 1.0 / (ROPE_THETA ** (jnp.arange(0, dim, 2, dtype=jnp.float32) / dim))
    ang = pos.astype(jnp.float32)[:, None] * inv[None, :]
    return jnp.cos(ang), jnp.sin(ang)


def apply_rope(x, cos, sin):
    half = x.shape[-1] // 2
    x1, x2 = x[..., :half], x[..., half:]
    c = cos.astype(x.dtype)
    s = sin.astype(x.dtype)
    return jnp.concatenate([x1 * c - x2 * s, x2 * c + x1 * s], axis=-1)


def to_blocks(x, qb):
    b, s = x.shape[:2]
    return jnp.moveaxis(x.reshape((b, s // qb, qb) + x.shape[2:]), 1, 0)


def from_blocks(y):
    nb, b, qb = y.shape[:3]
    return jnp.moveaxis(y, 0, 1).reshape((b, nb * qb) + y.shape[3:])


def mla_mixer(x, w_down, q_norm, kv_norm, w_uq, w_ukv, w_o):
    b, s, _ = x.shape
    down = x @ w_down
    cq = rms_norm(down[..., :MLA_Q_LORA], q_norm)
    ckv = rms_norm(down[..., MLA_Q_LORA:MLA_Q_LORA + MLA_KV_LORA], kv_norm)
    k_rope = down[..., MLA_Q_LORA + MLA_KV_LORA:]
    q = (cq @ w_uq).reshape(b, s, MLA_HEADS, MLA_NOPE + MLA_ROPE)
    kv = (ckv @ w_ukv).reshape(b, s, MLA_HEADS, MLA_NOPE + MLA_V)
    cos, sin = rope_tables(jnp.arange(s), MLA_ROPE)
    scale = (MLA_NOPE + MLA_ROPE) ** -0.5
    q_nope = q[..., :MLA_NOPE] * scale
    q_rope = apply_rope(q[..., MLA_NOPE:], cos[:, None], sin[:, None]) * scale
    k_nope = kv[..., :MLA_NOPE]
    v = kv[..., MLA_NOPE:]
    k_rope = apply_rope(k_rope, cos, sin)

    def attend(qs):
        qn, qr = qs
        sc = (jnp.einsum('bqhd,bkhd->bhqk', qn, k_nope, preferred_element_type=jnp.float32)
              + jnp.einsum('bqhr,bkr->bhqk', qr, k_rope, preferred_element_type=jnp.float32))
        pr = jax.nn.softmax(sc, axis=-1).astype(v.dtype)
        return jnp.einsum('bhqk,bkhd->bqhd', pr, v)

    o = from_blocks(lax.map(attend, (to_blocks(q_nope, Q_BLOCK), to_blocks(q_rope, Q_BLOCK))))
    return o.reshape(b, s, MLA_HEADS * MLA_V) @ w_o


def gqa_mixer(x, w_qkv, q_norm, k_norm, w_o):
    b, s, _ = x.shape
    hd = GQA_HEAD_DIM
    qkv = x @ w_qkv
    q = qkv[..., :GQA_Q_HEADS * hd].reshape(b, s, GQA_Q_HEADS, hd)
    k = qkv[..., GQA_Q_HEADS * hd:(GQA_Q_HEADS + GQA_KV_HEADS) * hd].reshape(b, s, GQA_KV_HEADS, hd)
    v = qkv[..., (GQA_Q_HEADS + GQA_KV_HEADS) * hd:].reshape(b, s, GQA_KV_HEADS, hd)
    q = rms_norm(q, q_norm)
    k = rms_norm(k, k_norm)
    t = jnp.arange(s)
    half = hd // 2
    rc, rs = rope_tables(t // GRID_W, half)
    cc, cs = rope_tables(t % GRID_W, half)

    def axial(z):
        return jnp.concatenate([apply_rope(z[..., :half], rc[:, None], rs[:, None]),
                                apply_rope(z[..., half:], cc[:, None], cs[:, None])], axis=-1)

    q = axial(q) * hd ** -0.5
    k = axial(k)
    g = GQA_Q_HEADS // GQA_KV_HEADS
    q = q.reshape(b, s, GQA_KV_HEADS, g, hd)

    def attend(qb):
        sc = jnp.einsum('bqkgd,bskd->bkgqs', qb, k, preferred_element_type=jnp.float32)
        pr = jax.nn.softmax(sc, axis=-1).astype(v.dtype)
        return jnp.einsum('bkgqs,bskd->bqkgd', pr, v)

    o = from_blocks(lax.map(attend, to_blocks(q, Q_BLOCK)))
    return o.reshape(b, s, GQA_Q_HEADS * hd) @ w_o


def na_indices(s):
    rows = s // GRID_W
    wr = min(NA_WIN_ROWS_MAX, rows)
    wc = NA_WIN_COLS
    t = np.arange(s)
    r = t // GRID_W
    c = t % GRID_W
    r0 = np.clip(r - wr // 2, 0, rows - wr)
    c0 = np.clip(c - wc // 2, 0, GRID_W - wc)
    kr = r0[:, None] + np.arange(wr)[None, :]
    kc = c0[:, None] + np.arange(wc)[None, :]
    idx = (kr[:, :, None] * GRID_W + kc[:, None, :]).reshape(s, wr * wc)
    rel = (((kr - r[:, None] + NA_WIN_ROWS_MAX - 1)[:, :, None] * NA_REL_COLS)
           + (kc - c[:, None] + NA_WIN_COLS - 1)[:, None, :]).reshape(s, wr * wc)
    return jnp.asarray(idx, dtype=jnp.int32), jnp.asarray(rel, dtype=jnp.int32)


def na_mixer(x, w_qkv, rpb, w_o):
    b, s, _ = x.shape
    qkv = (x @ w_qkv).reshape(b, s, 3, NA_HEADS, NA_HEAD_DIM)
    q = qkv[:, :, 0] * NA_HEAD_DIM ** -0.5
    k = qkv[:, :, 1]
    v = qkv[:, :, 2]
    idx, rel = na_indices(s)
    bias_tab = rpb.reshape(NA_HEADS, NA_REL_ROWS * NA_REL_COLS)
    nb = s // GRID_W

    def attend(blk):
        qb, ib, rb = blk
        kb = k[:, ib]
        vb = v[:, ib]
        sc = (jnp.einsum('bqhd,bqkhd->bhqk', qb, kb, preferred_element_type=jnp.float32)
              + bias_tab[:, rb].astype(jnp.float32)[None])
        pr = jax.nn.softmax(sc, axis=-1).astype(v.dtype)
        return jnp.einsum('bhqk,bqkhd->bqhd', pr, vb)

    o = from_blocks(lax.map(attend, (to_blocks(q, GRID_W),
                                     idx.reshape(nb, GRID_W, -1),
                                     rel.reshape(nb, GRID_W, -1))))
    return o.reshape(b, s, NA_HEADS * NA_HEAD_DIM) @ w_o


def conv_ffn(x, w_in, conv_w, conv_b, w_out):
    gu = x @ w_in
    g, u = gu[..., :D_FF], gu[..., D_FF:]
    gp = jnp.pad(g, ((0, 0), (1, 1), (0, 0)))
    g = gp[:, :-2] * conv_w[0] + gp[:, 1:-1] * conv_w[1] + gp[:, 2:] * conv_w[2] + conv_b
    return (jax.nn.gelu(g, approximate=True) * u) @ w_out


def run_trunk(x, p, w):
    for i in range(DEPTH):
        kind, j = i % N_MIXERS, i // N_MIXERS
        hn = rms_norm(x, w['norm_mix_pre'][i])
        if kind == 0:
            m = mla_mixer(hn, w['mla_w_down'][j], w['mla_q_norm'][j], w['mla_kv_norm'][j],
                          w['mla_w_uq'][j], w['mla_w_ukv'][j], w['mla_w_o'][j])
        elif kind == 1:
            m = gqa_mixer(hn, w['gqa_w_qkv'][j], w['gqa_q_norm'][j], w['gqa_k_norm'][j], w['gqa_w_o'][j])
        else:
            m = na_mixer(hn, w['na_w_qkv'][j], w['na_rpb'][j], w['na_w_o'][j])
        x = x + rms_norm(m, w['norm_mix_post'][i])
        f = conv_ffn(rms_norm(x, w['norm_ffn_pre'][i]), w['ffn_w_in'][i], w['ffn_conv_w'][i],
                     w['ffn_conv_b'][i], w['ffn_w_out'][i])
        x = x + rms_norm(f, w['norm_ffn_post'][i])
        e = p[i] @ w['ple_w_proj'][i]
        gate = jax.nn.sigmoid(rms_norm_plain(x) @ w['ple_w_gate'][i])
        x = x + rms_norm(gate * e, w['ple_norm'][i])
    return x


def setup_inputs(seed: int = 0) -> dict:
    key = jax.random.key(seed)
    ks = jax.random.split(key, 32)
    f32 = jnp.float32

    def lin(k, shape):
        return jax.random.normal(k, shape, f32) * (shape[-2] ** -0.5)

    def gain(k, shape):
        return 1.0 + 0.05 * jax.random.normal(k, shape, f32)

    d = D_MODEL
    return {
        "x_prompt": jax.random.normal(ks[0], (BATCH, SEQ, d), f32),
        "x_sample": jax.random.normal(ks[1], (DEC_BATCH, DEC_SEQ, d), f32),
        "p_prompt": jax.random.normal(ks[2], (DEPTH, BATCH, SEQ, PLE_DIM), f32),
        "p_sample": jax.random.normal(ks[3], (DEPTH, DEC_BATCH, DEC_SEQ, PLE_DIM), f32),
        "norm_mix_pre": gain(ks[4], (DEPTH, d)),
        "norm_mix_post": gain(ks[5], (DEPTH, d)),
        "norm_ffn_pre": gain(ks[6], (DEPTH, d)),
        "norm_ffn_post": gain(ks[7], (DEPTH, d)),
        "mla_w_down": lin(ks[8], (N_LAYERS_A, d, MLA_Q_LORA + MLA_KV_LORA + MLA_ROPE)),
        "mla_q_norm": gain(ks[9], (N_LAYERS_A, MLA_Q_LORA)),
        "mla_kv_norm": gain(ks[10], (N_LAYERS_A, MLA_KV_LORA)),
        "mla_w_uq": lin(ks[11], (N_LAYERS_A, MLA_Q_LORA, MLA_HEADS * (MLA_NOPE + MLA_ROPE))),
        "mla_w_ukv": lin(ks[12], (N_LAYERS_A, MLA_KV_LORA, MLA_HEADS * (MLA_NOPE + MLA_V))),
        "mla_w_o": lin(ks[13], (N_LAYERS_A, MLA_HEADS * MLA_V, d)),
        "gqa_w_qkv": lin(ks[14], (N_LAYERS_B, d, (GQA_Q_HEADS + 2 * GQA_KV_HEADS) * GQA_HEAD_DIM)),
        "gqa_q_norm": gain(ks[15], (N_LAYERS_B, GQA_HEAD_DIM)),
        "gqa_k_norm": gain(ks[16], (N_LAYERS_B, GQA_HEAD_DIM)),
        "gqa_w_o": lin(ks[17], (N_LAYERS_B, GQA_Q_HEADS * GQA_HEAD_DIM, d)),
        "na_w_qkv": lin(ks[18], (N_LAYERS_C, d, 3 * NA_HEADS * NA_HEAD_DIM)),
        "na_rpb": 0.1 * jax.random.normal(ks[19], (N_LAYERS_C, NA_HEADS, NA_REL_ROWS, NA_REL_COLS), f32),
        "na_w_o": lin(ks[20], (N_LAYERS_C, NA_HEADS * NA_HEAD_DIM, d)),
        "ffn_w_in": lin(ks[21], (DEPTH, d, 2 * D_FF)),
        "ffn_conv_w": jax.random.normal(ks[22], (DEPTH, CONV_WIDTH, D_FF), f32) * (CONV_WIDTH ** -0.5),
        "ffn_conv_b": 0.02 * jax.random.normal(ks[23], (DEPTH, D_FF), f32),
        "ffn_w_out": lin(ks[24], (DEPTH, D_FF, d)),
        "ple_w_proj": lin(ks[25], (DEPTH, PLE_DIM, d)),
        "ple_w_gate": lin(ks[26], (DEPTH, d, d)),
        "ple_norm": gain(ks[27], (DEPTH, d)),
    }


def reference(x_prompt, x_sample, p_prompt, p_sample, norm_mix_pre, norm_mix_post, norm_ffn_pre,
              norm_ffn_post, mla_w_down, mla_q_norm, mla_kv_norm, mla_w_uq, mla_w_ukv, mla_w_o,
              gqa_w_qkv, gqa_q_norm, gqa_k_norm, gqa_w_o, na_w_qkv, na_rpb, na_w_o,
              ffn_w_in, ffn_conv_w, ffn_conv_b, ffn_w_out, ple_w_proj, ple_w_gate, ple_norm):
    w = dict(norm_mix_pre=norm_mix_pre, norm_mix_post=norm_mix_post, norm_ffn_pre=norm_ffn_pre,
             norm_ffn_post=norm_ffn_post, mla_w_down=mla_w_down, mla_q_norm=mla_q_norm,
             mla_kv_norm=mla_kv_norm, mla_w_uq=mla_w_uq, mla_w_ukv=mla_w_ukv, mla_w_o=mla_w_o,
             gqa_w_qkv=gqa_w_qkv, gqa_q_norm=gqa_q_norm, gqa_k_norm=gqa_k_norm, gqa_w_o=gqa_w_o,
             na_w_qkv=na_w_qkv, na_rpb=na_rpb, na_w_o=na_w_o, ffn_w_in=ffn_w_in,
             ffn_conv_w=ffn_conv_w, ffn_conv_b=ffn_conv_b, ffn_w_out=ffn_w_out,
             ple_w_proj=ple_w_proj, ple_w_gate=ple_w_gate, ple_norm=ple_norm)
    y_prompt = run_trunk(x_prompt, p_prompt, w)
    y_sample = run_trunk(x_sample, p_sample, w)
    return (y_prompt, y_sample)
```

```python
import numpy as np
import ml_dtypes
from contextlib import ExitStack
import itertools
import concourse.bass as bass
import concourse.mybir as mybir
from concourse.bass_utils import run_bass_kernel_spmd

F32 = mybir.dt.float32
BF16 = mybir.dt.bfloat16
AF = mybir.ActivationFunctionType
ALU = mybir.AluOpType

D = 1024
S = 2048
NT = S // 128
DEPTH = 4
DFF = 4096
PLE = 256
EPS = 1e-6
NEG = -30000.0
SEM_ROT = 6000
NO_CC = False
NA_DBG = ''
STOP_MIX = False


class Buf:
    __slots__ = ("w", "rs")

    def __init__(self):
        self.w = None
        self.rs = []


class Eng:
    def __init__(self, kb, name, eng):
        self.kb = kb
        self.name = name
        self.eng = eng
        self.sem = None
        self.semi = -1
        self.cnt = 0
        self.waited = {}
        self.newsem()

    def newsem(self):
        self.sem = self.kb.nc.alloc_semaphore()
        self.kb.sems.append(self.sem)
        self.semi = len(self.kb.sems) - 1
        self.cnt = 0

    def wait(self, tok):
        si, val, _ = tok
        cur = self.kb.dma_slot_val.get(si)
        if cur is not None and cur > val:
            val = cur
        if self.waited.get(si, 0) >= val:
            return
        self.eng.wait_ge(self.kb.sems[si], val)
        self.waited[si] = val

    def cur(self):
        return (self.semi, self.cnt, self) if self.cnt > 0 else None


class KB:
    def __init__(self):
        self.nc = bass.Bass("TRN2", target_bir_lowering=False)
        nc = self.nc
        self.sems = []
        self.PE = Eng(self, "pe", nc.tensor)
        self.ACT = Eng(self, "act", nc.scalar)
        self.DVE = Eng(self, "dve", nc.vector)
        self.POOL = Eng(self, "pool", nc.gpsimd)
        self.SP = Eng(self, "sp", nc.sync)
        self.engs = [self.PE, self.ACT, self.DVE, self.POOL, self.SP]
        self.dma_sems = []
        for _ in range(16):
            s = nc.alloc_semaphore()
            self.sems.append(s)
            self.dma_sems.append([len(self.sems) - 1, 0])
        self.dma_rr = 0
        self.dma_slot_val = {}
        self.old_final = []

    def _deps(self, E, reads, writes):
        for b in reads:
            if b.w is not None:
                if b.w[2] is E and E is self.PE:
                    continue
                E.wait(b.w)
        for b in writes:
            if b.w is not None and b.w[2] is not E:
                E.wait(b.w)
            for t in b.rs:
                if t[2] is not E:
                    E.wait(t)

    def _commit(self, tok, reads, writes):
        for b in writes:
            b.w = tok
            b.rs = []
        for b in reads:
            b.rs.append(tok)
            if len(b.rs) > 12:
                best = {}
                for t in b.rs:
                    if t[0] not in best or best[t[0]][1] < t[1]:
                        best[t[0]] = t
                b.rs = list(best.values())

    def op(self, E, fn, reads=(), writes=()):
        if E.cnt >= SEM_ROT:
            self.old_final.append(E.cur())
            E.newsem()
        self._deps(E, reads, writes)
        ins = fn()
        E.cnt += 1
        ins.then_inc(E.sem, 1)
        tok = (E.semi, E.cnt, E)
        self._commit(tok, reads, writes)
        return tok

    def pe(self, fn, reads=(), writes=()):
        return self.op(self.PE, fn, reads, writes)

    def act(self, fn, reads=(), writes=()):
        return self.op(self.ACT, fn, reads, writes)

    def dve(self, fn, reads=(), writes=()):
        return self.op(self.DVE, fn, reads, writes)

    def pool(self, fn, reads=(), writes=()):
        return self.op(self.POOL, fn, reads, writes)

    def dma(self, Q, out, in_, reads=(), writes=()):
        self._deps(Q, reads, writes)
        slot = self.dma_sems[self.dma_rr]
        self.dma_rr = (self.dma_rr + 1) % len(self.dma_sems)
        if slot[1] >= SEM_ROT:
            self.old_final.append((slot[0], slot[1], None))
            s = self.nc.alloc_semaphore()
            self.sems.append(s)
            slot[0] = len(self.sems) - 1
            slot[1] = 0
        Q.eng.dma_start(out=out, in_=in_).then_inc(self.sems[slot[0]], 16)
        slot[1] += 16
        self.dma_slot_val[slot[0]] = slot[1]
        tok = (slot[0], slot[1], None)
        self._commit(tok, reads, writes)
        return tok

    def collective(self, ins, outs, reads, writes, groups):
        Q = self.POOL
        if NO_CC:
            n = ins[0].shape[0]
            return self.dma(Q, outs[0][0:n], ins[0], reads=reads, writes=writes)
        self._deps(Q, reads, writes)
        s = self.nc.alloc_semaphore()
        self.sems.append(s)
        si = len(self.sems) - 1
        Q.eng.collective_compute("AllGather", ALU.bypass, replica_groups=groups,
                                 ins=ins, outs=outs).then_inc(s, 1)
        tok = (si, 1, None)
        self._commit(tok, reads, writes)
        return tok

    def barrier(self):
        toks = [e.cur() for e in self.engs if e.cur() is not None]
        toks += [(s[0], s[1], None) for s in self.dma_sems if s[1] > 0]
        toks += self.old_final
        self.old_final = []
        for e in self.engs:
            for t in toks:
                if t[2] is not e:
                    e.wait(t)


def build_program(nlayers=DEPTH, debug=False):
    kb = KB()
    nc = kb.nc
    PE, ACT, DVE, POOL, SP = kb.PE, kb.ACT, kb.DVE, kb.POOL, kb.SP

    _ctr = [0]

    def sbt(name, shape, dt):
        _ctr[0] += 1
        return nc.sbuf_tensor("%s_%d" % (name, _ctr[0]), list(shape), dt)

    def din(name, shape, dt=F32):
        return nc.dram_tensor(name, list(shape), dt, kind="ExternalInput").ap()

    def dscr(name, shape, dt=BF16):
        return nc.dram_tensor(name, list(shape), dt).ap()

    x_in = din("x_in", [2, S, D])
    p_in = din("p_in", [DEPTH, 2, S, PLE])
    y_out = nc.dram_tensor("y_out", [2, S, D], F32, kind="ExternalOutput").ap()
    g_mix_pre = din("norm_mix_pre", [DEPTH, D])
    g_mix_post = din("norm_mix_post", [DEPTH, D])
    g_ffn_pre = din("norm_ffn_pre", [DEPTH, D])
    g_ffn_post = din("norm_ffn_post", [DEPTH, D])
    g_ple = din("ple_norm", [DEPTH, D])
    mla_w_down = din("mla_w_down", [2, D, 704])
    mla_q_norm = din("mla_q_norm", [2, 384])
    mla_kv_norm = din("mla_kv_norm", [2, 256])
    mla_w_uq = din("mla_w_uq", [2, 384, 1536])
    mla_w_uq_rp = din("mla_w_uq_rp", [2, 384, 512])
    mla_w_ukv = din("mla_w_ukv", [2, 256, 2048])
    mla_w_o = din("mla_w_o", [2, D, D])
    gqa_w_qkv = din("gqa_w_qkv", [1, D, 1536])
    gqa_q_norm = din("gqa_q_norm", [1, 128])
    gqa_k_norm = din("gqa_k_norm", [1, 128])
    gqa_w_o = din("gqa_w_o", [1, D, D])
    na_w_qkv = din("na_w_qkv", [1, D, 3072])
    na_w_o = din("na_w_o", [1, D, D])
    na_tab = din("na_tab", [2, 4, 128, 8 * 4 * 64])
    na_rmask = din("na_rmask", [2, 128, 32, 6])
    na_hsel = din("na_hsel", [128, 8])
    ffn_w_in = din("ffn_w_in", [DEPTH, D, 2 * DFF])
    ffn_conv_w = din("ffn_conv_wT", [DEPTH, 128, 3, 32])
    ffn_conv_b = din("ffn_conv_bT", [DEPTH, 128, 32])
    ffn_w_out = din("ffn_w_out", [DEPTH, DFF, D])
    ple_w_proj = din("ple_w_proj", [DEPTH, PLE, D])
    ple_w_gate = din("ple_w_gate", [DEPTH, D, D])
    ident_in = din("ident", [128, 128])
    halo_sel = din("halo_sel", [8, 2])
    mla_cs_tok = din("mla_cs_tok", [2, S, 64])
    mla_cs_feat = din("mla_cs_feat", [2, 2, 64, S])
    gqa_cs_tok = din("gqa_cs_tok", [2, S, 128])

    xres = nc.dram_tensor("xres", [2, S, D], F32).ap()
    xnT_d = dscr("xnT_d", [2, 8, 128, S])
    xfT_d = dscr("xfT_d", [2, 8, 128, S])
    w_in_bf = dscr("w_in_bf", [D, 2 * DFF])
    w_out_bf = dscr("w_out_bf", [DFF, D])
    cc_in_mla = dscr("cc_in_mla", [256, S])
    cc_out_mla = dscr("cc_out_mla", [4 * 256, S])
    cc_in_mlb = dscr("cc_in_mlb", [64, S])
    cc_out_mlb = dscr("cc_out_mlb", [4 * 64, S])
    cc_in_gk = dscr("cc_in_gk", [256, S])
    cc_out_gk = dscr("cc_out_gk", [4 * 256, S])
    cc_in_gv = dscr("cc_in_gv", [S, 256])
    cc_out_gv = dscr("cc_out_gv", [4 * S, 256])
    cc_in_h = dscr("cc_in_h", [2, D])
    cc_out_h = dscr("cc_out_h", [8, D])
    na_kv_d = dscr("na_kv_d", [2, S, 2048])
    cc_in_na = dscr("cc_in_na", [512, 2048])
    cc_out_na = dscr("cc_out_na", [4 * 512, 2048])
    GROUPS = [[0, 1, 2, 3], [4, 5, 6, 7]]

    B_xres = [[Buf() for _ in range(NT)] for _ in range(2)]
    B_xnT = [[Buf() for _ in range(NT)] for _ in range(2)]
    B_xfT = [[Buf() for _ in range(NT)] for _ in range(2)]
    B_win = Buf()
    B_wout = Buf()
    B_cc = {k: Buf() for k in ["in_mla", "out_mla", "in_mlb", "out_mlb", "in_gk", "out_gk", "in_gv", "out_gv", "in_h", "out_h",
                               "in_na", "out_na", "nakv0", "nakv1"]}

    ident = nc.alloc_sbuf_tensor("identb", [128, 128], BF16).ap()
    ones = nc.alloc_sbuf_tensor("onesb", [128, 128], BF16).ap()
    B_const = Buf()
    PS = nc.alloc_psum_tensor("PS", [128, 8, 512], F32).ap()
    PB = [Buf() for _ in range(8)]
    kb.dma(POOL, ident, ident_in, writes=[B_const])
    kb.dve(lambda: nc.vector.memset(ones, 1.0), writes=[B_const])

    def ps_bf(b0, nb=1):
        return PS[:, b0:b0 + nb, :].rearrange("p b n -> p (b n)").bitcast(BF16)

    def ps_f(b0, nb=1):
        return PS[:, b0:b0 + nb, :].rearrange("p b n -> p (b n)")

    def load_gain(es, name, src_row, n=D):
        t = es.enter_context(sbt(name, [128, n], F32))
        b = Buf()
        kb.dma(SP, t[:], src_row.to_broadcast([128, n]), writes=[b])
        return t, b

    def rms_stats(src_ap, src_bufs, n, scr, stat, stat_b, col, scr_b=None):
        kb.act(lambda: nc.scalar.activation(out=scr[:, 0:n], in_=src_ap, func=AF.Square,
                                            accum_out=stat[:, col:col + 1]),
               reads=src_bufs, writes=[stat_b] + ([scr_b] if scr_b is not None else []))
        kb.act(lambda: nc.scalar.activation(out=stat[:, col + 1:col + 2], in_=stat[:, col:col + 1],
                                            func=AF.Sqrt, scale=1.0 / n, bias=EPS),
               reads=[stat_b], writes=[stat_b])
        kb.dve(lambda: nc.vector.reciprocal(out=stat[:, col + 2:col + 3], in_=stat[:, col + 1:col + 2]),
               reads=[stat_b], writes=[stat_b])
        return stat[:, col + 2:col + 3]

    def transpose_to(src_bf, src_b, nchunk, pbank, dst_fn, dst_bufs, rows=128):
        pv = ps_bf(pbank)
        def f():
            ins = None
            for c in range(nchunk):
                ins = nc.tensor.transpose(pv[:, c * 128:(c + 1) * 128], src_bf[:, c * 128:(c + 1) * 128], ident)
            return ins
        kb.pe(f, reads=[src_b, B_const], writes=[PB[pbank]])
        return pv

    def norm_T_store(es_tiles, x_sb, x_b, gain, gain_b, dst_d, dst_b, seg, t, pbank, extra_row=None):
        scr, scr_b, stat, stat_b, xn, xn_b, xT, xT_b = es_tiles
        rstd = rms_stats(x_sb, [x_b], D, scr, stat, stat_b, 0, scr_b)
        if gain is not None:
            kb.dve(lambda: nc.vector.scalar_tensor_tensor(out=xn, in0=x_sb, scalar=rstd, in1=gain,
                                                          op0=ALU.mult, op1=ALU.mult),
                   reads=[x_b, stat_b, gain_b], writes=[xn_b])
        else:
            kb.dve(lambda: nc.vector.tensor_scalar(out=xn, in0=x_sb, scalar1=rstd, scalar2=None, op0=ALU.mult),
                   reads=[x_b, stat_b], writes=[xn_b])
        if extra_row is not None:
            extra_row(xn, xn_b)
        pv = transpose_to(xn, xn_b, 8, pbank, None, None)
        kb.act(lambda: nc.scalar.activation(out=xT, in_=pv, func=AF.Copy), reads=[PB[pbank]], writes=[xT_b])
        if dst_d is not None:
            kb.dma(POOL, dst_d[seg, :, :, t * 128:(t + 1) * 128].rearrange("c p n -> p c n"),
                   xT.rearrange("p (c n) -> p c n", c=8), reads=[xT_b], writes=[dst_b[seg][t]])

    def alloc_norm_tiles(es, tag):
        scr = es.enter_context(sbt("scr" + tag, [128, D], F32))[:]
        stat = es.enter_context(sbt("stat" + tag, [128, 16], F32))[:]
        xn = es.enter_context(sbt("xn" + tag, [128, D], BF16))[:]
        xT = es.enter_context(sbt("xT" + tag, [128, D], BF16))[:]
        return (scr, Buf(), stat, Buf(), xn, Buf(), xT, Buf())

    def stage_initial():
        with ExitStack() as es:
            gain, gain_b = load_gain(es, "g0", g_mix_pre[0:1, :])
            tl = [alloc_norm_tiles(es, "i%d" % i) for i in range(2)]
            xs = [es.enter_context(sbt("xi%d" % i, [128, D], F32))[:] for i in range(2)]
            xb = [Buf(), Buf()]
            k = 0
            for seg in range(2):
                for t in range(NT):
                    j = k % 2
                    kb.dma(SP, xs[j], x_in[seg, t * 128:(t + 1) * 128, :], writes=[xb[j]])
                    kb.dma(POOL, xres[seg, t * 128:(t + 1) * 128, :], xs[j], reads=[xb[j]], writes=[B_xres[seg][t]])
                    norm_T_store(tl[j], xs[j], xb[j], gain[:], gain_b, xnT_d, B_xnT, seg, t, j)
                    k += 1
        kb.barrier()

    def attn_head(es_at, nk, qparts, kparts, vfn, OT_dst, OT_b, extra_reads):
        PT, PT_b, rec, rec_b = es_at
        nkt = nk // 128
        ngrp = nkt // 2
        for qb in range(S // 512):
            qs = slice(qb * 512, (qb + 1) * 512)

            def scores(g):
                b0 = 2 * (g % 3)
                def f():
                    ins = None
                    for j in range(2):
                        kt = 2 * g + j
                        for pi, ((q_ap, rows), (k_ap, _)) in enumerate(zip(qparts, kparts)):
                            ins = nc.tensor.matmul(PS[:, b0 + j, :], lhsT=k_ap[0:rows, kt * 128:(kt + 1) * 128],
                                                   rhs=q_ap[0:rows, qs], start=(pi == 0),
                                                   stop=(pi == len(qparts) - 1))
                    return ins
                kb.pe(f, reads=extra_reads, writes=[PB[b0], PB[b0 + 1]])

            def expo(g):
                b0 = 2 * (g % 3)
                kb.act(lambda: nc.scalar.activation(out=PT[g % 3], in_=ps_f(b0, 2), func=AF.Exp),
                       reads=[PB[b0], PB[b0 + 1]], writes=[PT_b[g % 3]])

            def pv(g):
                def f():
                    ins = None
                    for j in range(2):
                        kt = 2 * g + j
                        rhs = PT[g % 3][:, j * 512:(j + 1) * 512]
                        nc.tensor.matmul(PS[:, 6, :], lhsT=vfn(kt), rhs=rhs, start=(kt == 0), stop=(kt == nkt - 1))
                        ins = nc.tensor.matmul(PS[:, 7, :], lhsT=ones, rhs=rhs, start=(kt == 0), stop=(kt == nkt - 1))
                    return ins
                kb.pe(f, reads=[PT_b[g % 3], B_const] + extra_reads, writes=[PB[6], PB[7]])

            for g in range(ngrp):
                scores(g)
                expo(g)
                if g > 1:
                    pv(g - 2)
            if ngrp > 1:
                pv(ngrp - 2)
            pv(ngrp - 1)
            kb.dve(lambda: nc.vector.reciprocal(out=rec, in_=PS[:, 7, :]), reads=[PB[7]], writes=[rec_b])
            kb.dve(lambda: nc.vector.tensor_tensor(out=OT_dst[:, qs], in0=PS[:, 6, :], in1=rec, op=ALU.mult),
                   reads=[PB[6], rec_b], writes=[OT_b])

    def alloc_attn_tiles(es):
        PT = [es.enter_context(sbt("PT%d" % i, [128, 1024], BF16))[:] for i in range(3)]
        rec = es.enter_context(sbt("rec", [128, 512], F32))[:]
        return (PT, [Buf(), Buf(), Buf()], rec, Buf())

    def stage_out(es, layer, seg, OT, OT_b, wo_src, nchunk_rows, last_phase_cb=None):
        rows, nch = nchunk_rows
        wo = es.enter_context(sbt("wo", [rows, nch, D], BF16))
        wo_b = Buf()
        kb.dma(POOL, wo[:], wo_src.rearrange("(c p) n -> p c n", p=rows), writes=[wo_b])
        gpost, gpost_b = load_gain(es, "gpost", g_mix_post[layer:layer + 1, :])
        gfpre, gfpre_b = load_gain(es, "gfpre", g_ffn_pre[layer:layer + 1, :])
        tl = [alloc_norm_tiles(es, "c%d" % i) for i in range(2)]
        xs = [es.enter_context(sbt("xc%d" % i, [128, D], F32))[:] for i in range(2)]
        xb = [Buf(), Buf()]
        x1 = [es.enter_context(sbt("x1c%d" % i, [128, D], F32))[:] for i in range(2)]
        x1b = [Buf(), Buf()]
        st2 = es.enter_context(sbt("st2", [128, 16], F32))[:]
        st2_b = Buf()
        for t in range(NT):
            j = t % 2
            ts = slice(t * 128, (t + 1) * 128)
            kb.dma(SP, xs[j], xres[seg, ts, :], reads=[B_xres[seg][t]], writes=[xb[j]])
            b0 = 6 if False else (2 * j)
            def f():
                ins = None
                for half in range(2):
                    for c in range(nch):
                        ins = nc.tensor.matmul(PS[:, b0 + half, :], lhsT=OT[0:rows, c, ts],
                                               rhs=wo[0:rows, c, half * 512:(half + 1) * 512],
                                               start=(c == 0), stop=(c == nch - 1))
                return ins
            kb.pe(f, reads=[OT_b, wo_b], writes=[PB[b0], PB[b0 + 1]])
            scr, scr_b = tl[j][0], tl[j][1]
            rstd = rms_stats(ps_f(b0, 2), [PB[b0], PB[b0 + 1]], D, scr, st2, st2_b, 4 * j, scr_b)
            kb.dve(lambda: nc.vector.scalar_tensor_tensor(out=scr, in0=ps_f(b0, 2), scalar=rstd, in1=gpost[:],
                                                          op0=ALU.mult, op1=ALU.mult),
                   reads=[PB[b0], PB[b0 + 1], st2_b, gpost_b], writes=[scr_b])
            kb.dve(lambda: nc.vector.tensor_tensor(out=x1[j], in0=scr, in1=xs[j], op=ALU.add),
                   reads=[scr_b, xb[j]], writes=[x1b[j]])
            kb.dma(POOL, xres[seg, ts, :], x1[j], reads=[x1b[j]], writes=[B_xres[seg][t]])
            extra = None
            if seg == 1 and t in (0, NT - 1):
                def extra(xn, xn_b, t=t):
                    if t == 0:
                        kb.dma(POOL, cc_in_h[0:1, :], xn[0:1, :], reads=[xn_b], writes=[B_cc["in_h"]])
                    else:
                        kb.dma(POOL, cc_in_h[1:2, :], xn[127:128, :], reads=[xn_b], writes=[B_cc["in_h"]])
            norm_T_store(tl[j], x1[j], x1b[j], gfpre[:], gfpre_b, xfT_d, B_xfT, seg, t, 6 + j, extra_row=extra)
        if seg == 1:
            kb.collective([cc_in_h], [cc_out_h], reads=[B_cc["in_h"]], writes=[B_cc["out_h"]], groups=GROUPS)

    def mixer_mla(layer, seg, j):
        nk = S if seg == 0 else 4 * S
        with ExitStack() as es0, ExitStack() as es:
            OT = es0.enter_context(sbt("OT", [128, 8, S], BF16))[:]
            sb = lambda name, shape, dt=BF16: es.enter_context(sbt(name, shape, dt))[:]
            ckvT = sb("ckvT", [128, 2, nk])
            kropeT = sb("kropeT", [64, nk])
            cqT = sb("cqT", [128, 3, S])
            B_ckv, B_cq, B_OT = Buf(), Buf(), Buf()
            w_uq = sb("w_uq", [128, 3, 1536])
            w_uqr = sb("w_uqr", [128, 3, 512])
            w_ukv = sb("w_ukv", [128, 2, 2048])
            B_w = Buf()
            kb.dma(POOL, w_uq, mla_w_uq[j].rearrange("(c p) n -> p c n", p=128), writes=[B_w])
            kb.dma(POOL, w_uqr, mla_w_uq_rp[j].rearrange("(c p) n -> p c n", p=128), writes=[B_w])
            kb.dma(POOL, w_ukv, mla_w_ukv[j].rearrange("(c p) n -> p c n", p=128), writes=[B_w])
            csf = sb("csf", [64, 2, S], F32)
            B_csf = Buf()
            kb.dma(SP, csf, mla_cs_feat[seg].rearrange("a p n -> p a n"), writes=[B_csf])
            with ExitStack() as es2:
                sb2 = lambda name, shape, dt=BF16: es2.enter_context(sbt(name, shape, dt))[:]
                xnTb = [sb2("xnTb%d" % i, [128, 8, 512]) for i in range(2)]
                B_xnb = [Buf(), Buf()]
                wd = sb2("wd", [128, 8, 704])
                B_wd = Buf()
                kb.dma(POOL, wd, mla_w_down[j].rearrange("(c p) n -> p c n", p=128), writes=[B_wd])
                gq, gq_b = load_gain(es2, "gq", mla_q_norm[j:j + 1, :], 384)
                gkv, gkv_b = load_gain(es2, "gkv", mla_kv_norm[j:j + 1, :], 256)
                cst = sb2("cst", [128, NT, 64], F32)
                B_cst = Buf()
                kb.dma(SP, cst, mla_cs_tok[seg].rearrange("(t p) n -> p t n", p=128), writes=[B_cst])
                scr = sb2("scrA", [128, 512], F32)
                stat = sb2("statA", [128, 16], F32)
                stat_b = Buf()
                dnb = [sb2("dnb%d" % i, [128, 768]) for i in range(2)]
                dnb_b = [Buf(), Buf()]
                tmp = [sb2("tmpA%d" % i, [128, 128], F32) for i in range(2)]
                tmp_b = [Buf(), Buf()]
                for t in range(NT):
                    i = t % 2
                    ts = slice(t * 128, (t + 1) * 128)
                    b0 = 2 * i
                    xb_i = (t // 4) % 2
                    if t % 4 == 0:
                        kb.dma(SP, xnTb[xb_i], xnT_d[seg, :, :, t * 128:t * 128 + 512].rearrange("c p n -> p c n"),
                               reads=B_xnT[seg][t:t + 4], writes=[B_xnb[xb_i]])
                    xnT = xnTb[xb_i]
                    B_xn = B_xnb[xb_i]
                    tl_ = slice((t % 4) * 128, (t % 4 + 1) * 128)
                    def f():
                        ins = None
                        for c in range(8):
                            nc.tensor.matmul(PS[:, b0, 0:384], lhsT=xnT[:, c, tl_], rhs=wd[:, c, 0:384],
                                             start=(c == 0), stop=(c == 7))
                        for c in range(8):
                            ins = nc.tensor.matmul(PS[:, b0 + 1, 0:320], lhsT=xnT[:, c, tl_], rhs=wd[:, c, 384:704],
                                                   start=(c == 0), stop=(c == 7))
                        return ins
                    kb.pe(f, reads=[B_xn, B_wd], writes=[PB[b0], PB[b0 + 1]])
                    rq = rms_stats(PS[:, b0, 0:384], [PB[b0]], 384, scr, stat, stat_b, 8 * i)
                    rkv = rms_stats(PS[:, b0 + 1, 0:256], [PB[b0 + 1]], 256, scr, stat, stat_b, 8 * i + 4)
                    kb.dve(lambda: nc.vector.scalar_tensor_tensor(out=dnb[i][:, 0:384], in0=PS[:, b0, 0:384], scalar=rq,
                                                                  in1=gq[:], op0=ALU.mult, op1=ALU.mult),
                           reads=[PB[b0], stat_b, gq_b], writes=[dnb_b[i]])
                    kb.dve(lambda: nc.vector.scalar_tensor_tensor(out=dnb[i][:, 384:640], in0=PS[:, b0 + 1, 0:256], scalar=rkv,
                                                                  in1=gkv[:], op0=ALU.mult, op1=ALU.mult),
                           reads=[PB[b0 + 1], stat_b, gkv_b], writes=[dnb_b[i]])
                    x1 = PS[:, b0 + 1, 256:288]
                    x2 = PS[:, b0 + 1, 288:320]
                    cs_c = cst[:, t, 0:32]
                    cs_s = cst[:, t, 32:64]
                    tm = tmp[i]
                    kb.dve(lambda: nc.vector.tensor_tensor(out=tm[:, 0:32], in0=x1, in1=cs_c, op=ALU.mult),
                           reads=[PB[b0 + 1], B_cst], writes=[tmp_b[i]])
                    kb.dve(lambda: nc.vector.tensor_tensor(out=tm[:, 32:64], in0=x2, in1=cs_s, op=ALU.mult),
                           reads=[PB[b0 + 1], B_cst], writes=[tmp_b[i]])
                    kb.dve(lambda: nc.vector.tensor_tensor(out=tm[:, 64:96], in0=x2, in1=cs_c, op=ALU.mult),
                           reads=[PB[b0 + 1], B_cst], writes=[tmp_b[i]])
                    kb.dve(lambda: nc.vector.tensor_tensor(out=tm[:, 96:128], in0=x1, in1=cs_s, op=ALU.mult),
                           reads=[PB[b0 + 1], B_cst], writes=[tmp_b[i]])
                    kb.dve(lambda: nc.vector.tensor_tensor(out=dnb[i][:, 640:672], in0=tm[:, 0:32], in1=tm[:, 32:64],
                                                           op=ALU.subtract), reads=[tmp_b[i]], writes=[dnb_b[i]])
                    kb.dve(lambda: nc.vector.tensor_tensor(out=dnb[i][:, 672:704], in0=tm[:, 64:96], in1=tm[:, 96:128],
                                                           op=ALU.add), reads=[tmp_b[i]], writes=[dnb_b[i]])
                    pbank = 4 + i
                    pvw = ps_bf(pbank)
                    def ft():
                        ins = None
                        for c in range(5):
                            ins = nc.tensor.transpose(pvw[:, c * 128:(c + 1) * 128], dnb[i][:, c * 128:(c + 1) * 128], ident)
                        ins = nc.tensor.transpose(pvw[0:64, 640:768], dnb[i][:, 640:704], ident)
                        return ins
                    kb.pe(ft, reads=[dnb_b[i], B_const], writes=[PB[pbank]])
                    off = 0 if seg == 0 else 0
                    kb.act(lambda: nc.scalar.activation(out=cqT[:, :, ts], in_=pvw[:, 0:384].rearrange("p (c n) -> p c n", c=3),
                                                        func=AF.Copy), reads=[PB[pbank]], writes=[B_cq])
                    if seg == 0:
                        kb.act(lambda: nc.scalar.activation(out=ckvT[:, :, ts], in_=pvw[:, 384:640].rearrange("p (c n) -> p c n", c=2),
                                                            func=AF.Copy), reads=[PB[pbank]], writes=[B_ckv])
                        kb.act(lambda: nc.scalar.activation(out=kropeT[:, ts], in_=pvw[0:64, 640:768], func=AF.Copy),
                               reads=[PB[pbank]], writes=[B_ckv])
                    else:
                        kb.act(lambda: nc.scalar.activation(out=ckvT[:, :, ts], in_=pvw[:, 384:640].rearrange("p (c n) -> p c n", c=2),
                                                            func=AF.Copy), reads=[PB[pbank]], writes=[B_ckv])
                        kb.act(lambda: nc.scalar.activation(out=kropeT[:, ts], in_=pvw[0:64, 640:768], func=AF.Copy),
                               reads=[PB[pbank]], writes=[B_ckv])
                if seg == 1:
                    kb.dma(POOL, cc_in_mla.rearrange("(c p) n -> p c n", p=128), ckvT[:, :, 0:S],
                           reads=[B_ckv], writes=[B_cc["in_mla"]])
                    kb.dma(POOL, cc_in_mlb, kropeT[:, 0:S], reads=[B_ckv], writes=[B_cc["in_mlb"]])
                    kb.collective([cc_in_mla], [cc_out_mla], reads=[B_cc["in_mla"]], writes=[B_cc["out_mla"]], groups=GROUPS)
                    kb.collective([cc_in_mlb], [cc_out_mlb], reads=[B_cc["in_mlb"]], writes=[B_cc["out_mlb"]], groups=GROUPS)
                    for r in range(4):
                        kb.dma(SP, ckvT[:, :, r * S:(r + 1) * S],
                               cc_out_mla[r * 256:(r + 1) * 256, :].rearrange("(c p) n -> p c n", p=128),
                               reads=[B_cc["out_mla"]], writes=[B_ckv])
                        kb.dma(SP, kropeT[:, r * S:(r + 1) * S], cc_out_mlb[r * 64:(r + 1) * 64, :],
                               reads=[B_cc["out_mlb"]], writes=[B_ckv])
            kb.barrier()
            with ExitStack() as es3:
                sb3 = lambda name, shape, dt=BF16: es3.enter_context(sbt(name, shape, dt))[:]
                qnT = sb3("qnT", [128, S])
                qrT = sb3("qrT", [64, S])
                KhT = sb3("KhT", [128, nk])
                Vh = sb3("Vh", [128, nk // 128, 128])
                B_q, B_K, B_V = Buf(), Buf(), Buf()
                t1 = sb3("t1", [64, 512], F32)
                t2 = sb3("t2", [64, 512], F32)
                B_t = Buf()
                at = alloc_attn_tiles(es3)
                scale = 192.0 ** -0.5
                for h in range(8):
                    for qb in range(S // 512):
                        qs = slice(qb * 512, (qb + 1) * 512)
                        bq = 4 + (qb % 2)
                        def f():
                            ins = None
                            for c in range(3):
                                ins = nc.tensor.matmul(PS[:, bq, :], lhsT=w_uq[:, c, h * 192:h * 192 + 128], rhs=cqT[:, c, qs],
                                                       start=(c == 0), stop=(c == 2))
                            return ins
                        kb.pe(f, reads=[B_w, B_cq], writes=[PB[bq]])
                        kb.act(lambda: nc.scalar.activation(out=qnT[:, qs], in_=PS[:, bq, :], func=AF.Copy, scale=scale),
                               reads=[PB[bq]], writes=[B_q])
                        def f3():
                            ins = None
                            for c in range(3):
                                ins = nc.tensor.matmul(PS[0:64, bq, :], lhsT=w_uq[:, c, h * 192 + 128:h * 192 + 192], rhs=cqT[:, c, qs],
                                                       start=(c == 0), stop=(c == 2))
                            return ins
                        kb.pe(f3, reads=[B_w, B_cq], writes=[PB[bq]])
                        kb.dve(lambda: nc.vector.tensor_tensor(out=t1, in0=PS[0:64, bq, :], in1=csf[:, 0, qs], op=ALU.mult),
                               reads=[PB[bq], B_csf], writes=[B_t])
                        def f4():
                            ins = None
                            for c in range(3):
                                ins = nc.tensor.matmul(PS[0:64, bq, :], lhsT=w_uqr[:, c, h * 64:(h + 1) * 64], rhs=cqT[:, c, qs],
                                                       start=(c == 0), stop=(c == 2))
                            return ins
                        kb.pe(f4, reads=[B_w, B_cq], writes=[PB[bq]])
                        kb.dve(lambda: nc.vector.tensor_tensor(out=t2, in0=PS[0:64, bq, :], in1=csf[:, 1, qs], op=ALU.mult),
                               reads=[PB[bq], B_csf], writes=[B_t])
                        kb.dve(lambda: nc.vector.tensor_tensor(out=qrT[:, qs], in0=t1, in1=t2, op=ALU.add),
                               reads=[B_t], writes=[B_q])
                    for kbk in range(nk // 512):
                        ks = slice(kbk * 512, (kbk + 1) * 512)
                        bq = 4 + (kbk % 2)
                        def f():
                            ins = None
                            for c in range(2):
                                ins = nc.tensor.matmul(PS[:, bq, :], lhsT=w_ukv[:, c, h * 256:h * 256 + 128], rhs=ckvT[:, c, ks],
                                                       start=(c == 0), stop=(c == 1))
                            return ins
                        kb.pe(f, reads=[B_w, B_ckv], writes=[PB[bq]])
                        kb.act(lambda: nc.scalar.activation(out=KhT[:, ks], in_=PS[:, bq, :], func=AF.Copy),
                               reads=[PB[bq]], writes=[B_K])
                    for kg in range(nk // 512):
                        bq = 4 + (kg % 2)
                        def f():
                            ins = None
                            for jj in range(4):
                                kt = kg * 4 + jj
                                for c in range(2):
                                    ins = nc.tensor.matmul(PS[:, bq, jj * 128:(jj + 1) * 128], lhsT=ckvT[:, c, kt * 128:(kt + 1) * 128],
                                                           rhs=w_ukv[:, c, h * 256 + 128:h * 256 + 256], start=(c == 0), stop=(c == 1))
                            return ins
                        kb.pe(f, reads=[B_w, B_ckv], writes=[PB[bq]])
                        kb.dve(lambda: nc.vector.tensor_copy(out=Vh[:, kg * 4:(kg + 1) * 4, :],
                                                             in_=PS[:, bq, :].rearrange("p (a n) -> p a n", a=4)),
                               reads=[PB[bq]], writes=[B_V])
                    attn_head(at, nk, [(qnT, 128), (qrT, 64)], [(KhT, 128), (kropeT, 64)],
                              lambda kt: Vh[:, kt, :], OT[:, h, :], B_OT, [B_q, B_K, B_V, B_ckv])
            kb.barrier()
            es.close()
            with ExitStack() as es4:
                stage_out(es4, layer, seg, OT, B_OT, mla_w_o[j], (128, 8))
        kb.barrier()

    def mixer_gqa(layer, seg, j):
        nk = S if seg == 0 else 4 * S
        with ExitStack() as es0, ExitStack() as es:
            OT = es0.enter_context(sbt("OTg", [128, 8, S], BF16))[:]
            B_OT = Buf()
            sb = lambda name, shape, dt=BF16: es.enter_context(sbt(name, shape, dt))[:]
            qT = sb("qT", [128, 8, S])
            kT = sb("kT", [128, 2, nk])
            Vall = sb("Vall", [128, nk // 128, 256])
            B_q, B_k, B_v = Buf(), Buf(), Buf()
            with ExitStack() as es2:
                sb2 = lambda name, shape, dt=BF16: es2.enter_context(sbt(name, shape, dt))[:]
                xnTb = [sb2("xnTg%d" % i, [128, 8, 512]) for i in range(2)]
                B_xnb = [Buf(), Buf()]
                wqkv = sb2("wqkv", [128, 8, 1536])
                B_w = Buf()
                kb.dma(POOL, wqkv, gqa_w_qkv[j].rearrange("(c p) n -> p c n", p=128), writes=[B_w])
                gq, gq_b = load_gain(es2, "ggq", gqa_q_norm[j:j + 1, :], 128)
                gk, gk_b = load_gain(es2, "ggk", gqa_k_norm[j:j + 1, :], 128)
                kb.act(lambda: nc.scalar.mul(out=gq[:], in_=gq[:], mul=128.0 ** -0.5), reads=[gq_b], writes=[gq_b])
                cst = sb2("cstg", [128, NT, 128], F32)
                B_cst = Buf()
                kb.dma(SP, cst, gqa_cs_tok[seg].rearrange("(t p) n -> p t n", p=128), writes=[B_cst])
                sq = sb2("sqg", [128, 1280], F32)
                B_sq = Buf()
                stat = sb2("statg", [128, 32], F32)
                stat_b = Buf()
                qn = sb2("qng", [128, 1280], F32)
                B_qn = Buf()
                tm = sb2("tmg", [128, 4, 320], F32)
                B_tm = Buf()
                qkb = [sb2("qkb%d" % i, [128, 1280]) for i in range(2)]
                qkb_b = [Buf(), Buf()]
                for t in range(NT):
                    i = t % 2
                    ts = slice(t * 128, (t + 1) * 128)
                    b0 = 3 * i
                    xb_i = (t // 4) % 2
                    if t % 4 == 0:
                        kb.dma(SP, xnTb[xb_i], xnT_d[seg, :, :, t * 128:t * 128 + 512].rearrange("c p n -> p c n"),
                               reads=B_xnT[seg][t:t + 4], writes=[B_xnb[xb_i]])
                    xnT = xnTb[xb_i]
                    tl_ = slice((t % 4) * 128, (t % 4 + 1) * 128)
                    def f():
                        ins = None
                        for pc in range(3):
                            for c in range(8):
                                ins = nc.tensor.matmul(PS[:, b0 + pc, :], lhsT=xnT[:, c, tl_], rhs=wqkv[:, c, pc * 512:(pc + 1) * 512],
                                                       start=(c == 0), stop=(c == 7))
                        return ins
                    kb.pe(f, reads=[B_xnb[xb_i], B_w], writes=[PB[b0], PB[b0 + 1], PB[b0 + 2]])
                    kb.act(lambda: nc.scalar.activation(out=Vall[:, t, :], in_=PS[:, b0 + 2, 256:512], func=AF.Copy),
                           reads=[PB[b0 + 2]], writes=[B_v])
                    kb.act(lambda: nc.scalar.activation(out=sq[:, 0:1024], in_=ps_f(b0, 2), func=AF.Square),
                           reads=[PB[b0], PB[b0 + 1]], writes=[B_sq])
                    kb.act(lambda: nc.scalar.activation(out=sq[:, 1024:1280], in_=PS[:, b0 + 2, 0:256], func=AF.Square),
                           reads=[PB[b0 + 2]], writes=[B_sq])
                    kb.dve(lambda: nc.vector.tensor_reduce(out=stat[:, 0:10], in_=sq.rearrange("p (h d) -> p h d", d=128),
                                                           axis=mybir.AxisListType.X, op=ALU.add),
                           reads=[B_sq], writes=[stat_b])
                    kb.act(lambda: nc.scalar.activation(out=stat[:, 10:20], in_=stat[:, 0:10], func=AF.Sqrt, scale=1.0 / 128, bias=EPS),
                           reads=[stat_b], writes=[stat_b])
                    kb.dve(lambda: nc.vector.reciprocal(out=stat[:, 20:30], in_=stat[:, 10:20]), reads=[stat_b], writes=[stat_b])
                    for h in range(10):
                        src = PS[:, b0 + h // 4, (h % 4) * 128:(h % 4 + 1) * 128]
                        g_ = gq if h < 8 else gk
                        g_b = gq_b if h < 8 else gk_b
                        kb.dve(lambda: nc.vector.scalar_tensor_tensor(out=qn[:, h * 128:(h + 1) * 128], in0=src, scalar=stat[:, 20 + h:21 + h],
                                                                      in1=g_[:], op0=ALU.mult, op1=ALU.mult),
                               reads=[PB[b0 + h // 4], stat_b, g_b], writes=[B_qn])
                    q3 = qn.rearrange("p (h d) -> p h d", d=128)
                    o3 = qkb[i].rearrange("p (h d) -> p h d", d=128)
                    for part in range(2):
                        o = part * 64
                        x1 = q3[:, :, o:o + 32]
                        x2 = q3[:, :, o + 32:o + 64]
                        cc_ = cst[:, t, o:o + 32].rearrange("p (o n) -> p o n", o=1).to_broadcast([128, 10, 32])
                        ss_ = cst[:, t, o + 32:o + 64].rearrange("p (o n) -> p o n", o=1).to_broadcast([128, 10, 32])
                        tv = [tm[:, k, :].rearrange("p (h n) -> p h n", n=32) for k in range(4)]
                        kb.dve(lambda: nc.vector.tensor_tensor(out=tv[0], in0=x1, in1=cc_, op=ALU.mult), reads=[B_qn, B_cst], writes=[B_tm])
                        kb.dve(lambda: nc.vector.tensor_tensor(out=tv[1], in0=x2, in1=ss_, op=ALU.mult), reads=[B_qn, B_cst], writes=[B_tm])
                        kb.dve(lambda: nc.vector.tensor_tensor(out=tv[2], in0=x2, in1=cc_, op=ALU.mult), reads=[B_qn, B_cst], writes=[B_tm])
                        kb.dve(lambda: nc.vector.tensor_tensor(out=tv[3], in0=x1, in1=ss_, op=ALU.mult), reads=[B_qn, B_cst], writes=[B_tm])
                        kb.dve(lambda: nc.vector.tensor_tensor(out=o3[:, :, o:o + 32], in0=tv[0], in1=tv[1], op=ALU.subtract),
                               reads=[B_tm], writes=[qkb_b[i]])
                        kb.dve(lambda: nc.vector.tensor_tensor(out=o3[:, :, o + 32:o + 64], in0=tv[2], in1=tv[3], op=ALU.add),
                               reads=[B_tm], writes=[qkb_b[i]])
                    def ft():
                        ins = None
                        for c in range(8):
                            ins = nc.tensor.transpose(ps_bf(6)[:, c * 128:(c + 1) * 128], qkb[i][:, c * 128:(c + 1) * 128], ident)
                        for c in range(2):
                            ins = nc.tensor.transpose(ps_bf(7)[:, c * 128:(c + 1) * 128], qkb[i][:, (8 + c) * 128:(9 + c) * 128], ident)
                        return ins
                    kb.pe(ft, reads=[qkb_b[i], B_const], writes=[PB[6], PB[7]])
                    kb.act(lambda: nc.scalar.activation(out=qT[:, :, ts], in_=ps_bf(6).rearrange("p (c n) -> p c n", c=8), func=AF.Copy),
                           reads=[PB[6]], writes=[B_q])
                    kb.act(lambda: nc.scalar.activation(out=kT[:, :, ts], in_=ps_bf(7)[:, 0:256].rearrange("p (c n) -> p c n", c=2), func=AF.Copy),
                           reads=[PB[7]], writes=[B_k])
                if seg == 1:
                    kb.dma(POOL, cc_in_gk.rearrange("(c p) n -> p c n", p=128), kT[:, :, 0:S], reads=[B_k], writes=[B_cc["in_gk"]])
                    kb.dma(POOL, cc_in_gv.rearrange("(t p) n -> p t n", p=128), Vall[:, 0:NT, :], reads=[B_v], writes=[B_cc["in_gv"]])
                    kb.collective([cc_in_gk], [cc_out_gk], reads=[B_cc["in_gk"]], writes=[B_cc["out_gk"]], groups=GROUPS)
                    kb.collective([cc_in_gv], [cc_out_gv], reads=[B_cc["in_gv"]], writes=[B_cc["out_gv"]], groups=GROUPS)
                    for r in range(4):
                        kb.dma(SP, kT[:, :, r * S:(r + 1) * S], cc_out_gk[r * 256:(r + 1) * 256, :].rearrange("(c p) n -> p c n", p=128),
                               reads=[B_cc["out_gk"]], writes=[B_k])
                        kb.dma(SP, Vall[:, r * NT:(r + 1) * NT, :], cc_out_gv[r * S:(r + 1) * S, :].rearrange("(t p) n -> p t n", p=128),
                               reads=[B_cc["out_gv"]], writes=[B_v])
            kb.barrier()
            with ExitStack() as es3:
                at = alloc_attn_tiles(es3)
                for h in range(8):
                    kvh = h // 4
                    attn_head(at, nk, [(qT[:, h, :], 128)], [(kT[:, kvh, :], 128)],
                              lambda kt, kvh=kvh: Vall[:, kt, kvh * 128:(kvh + 1) * 128], OT[:, h, :], B_OT, [B_q, B_k, B_v])
            kb.barrier()
            es.close()
            with ExitStack() as es4:
                stage_out(es4, layer, seg, OT, B_OT, gqa_w_o[j], (128, 8))
        kb.barrier()

    NBT = 2560
    qTn_d = dscr("qTn_d", [2, 8, 128, S])
    kTn_d = dscr("kTn_d", [2, 8, 128, NBT])
    vn_d = dscr("vn_d", [2, NBT, D])
    cc_in_nak = dscr("cc_in_nak", [8 * 128, 512])
    cc_out_nak = dscr("cc_out_nak", [4 * 8 * 128, 512])
    cc_in_nav = dscr("cc_in_nav", [512, D])
    cc_out_nav = dscr("cc_out_nav", [4 * 512, D])
    B_na = {k: Buf() for k in ["q0", "q1", "k0", "k1", "v0", "v1", "ink", "outk", "inv", "outv"]}

    def na_window(lr):
        if lr < 4:
            lo, hi = lr - 4, 7
        elif lr >= 28:
            lo, hi = 24, lr + 3
        else:
            lo, hi = lr - 4, lr + 3
        blo, bhi = lo + 4, hi + 4
        ws = blo - (blo % 2)
        nrows = bhi - ws + 1
        nch = (nrows + 1) // 2
        rho0 = ws - lr + 3
        layout = 0 if rho0 % 2 == 0 else 1
        pi0 = rho0 // 2
        return ws, nch, layout, pi0

    def mixer_na(layer, seg, j):
        Bq, Bk, Bv = B_na["q%d" % seg], B_na["k%d" % seg], B_na["v%d" % seg]
        with ExitStack() as es2:
            sb2 = lambda name, shape, dt=BF16: es2.enter_context(sbt(name, shape, dt))[:]
            wq = sb2("wna", [128, 8, 3072])
            B_w = Buf()
            for k3 in range(3):
                kb.dma(POOL, wq[:, :, k3 * 1024:(k3 + 1) * 1024],
                       na_w_qkv[j, :, k3 * 1024:(k3 + 1) * 1024].rearrange("(c p) n -> p c n", p=128), writes=[B_w])
            xnTb = [sb2("xnTn%d" % i, [128, 8, 512]) for i in range(2)]
            B_xnb = [Buf(), Buf()]
            stg = [sb2("stgn%d" % i, [128, 512]) for i in range(2)]
            stg_b = [Buf(), Buf()]
            vst = [sb2("vstn%d" % i, [128, D]) for i in range(2)]
            vst_b = [Buf(), Buf()]
            zt = sb2("zt", [128, 8, 256])
            B_z = Buf()
            kb.dve(lambda: nc.vector.memset(zt, 0.0), writes=[B_z])
            if seg == 0:
                kb.dma(POOL, kTn_d[seg, :, :, 0:256].rearrange("c p n -> p c n"), zt, reads=[B_z], writes=[Bk])
                kb.dma(POOL, kTn_d[seg, :, :, NBT - 256:NBT].rearrange("c p n -> p c n"), zt, reads=[B_z], writes=[Bk])
                kb.dma(POOL, vn_d[seg, 0:256, :].rearrange("(t p) n -> p t n", p=128), zt.rearrange("p a (b n) -> p (a b) n", b=2)[:, 0:2, :].rearrange("p t n -> p t n") if False else zt[:, 0:8, :].rearrange("p c n -> p (c n)")[:, 0:2048].rearrange("p (t n) -> p t n", t=2),
                       reads=[B_z], writes=[Bv])
                kb.dma(POOL, vn_d[seg, NBT - 256:NBT, :].rearrange("(t p) n -> p t n", p=128),
                       zt[:, 0:8, :].rearrange("p c n -> p (c n)")[:, 0:2048].rearrange("p (t n) -> p t n", t=2),
                       reads=[B_z], writes=[Bv])
            k_ = 0
            for blk in range(4):
                xi = blk % 2
                kb.dma(SP, xnTb[xi], xnT_d[seg, :, :, blk * 512:(blk + 1) * 512].rearrange("c p n -> p c n"),
                       reads=B_xnT[seg][blk * 4:blk * 4 + 4], writes=[B_xnb[xi]])
                xnT = xnTb[xi]
                for m in range(16):
                    i = k_ % 2
                    k_ += 1
                    bq = 4 + i
                    def f():
                        ins = None
                        for c in range(8):
                            ins = nc.tensor.matmul(PS[:, bq, :], lhsT=wq[:, c, m * 128:(m + 1) * 128], rhs=xnT[:, c, :],
                                                   start=(c == 0), stop=(c == 7))
                        return ins
                    kb.pe(f, reads=[B_w, B_xnb[xi]], writes=[PB[bq]])
                    sc_ = 0.125 if m < 8 else 1.0
                    kb.act(lambda: nc.scalar.activation(out=stg[i], in_=PS[:, bq, :], func=AF.Copy, scale=sc_),
                           reads=[PB[bq]], writes=[stg_b[i]])
                    if m < 8:
                        kb.dma(POOL, qTn_d[seg, m, :, blk * 512:(blk + 1) * 512], stg[i], reads=[stg_b[i]], writes=[Bq])
                    else:
                        kb.dma(POOL, kTn_d[seg, m - 8, :, 256 + blk * 512:256 + (blk + 1) * 512], stg[i], reads=[stg_b[i]], writes=[Bk])
                for tt in range(4):
                    t = blk * 4 + tt
                    i = t % 2
                    b0 = 6 if False else (0 + 2 * i)
                    def f():
                        ins = None
                        for half in range(2):
                            for c in range(8):
                                ins = nc.tensor.matmul(PS[:, b0 + half, :], lhsT=xnT[:, c, tt * 128:(tt + 1) * 128],
                                                       rhs=wq[:, c, 2048 + half * 512:2048 + (half + 1) * 512], start=(c == 0), stop=(c == 7))
                        return ins
                    kb.pe(f, reads=[B_w, B_xnb[xi]], writes=[PB[b0], PB[b0 + 1]])
                    kb.act(lambda: nc.scalar.activation(out=vst[i], in_=ps_f(b0, 2), func=AF.Copy), reads=[PB[b0], PB[b0 + 1]], writes=[vst_b[i]])
                    kb.dma(POOL, vn_d[seg, 256 + t * 128:256 + (t + 1) * 128, :], vst[i], reads=[vst_b[i]], writes=[Bv])
            if seg == 1:
                kb.dma(POOL, cc_in_nak[:, 0:256].rearrange("(c p) n -> c p n", p=128), kTn_d[seg, :, :, 256:512], reads=[Bk], writes=[B_na["ink"]])
                kb.dma(POOL, cc_in_nak[:, 256:512].rearrange("(c p) n -> c p n", p=128), kTn_d[seg, :, :, NBT - 512:NBT - 256], reads=[Bk], writes=[B_na["ink"]])
                kb.dma(POOL, cc_in_nav[0:256, :], vn_d[seg, 256:512, :], reads=[Bv], writes=[B_na["inv"]])
                kb.dma(POOL, cc_in_nav[256:512, :], vn_d[seg, NBT - 512:NBT - 256, :], reads=[Bv], writes=[B_na["inv"]])
                kb.collective([cc_in_nak], [cc_out_nak], reads=[B_na["ink"]], writes=[B_na["outk"]], groups=GROUPS)
                kb.collective([cc_in_nav], [cc_out_nav], reads=[B_na["inv"]], writes=[B_na["outv"]], groups=GROUPS)
                hsel = sb2("hsel", [128, 8], F32)
                B_hs = Buf()
                kb.dma(SP, hsel, na_hsel, writes=[B_hs])
                candk = sb2("candk", [128, 4, 8, 512])
                candv = sb2("candv", [128, 4, 4, D])
                B_ck, B_cv = Buf(), Buf()
                for r in range(4):
                    kb.dma(SP, candk[:, r], cc_out_nak[r * 1024:(r + 1) * 1024, :].rearrange("(c p) n -> p c n", p=128),
                           reads=[B_na["outk"]], writes=[B_ck])
                    kb.dma(SP, candv[:, r], cc_out_nav[r * 512:(r + 1) * 512, :].rearrange("(t p) n -> p t n", p=128),
                           reads=[B_na["outv"]], writes=[B_cv])
                hk = sb2("hk", [128, 2, 8, 256])
                hv = sb2("hv", [128, 2, 2, D])
                B_hk, B_hv = Buf(), Buf()
                for side in range(2):
                    ksl = slice(256, 512) if side == 0 else slice(0, 256)
                    vsl = slice(2, 4) if side == 0 else slice(0, 2)
                    for r in range(4):
                        msk = hsel[:, side * 4 + r:side * 4 + r + 1]
                        if r == 0:
                            kb.dve(lambda: nc.vector.tensor_scalar(out=hk[:, side], in0=candk[:, r, :, ksl], scalar1=msk, scalar2=None, op0=ALU.mult),
                                   reads=[B_ck, B_hs], writes=[B_hk])
                            kb.dve(lambda: nc.vector.tensor_scalar(out=hv[:, side], in0=candv[:, r, vsl, :], scalar1=msk, scalar2=None, op0=ALU.mult),
                                   reads=[B_cv, B_hs], writes=[B_hv])
                        else:
                            kb.dve(lambda: nc.vector.scalar_tensor_tensor(out=hk[:, side], in0=candk[:, r, :, ksl], scalar=msk, in1=hk[:, side],
                                                                          op0=ALU.mult, op1=ALU.add), reads=[B_ck, B_hs, B_hk], writes=[B_hk])
                            kb.dve(lambda: nc.vector.scalar_tensor_tensor(out=hv[:, side], in0=candv[:, r, vsl, :], scalar=msk, in1=hv[:, side],
                                                                          op0=ALU.mult, op1=ALU.add), reads=[B_cv, B_hs, B_hv], writes=[B_hv])
                kb.dma(POOL, kTn_d[seg, :, :, 0:256].rearrange("c p n -> p c n"), hk[:, 0], reads=[B_hk], writes=[Bk])
                kb.dma(POOL, kTn_d[seg, :, :, NBT - 256:NBT].rearrange("c p n -> p c n"), hk[:, 1], reads=[B_hk], writes=[Bk])
                kb.dma(POOL, vn_d[seg, 0:256, :].rearrange("(t p) n -> p t n", p=128), hv[:, 0], reads=[B_hv], writes=[Bv])
                kb.dma(POOL, vn_d[seg, NBT - 256:NBT, :].rearrange("(t p) n -> p t n", p=128), hv[:, 1], reads=[B_hv], writes=[Bv])
        kb.barrier()
        with ExitStack() as es0:
            Otok = es0.enter_context(sbt("Otok", [128, NT, D], BF16))[:]
            B_Ot = Buf()
            with ExitStack() as es:
                sb = lambda name, shape, dt=BF16: es.enter_context(sbt(name, shape, dt))[:]
                rmask = sb("rmask", [128, 32, 6], F32)
                B_rm = Buf()
                kb.dma(SP, rmask, na_rmask[seg], writes=[B_rm])
                qg = [sb("qg%d" % i, [64, 4, S]) for i in range(2)]
                kg = [sb("kg%d" % i, [64, 4, NBT]) for i in range(2)]
                vg = [sb("vg%d" % i, [128, 20, 4, 72]) for i in range(2)]
                tabs = [[sb("tab%d_%d" % (i, l), [128, 8, 4, 64], F32) for l in range(2)] for i in range(2)]
                B_g = [Buf(), Buf()]
                for i in range(2):
                    kb.dve(lambda: nc.vector.memset(vg[i][:, :, :, 64:65], 1.0), writes=[B_g[i]])
                scb = [sb("scb%d" % i, [128, 6, 256], F32) for i in range(2)]
                scb_b = [Buf(), Buf()]
                PTn = [sb("PTn%d" % i, [128, 6, 4, 64]) for i in range(2)]
                PTn_b = [Buf(), Buf()]
                rec = sb("recn", [64, 8], F32)
                rec_b = Buf()
                vstg = sb("vstg", [128, 20, 256])
                B_vs = Buf()
                if NA_DBG == 'A':
                    kb.dve(lambda: nc.vector.memset(Otok, 0.0), writes=[B_Ot])
                for G in range(4 if NA_DBG != 'A' else 0):
                    gi = G % 2
                    kb.dma(SP, qg[gi], qTn_d[seg].rearrange("c p n -> (c p) n")[4 * G * 64:(4 * G + 4) * 64, :].rearrange("(h p) n -> p h n", p=64),
                           reads=[Bq], writes=[B_g[gi]])
                    kb.dma(SP, kg[gi], kTn_d[seg].rearrange("c p n -> (c p) n")[4 * G * 64:(4 * G + 4) * 64, :].rearrange("(h p) n -> p h n", p=64),
                           reads=[Bk], writes=[B_g[gi]])
                    for q5 in range(4):
                        kb.dma(SP, vstg[:, q5 * 5:(q5 + 1) * 5, :],
                               vn_d[seg, q5 * 640:(q5 + 1) * 640, 4 * G * 64:(4 * G + 4) * 64].rearrange("(c p) n -> p c n", p=128),
                               reads=[Bv], writes=[B_vs])
                    kb.dve(lambda: nc.vector.tensor_copy(out=vg[gi][:, :, :, 0:64], in_=vstg.rearrange("p c (h d) -> p c h d", d=64)),
                           reads=[B_vs], writes=[B_g[gi]])
                    for l in range(2 if NA_DBG != 'L2' else 0):
                        kb.dma(SP, tabs[gi][l].rearrange("p a h c -> p (a h c)"), na_tab[l, G], writes=[B_g[gi]])
                    if NA_DBG in ('L', 'L2', 'S', 'S0', 'S1'):
                        kb.dve(lambda: nc.vector.memset(Otok, 0.0), writes=[B_Ot])
                    for lr in range(32 if NA_DBG not in ('L', 'L2') else 0):
                        ws, nch, layout, pi0 = na_window(lr)
                        i = lr % 2
                        sb0 = 3 * i
                        ob = 6 + i
                        scv = PS[:, sb0:sb0 + 3, :].rearrange("p b n -> p (b n)")
                        def f():
                            ins = None
                            for ci in range(nch):
                                for hh in range(4):
                                    col = (ci * 4 + hh) * 64
                                    ins = nc.tensor.matmul(scv[:, col:col + 64],
                                                           lhsT=kg[gi][0:64, hh, (ws + 2 * ci) * 64:(ws + 2 * ci) * 64 + 128],
                                                           rhs=qg[gi][0:64, hh, lr * 64:(lr + 1) * 64], start=True, stop=True)
                            return ins
                        kb.pe(f, reads=[B_g[gi]], writes=[PB[sb0], PB[sb0 + 1], PB[sb0 + 2]])
                        n_el = nch * 256
                        if NA_DBG == 'S0':
                            continue
                        kb.dve(lambda: nc.vector.tensor_tensor(out=scb[i].rearrange("p a n -> p (a n)")[:, 0:n_el], in0=scv[:, 0:n_el],
                                                               in1=tabs[gi][layout][:, pi0:pi0 + nch].rearrange("p a h c -> p (a h c)"),
                                                               op=ALU.add),
                               reads=[PB[sb0], PB[sb0 + 1], PB[sb0 + 2], B_g[gi]], writes=[scb_b[i]])
                        if NA_DBG == 'S1':
                            continue
                        for ci in range(nch):
                            kb.act(lambda: nc.scalar.activation(out=PTn[i][:, ci].rearrange("p h c -> p (h c)"), in_=scb[i][:, ci, :],
                                                                func=AF.Exp, bias=rmask[:, lr, ci:ci + 1]),
                                   reads=[scb_b[i], B_rm], writes=[PTn_b[i]])
                        if NA_DBG == 'S':
                            continue
                        ov = PS[0:64, ob, 0:260].rearrange("p (h n) -> p h n", n=65)
                        def f2():
                            ins = None
                            for hh in range(4):
                                for ci in range(nch):
                                    ins = nc.tensor.matmul(ov[:, hh, :], lhsT=PTn[i][:, ci, hh, :], rhs=vg[gi][:, ws // 2 + ci, hh, 0:65],
                                                           start=(ci == 0), stop=(ci == nch - 1))
                            return ins
                        kb.pe(f2, reads=[PTn_b[i], B_g[gi]], writes=[PB[ob]])
                        kb.dve(lambda: nc.vector.reciprocal(out=rec[:, 4 * i:4 * i + 4], in_=ov[:, :, 64]), reads=[PB[ob]], writes=[rec_b])
                        pr = (lr % 2) * 64
                        for hh in range(4):
                            kb.dve(lambda: nc.vector.tensor_scalar(out=Otok[pr:pr + 64, lr // 2, (4 * G + hh) * 64:(4 * G + hh + 1) * 64],
                                                                   in0=ov[:, hh, 0:64], scalar1=rec[:, 4 * i + hh:4 * i + hh + 1],
                                                                   scalar2=None, op0=ALU.mult),
                                   reads=[PB[ob], rec_b], writes=[B_Ot])
            kb.barrier()
            if NA_DBG == 'O':
                with ExitStack() as esd:
                    tcp = esd.enter_context(sbt("tcpo", [128, D], F32))[:]
                    bcp = Buf()
                    for t in range(NT):
                        kb.act(lambda: nc.scalar.activation(out=tcp, in_=Otok[:, t, :], func=AF.Copy), reads=[B_Ot], writes=[bcp])
                        kb.dma(SP, y_out[seg, t * 128:(t + 1) * 128, :], tcp, reads=[bcp], writes=[Buf()])
                kb.barrier()
            with ExitStack() as es4:
                OT = es4.enter_context(sbt("OTn", [128, 8, S], BF16))[:]
                B_OT = Buf()
                for t in range(NT):
                    pbank = t % 2
                    pv = transpose_to(Otok[:, t, :], B_Ot, 8, pbank, None, None)
                    kb.act(lambda: nc.scalar.activation(out=OT[:, :, t * 128:(t + 1) * 128], in_=pv.rearrange("p (c n) -> p c n", c=8),
                                                        func=AF.Copy), reads=[PB[pbank]], writes=[B_OT])
                kb.barrier()
                stage_out(es4, layer, seg, OT, B_OT, na_w_o[j], (128, 8))
        kb.barrier()

    def weights_to_bf16(layer):
        for r in range(8):
            kb.dma(POOL, w_in_bf[r * 128:(r + 1) * 128, :], ffn_w_in[layer, r * 128:(r + 1) * 128, :],
                   reads=[], writes=[B_win])
        for r in range(8):
            kb.dma(POOL, w_out_bf[r * 512:(r + 1) * 512, :], ffn_w_out[layer, r * 512:(r + 1) * 512, :],
                   reads=[], writes=[B_wout])

    def stage_ffn(layer, seg):
        TB = 512
        NB = S // TB
        with ExitStack() as es:
            sb = lambda name, shape, dt=BF16: es.enter_context(sbt(name, shape, dt))[:]
            halo = sb("halo", [128, 8, 2])
            B_halo = Buf()
            if seg == 0:
                kb.dve(lambda: nc.vector.memset(halo, 0.0), writes=[B_halo])
            else:
                cand = sb("cand", [8, D])
                selm = sb("selm", [8, 2])
                B_cand = Buf()
                kb.dma(SP, cand, cc_out_h, reads=[B_cc["out_h"]], writes=[B_cand])
                kb.dma(POOL, selm, halo_sel, writes=[B_cand])
                def f():
                    ins = None
                    for c in range(8):
                        ins = nc.tensor.matmul(PS[:, 0, 2 * c:2 * c + 2], lhsT=cand[:, c * 128:(c + 1) * 128], rhs=selm,
                                               start=True, stop=True)
                    return ins
                kb.pe(f, reads=[B_cand], writes=[PB[0]])
                kb.dve(lambda: nc.vector.tensor_copy(out=halo, in_=PS[:, 0, 0:16].rearrange("p (c n) -> p c n", n=2)),
                       reads=[PB[0]], writes=[B_halo])
            cw = sb("cw", [128, 3, 32], F32)
            cbias = sb("cbias", [128, 32], F32)
            B_cw = Buf()
            kb.dma(SP, cw, ffn_conv_w[layer], writes=[B_cw])
            kb.dma(SP, cbias, ffn_conv_b[layer], writes=[B_cw])
            w_gate = sb("w_gate", [128, 8, D])
            w_proj = sb("w_proj", [128, 2, D])
            B_wg = Buf()
            kb.dma(POOL, w_gate, ple_w_gate[layer].rearrange("(c p) n -> p c n", p=128), writes=[B_wg])
            kb.dma(POOL, w_proj, ple_w_proj[layer].rearrange("(c p) n -> p c n", p=128), writes=[B_wg])
            gfpost, gfpost_b = load_gain(es, "gfpost", g_ffn_post[layer:layer + 1, :])
            gple, gple_b = load_gain(es, "gple", g_ple[layer:layer + 1, :])
            if layer + 1 < nlayers:
                gnext, gnext_b = load_gain(es, "gnext", g_mix_pre[layer + 1:layer + 2, :])
            xfb = [sb("xfb%d" % i, [128, 8, TB + 2]) for i in range(2)]
            xfb_b = [Buf(), Buf()]
            hT = sb("hT", [128, 32, TB])
            B_hT = Buf()
            win = [sb("win%d" % i, [128, 8, 256]) for i in range(3)]
            win_b = [Buf() for _ in range(3)]
            wout = [sb("wout%d" % i, [128, 32, 256]) for i in range(2)]
            wout_b = [Buf(), Buf()]
            gs = [sb("gs%d" % i, [128, TB + 2], F32) for i in range(2)]
            gs_b = [Buf(), Buf()]
            ta = [sb("ta%d" % i, [128, TB], F32) for i in range(2)]
            ta_b = [Buf(), Buf()]
            tb_ = [sb("tb%d" % i, [128, TB], F32) for i in range(2)]
            tb_b = [Buf(), Buf()]
            yblk = [sb("yblk%d" % i, [128, D], F32) for i in range(4)]
            yblk_b = [Buf() for _ in range(4)]
            tl1 = alloc_norm_tiles(es, "f0")
            xs1 = sb("xf0", [128, D], F32)
            xs1_b = Buf()
            x21 = sb("x2f0", [128, D], F32)
            x21_b = Buf()
            pt1 = sb("pt0", [128, PLE])
            pt1_b = Buf()
            pT1 = sb("pT0", [128, 2, 128])
            pT1_b = Buf()
            gate1 = sb("gate0", [128, D], F32)
            gate1_b = Buf()
            st3 = sb("st3", [128, 32], F32)
            st3_b = Buf()
            def rms_steps(src_ap, src_bufs, n, scr, scr_b, col):
                kb.act(lambda: nc.scalar.activation(out=scr[:, 0:n], in_=src_ap, func=AF.Square, accum_out=st3[:, col:col + 1]),
                       reads=src_bufs, writes=[st3_b, scr_b])
                yield
                kb.act(lambda: nc.scalar.activation(out=st3[:, col + 1:col + 2], in_=st3[:, col:col + 1], func=AF.Sqrt, scale=1.0 / n, bias=EPS),
                       reads=[st3_b], writes=[st3_b])
                yield
                kb.dve(lambda: nc.vector.reciprocal(out=st3[:, col + 2:col + 3], in_=st3[:, col + 1:col + 2]), reads=[st3_b], writes=[st3_b])
                yield

            def epi(blk, tt):
                t = blk * (TB // 128) + tt
                ts = slice(t * 128, (t + 1) * 128)
                xs_, xs_b_, x2_, x2_b_, pt_, pt_b_, pT_, pT_b_, gate_, gate_b_ = xs1, xs1_b, x21, x21_b, pt1, pt1_b, pT1, pT1_b, gate1, gate1_b
                tl = tl1
                scr, scr_b = tl[0], tl[1]
                xn, xn_b = tl[4], tl[5]
                xT, xT_b = tl[6], tl[7]
                y = yblk[tt]
                kb.dma(SP, xs_, xres[seg, ts, :], reads=[B_xres[seg][t]], writes=[xs_b_])
                yield
                kb.dma(POOL, pt_, p_in[layer, seg, ts, :], writes=[pt_b_])
                yield
                yield from rms_steps(y, [yblk_b[tt]], D, scr, scr_b, 0)
                kb.dve(lambda: nc.vector.scalar_tensor_tensor(out=scr, in0=y, scalar=st3[:, 2:3], in1=gfpost[:], op0=ALU.mult, op1=ALU.mult),
                       reads=[yblk_b[tt], st3_b, gfpost_b], writes=[scr_b])
                yield
                kb.dve(lambda: nc.vector.tensor_tensor(out=x2_, in0=scr, in1=xs_, op=ALU.add), reads=[scr_b, xs_b_], writes=[x2_b_])
                yield
                yield from rms_steps(x2_, [x2_b_], D, scr, scr_b, 4)
                kb.dve(lambda: nc.vector.tensor_scalar(out=xn, in0=x2_, scalar1=st3[:, 6:7], scalar2=None, op0=ALU.mult),
                       reads=[x2_b_, st3_b], writes=[xn_b])
                yield
                pv = transpose_to(xn, xn_b, 8, 6, None, None)
                yield
                kb.act(lambda: nc.scalar.activation(out=xT, in_=pv, func=AF.Copy), reads=[PB[6]], writes=[xT_b])
                yield
                def fp():
                    ins = None
                    for c in range(2):
                        ins = nc.tensor.transpose(ps_bf(7)[:, c * 128:(c + 1) * 128], pt_[:, c * 128:(c + 1) * 128], ident)
                    return ins
                kb.pe(fp, reads=[pt_b_, B_const], writes=[PB[7]])
                yield
                kb.act(lambda: nc.scalar.activation(out=pT_, in_=ps_bf(7)[:, 0:256].rearrange("p (c n) -> p c n", c=2), func=AF.Copy),
                       reads=[PB[7]], writes=[pT_b_])
                yield
                for half in range(2):
                    bkx = 6 + half
                    hs_ = slice(half * 512, (half + 1) * 512)
                    def fg():
                        ins = None
                        for c in range(8):
                            ins = nc.tensor.matmul(PS[:, bkx, :], lhsT=xT[:, c * 128:(c + 1) * 128], rhs=w_gate[:, c, hs_],
                                                   start=(c == 0), stop=(c == 7))
                        return ins
                    kb.pe(fg, reads=[xT_b, B_wg], writes=[PB[bkx]])
                    yield
                    kb.act(lambda: nc.scalar.activation(out=gate_[:, hs_], in_=PS[:, bkx, :], func=AF.Sigmoid),
                           reads=[PB[bkx]], writes=[gate_b_])
                    yield
                for half in range(2):
                    bkx = 6 + half
                    hs_ = slice(half * 512, (half + 1) * 512)
                    def fe():
                        ins = None
                        for c in range(2):
                            ins = nc.tensor.matmul(PS[:, bkx, :], lhsT=pT_[:, c, :], rhs=w_proj[:, c, hs_], start=(c == 0), stop=(c == 1))
                        return ins
                    kb.pe(fe, reads=[pT_b_, B_wg], writes=[PB[bkx]])
                    yield
                    kb.dve(lambda: nc.vector.tensor_tensor(out=gate_[:, hs_], in0=gate_[:, hs_], in1=PS[:, bkx, :], op=ALU.mult),
                           reads=[PB[bkx], gate_b_], writes=[gate_b_])
                    yield
                yield from rms_steps(gate_, [gate_b_], D, scr, scr_b, 8)
                kb.dve(lambda: nc.vector.scalar_tensor_tensor(out=scr, in0=gate_, scalar=st3[:, 10:11], in1=gple[:], op0=ALU.mult, op1=ALU.mult),
                       reads=[gate_b_, st3_b, gple_b], writes=[scr_b])
                yield
                kb.dve(lambda: nc.vector.tensor_tensor(out=xs_, in0=scr, in1=x2_, op=ALU.add), reads=[scr_b, x2_b_], writes=[xs_b_])
                yield
                if layer + 1 < nlayers:
                    kb.dma(POOL, xres[seg, ts, :], xs_, reads=[xs_b_], writes=[B_xres[seg][t]])
                    yield
                    yield from rms_steps(xs_, [xs_b_], D, scr, scr_b, 12)
                    kb.dve(lambda: nc.vector.scalar_tensor_tensor(out=xn, in0=xs_, scalar=st3[:, 14:15], in1=gnext[:], op0=ALU.mult, op1=ALU.mult),
                           reads=[xs_b_, st3_b, gnext_b], writes=[xn_b])
                    yield
                    pv2 = transpose_to(xn, xn_b, 8, 6, None, None)
                    yield
                    kb.act(lambda: nc.scalar.activation(out=xT, in_=pv2, func=AF.Copy), reads=[PB[6]], writes=[xT_b])
                    yield
                    kb.dma(POOL, xnT_d[seg, :, :, t * 128:(t + 1) * 128].rearrange("c p n -> p c n"),
                           xT.rearrange("p (c n) -> p c n", c=8), reads=[xT_b], writes=[B_xnT[seg][t]])
                    yield
                else:
                    kb.dma(POOL, y_out[seg, ts, :], xs_, reads=[xs_b_], writes=[B_xres[seg][t]])
                    yield

            pending = iter(())
            wk = 0
            wo_k = 0
            for blk in range(NB):
                c0 = blk * TB
                xi = blk % 2
                xfT = xfb[xi]
                B_xf = xfb_b[xi]
                lo = c0 - 1 if blk > 0 else c0
                hi = c0 + TB + 1 if blk < NB - 1 else c0 + TB
                kb.dma(SP, xfT[:, :, (lo - (c0 - 1)):(hi - (c0 - 1))], xfT_d[seg, :, :, lo:hi].rearrange("c p n -> p c n"),
                       reads=B_xfT[seg], writes=[B_xf])
                if blk == 0:
                    kb.dve(lambda: nc.vector.tensor_copy(out=xfT[:, :, 0:1], in_=halo[:, :, 0:1]), reads=[B_halo], writes=[B_xf])
                if blk == NB - 1:
                    kb.dve(lambda: nc.vector.tensor_copy(out=xfT[:, :, TB + 1:TB + 2], in_=halo[:, :, 1:2]), reads=[B_halo], writes=[B_xf])
                for ch in range(32):
                    wi = wk % 3
                    wk += 1
                    kb.dma(SP, win[wi][:, :, 0:128], w_in_bf[:, ch * 128:(ch + 1) * 128].rearrange("(c p) n -> p c n", p=128),
                           reads=[B_win], writes=[win_b[wi]])
                    kb.dma(SP, win[wi][:, :, 128:256],
                           w_in_bf[:, DFF + ch * 128:DFF + (ch + 1) * 128].rearrange("(c p) n -> p c n", p=128),
                           reads=[B_win], writes=[win_b[wi]])
                    i = ch % 2
                    bg, bh, bu = 3 * i, 3 * i + 1, 3 * i + 2
                    def f():
                        ins = None
                        for c in range(8):
                            nc.tensor.matmul(PS[:, bg, :], lhsT=win[wi][:, c, 0:128], rhs=xfT[:, c, 0:TB],
                                             start=(c == 0), stop=(c == 7))
                        for c in range(8):
                            nc.tensor.matmul(PS[:, bh, 0:2], lhsT=win[wi][:, c, 0:128], rhs=xfT[:, c, TB:TB + 2],
                                             start=(c == 0), stop=(c == 7))
                        for c in range(8):
                            ins = nc.tensor.matmul(PS[:, bu, :], lhsT=win[wi][:, c, 128:256], rhs=xfT[:, c, 1:1 + TB],
                                                   start=(c == 0), stop=(c == 7))
                        return ins
                    kb.pe(f, reads=[win_b[wi], B_xf], writes=[PB[bg], PB[bh], PB[bu]])
                    kb.act(lambda: nc.scalar.activation(out=gs[i][:, 0:TB], in_=PS[:, bg, :], func=AF.Copy),
                           reads=[PB[bg]], writes=[gs_b[i]])
                    kb.act(lambda: nc.scalar.activation(out=gs[i][:, TB:TB + 2], in_=PS[:, bh, 0:2], func=AF.Copy),
                           reads=[PB[bh]], writes=[gs_b[i]])
                    kb.dve(lambda: nc.vector.tensor_scalar(out=ta[i], in0=gs[i][:, 1:TB + 1], scalar1=cw[:, 1, ch:ch + 1],
                                                           scalar2=cbias[:, ch:ch + 1], op0=ALU.mult, op1=ALU.add),
                           reads=[gs_b[i], B_cw], writes=[ta_b[i]])
                    kb.dve(lambda: nc.vector.scalar_tensor_tensor(out=tb_[i], in0=gs[i][:, 0:TB], scalar=cw[:, 0, ch:ch + 1],
                                                                  in1=ta[i], op0=ALU.mult, op1=ALU.add),
                           reads=[gs_b[i], B_cw, ta_b[i]], writes=[tb_b[i]])
                    kb.dve(lambda: nc.vector.scalar_tensor_tensor(out=ta[i], in0=gs[i][:, 2:TB + 2], scalar=cw[:, 2, ch:ch + 1],
                                                                  in1=tb_[i], op0=ALU.mult, op1=ALU.add),
                           reads=[gs_b[i], B_cw, tb_b[i]], writes=[ta_b[i]])
                    kb.act(lambda: nc.scalar.activation(out=tb_[i], in_=ta[i], func=AF.Gelu_apprx_tanh),
                           reads=[ta_b[i]], writes=[tb_b[i]])
                    kb.dve(lambda: nc.vector.tensor_tensor(out=hT[:, ch, :], in0=tb_[i], in1=PS[:, bu, :], op=ALU.mult),
                           reads=[tb_b[i], PB[bu]], writes=[B_hT])
                    for _ in range(5):
                        next(pending, None)
                for _ in pending:
                    pass
                for q4 in range(4):
                    wo = wo_k % 2
                    wo_k += 1
                    kb.dma(SP, wout[wo], w_out_bf[:, q4 * 256:(q4 + 1) * 256].rearrange("(c p) n -> p c n", p=128),
                           reads=[B_wout], writes=[wout_b[wo]])
                    for tt in range(TB // 128):
                        by = 6 + (tt % 2)
                        def f():
                            ins = None
                            for c in range(32):
                                ins = nc.tensor.matmul(PS[:, by, 0:256], lhsT=hT[:, c, tt * 128:(tt + 1) * 128], rhs=wout[wo][:, c, :],
                                                       start=(c == 0), stop=(c == 31))
                            return ins
                        kb.pe(f, reads=[B_hT, wout_b[wo]], writes=[PB[by]])
                        kb.act(lambda: nc.scalar.activation(out=yblk[tt][:, q4 * 256:(q4 + 1) * 256], in_=PS[:, by, 0:256], func=AF.Copy),
                               reads=[PB[by]], writes=[yblk_b[tt]])
                pending = itertools.chain(*[epi(blk, tt) for tt in range(TB // 128)])
            for _ in pending:
                pass
        kb.barrier()

    stage_initial()
    for layer in range(nlayers):
        kind, j = layer % 3, layer // 3
        for seg in (1, 0):
            if kind == 0:
                mixer_mla(layer, seg, j)
            elif kind == 1:
                mixer_gqa(layer, seg, j)
            else:
                mixer_na(layer, seg, j)
            if seg == 1:
                weights_to_bf16(layer)
        if STOP_MIX and layer == nlayers - 1 and NA_DBG == 'O':
            break
        if STOP_MIX and layer == nlayers - 1:
            with ExitStack() as esd:
                tcp = esd.enter_context(sbt("tcp", [128, D], F32))[:]
                bcp = Buf()
                for seg in (0, 1):
                    for t in range(NT):
                        kb.dma(SP, tcp, xres[seg, t * 128:(t + 1) * 128, :], reads=[B_xres[seg][t]], writes=[bcp])
                        kb.dma(SP, y_out[seg, t * 128:(t + 1) * 128, :], tcp, reads=[bcp], writes=[B_xres[seg][t]])
            break
        for seg in (0, 1):
            stage_ffn(layer, seg)
    kb.barrier()
    return nc


def _na_window(lr):
    if lr < 4:
        lo, hi = lr - 4, 7
    elif lr >= 28:
        lo, hi = 24, lr + 3
    else:
        lo, hi = lr - 4, lr + 3
    blo, bhi = lo + 4, hi + 4
    ws = blo - (blo % 2)
    nrows = bhi - ws + 1
    nch = (nrows + 1) // 2
    rho0 = ws - lr + 3
    layout = 0 if rho0 % 2 == 0 else 1
    pi0 = rho0 // 2
    return ws, nch, layout, pi0


def _rope_tables(pos, dim):
    inv = 1.0 / (10000.0 ** (np.arange(0, dim, 2, dtype=np.float32) / np.float32(dim)))
    ang = pos.astype(np.float32)[:, None] * inv[None, :].astype(np.float32)
    return np.cos(ang).astype(np.float32), np.sin(ang).astype(np.float32)


_NC_CACHE = {}


def kernel(_nlayers=DEPTH, **inputs):
    inp = {k: np.ascontiguousarray(np.asarray(v)) for k, v in inputs.items()}
    if _nlayers not in _NC_CACHE:
        _NC_CACHE[_nlayers] = build_program(_nlayers)
    nc = _NC_CACHE[_nlayers]
    f32 = np.float32
    w_uq = inp["mla_w_uq"]
    rp = np.empty((2, 384, 512), f32)
    for h in range(8):
        base = h * 192 + 128
        idx = base + (np.arange(64) + 32) % 64
        rp[:, :, h * 64:(h + 1) * 64] = w_uq[:, :, idx]
    shared = {k: inp[k] for k in ["norm_mix_pre", "norm_mix_post", "norm_ffn_pre", "norm_ffn_post", "ple_norm",
                                  "mla_w_down", "mla_q_norm", "mla_kv_norm", "mla_w_uq", "mla_w_ukv", "mla_w_o",
                                  "gqa_w_qkv", "gqa_q_norm", "gqa_k_norm", "gqa_w_o", "na_w_qkv", "na_w_o",
                                  "ffn_w_in", "ffn_w_out", "ple_w_proj", "ple_w_gate"]}
    shared["ffn_conv_wT"] = np.ascontiguousarray(inp["ffn_conv_w"].reshape(DEPTH, 3, 32, 128).transpose(0, 3, 1, 2))
    shared["ffn_conv_bT"] = np.ascontiguousarray(inp["ffn_conv_b"].reshape(DEPTH, 32, 128).transpose(0, 2, 1))
    shared["mla_w_uq_rp"] = rp
    shared["ident"] = np.eye(128, dtype=f32)
    rpb = inp["na_rpb"][0]
    cq = np.arange(64)
    c0 = np.clip(cq - 8, 0, 48)
    kcs = np.arange(64)
    valid = (kcs[:, None] >= c0[None, :]) & (kcs[:, None] < c0[None, :] + 16)
    relc = np.clip(kcs[:, None] - cq[None, :] + 15, 0, 30)
    T2 = np.full((16, 17, 64, 64), NEG, f32)
    for rho in range(15):
        T2[:, rho] = np.where(valid[None], rpb[:, rho][:, relc], f32(NEG))
    na_tab = np.empty((2, 128, 8, 16, 64), f32)
    for layout in range(2):
        for k in range(8):
            for jj in range(2):
                rho = 2 * k + jj + layout
                na_tab[layout, jj * 64:(jj + 1) * 64, k] = T2[:, rho].transpose(1, 0, 2)
    shared["na_tab"] = np.ascontiguousarray(
        na_tab.reshape(2, 128, 8, 4, 4, 64).transpose(0, 3, 1, 2, 4, 5).reshape(2, 4, 128, 8 * 4 * 64))
    in_maps = []
    scale = f32(192.0 ** -0.5)
    for core in range(8):
        grp, qtr = core // 4, core % 4
        m = dict(shared)
        m["x_in"] = np.stack([inp["x_prompt"][core], inp["x_sample"][grp, qtr * S:(qtr + 1) * S]], 0)
        m["p_in"] = np.stack([inp["p_prompt"][:, core], inp["p_sample"][:, grp, qtr * S:(qtr + 1) * S]], 1)
        cs_tok = np.zeros((2, S, 64), f32)
        cs_feat = np.zeros((2, 2, 64, S), f32)
        gq_tok = np.zeros((2, S, 128), f32)
        for seg in range(2):
            pos = np.arange(S) + (0 if seg == 0 else qtr * S)
            c, s = _rope_tables(pos, 64)
            cs_tok[seg, :, 0:32] = c
            cs_tok[seg, :, 32:64] = s
            cs_feat[seg, 0] = np.concatenate([c, c], 1).T * scale
            cs_feat[seg, 1] = np.concatenate([-s, s], 1).T * scale
            rc, rs = _rope_tables(pos // 64, 64)
            cc, cs_ = _rope_tables(pos % 64, 64)
            gq_tok[seg] = np.concatenate([rc, rs, cc, cs_], 1)
        m["mla_cs_tok"] = cs_tok
        m["mla_cs_feat"] = cs_feat
        m["gqa_cs_tok"] = gq_tok
        hs = np.zeros((8, 2), f32)
        if qtr > 0:
            hs[2 * (qtr - 1) + 1, 0] = 1.0
        if qtr < 3:
            hs[2 * (qtr + 1), 1] = 1.0
        m["halo_sel"] = hs
        rm = np.full((2, 128, 32, 6), NEG, f32)
        for seg in range(2):
            base_row = 0 if seg == 0 else qtr * 32
            R = 32 if seg == 0 else 128
            for lr in range(32):
                ws, nch, layout, pi0 = _na_window(lr)
                rq = base_row + lr
                r0 = min(max(rq - 4, 0), R - 8)
                for ci in range(nch):
                    for jj in range(2):
                        rg = base_row + (ws + 2 * ci + jj - 4)
                        if r0 <= rg <= r0 + 7:
                            rm[seg, jj * 64:(jj + 1) * 64, lr, ci] = 0.0
        m["na_rmask"] = rm
        hsel = np.zeros((128, 8), f32)
        if qtr > 0:
            hsel[:, qtr - 1] = 1.0
        if qtr < 3:
            hsel[:, 4 + qtr + 1] = 1.0
        m["na_hsel"] = hsel
        in_maps.append(m)
    res = run_bass_kernel_spmd(nc, in_maps, core_ids=list(range(8)))
    outs = [r["y_out"] for r in res.results]
    y_prompt = np.stack([outs[c][0] for c in range(8)], 0).astype(f32)
    y_sample = np.stack([np.concatenate([outs[g * 4 + q][1] for q in range(4)], 0) for g in range(2)], 0).astype(f32)
    return (y_prompt, y_sample)
```

```python
import numpy as np
import ml_dtypes
from contextlib import ExitStack
import itertools
import concourse.bass as bass
import concourse.mybir as mybir
from concourse.bass_utils import run_bass_kernel_spmd

F32 = mybir.dt.float32
BF16 = mybir.dt.bfloat16
AF = mybir.ActivationFunctionType
ALU = mybir.AluOpType

D = 1024
S = 2048
NT = S // 128
DEPTH = 4
DFF = 4096
PLE = 256
EPS = 1e-6
NEG = -30000.0
SEM_ROT = 6000
NO_CC = False
NA_DBG = ''
STOP_MIX = False


class Buf:
    __slots__ = ("w", "rs")

    def __init__(self):
        self.w = None
        self.rs = []


class Eng:
    def __init__(self, kb, name, eng):
        self.kb = kb
        self.name = name
        self.eng = eng
        self.sem = None
        self.semi = -1
        self.cnt = 0
        self.waited = {}
        self.newsem()

    def newsem(self):
        self.sem = self.kb.nc.alloc_semaphore()
        self.kb.sems.append(self.sem)
        self.semi = len(self.kb.sems) - 1
        self.cnt = 0

    def wait(self, tok):
        si, val, _ = tok
        cur = self.kb.dma_slot_val.get(si)
        if cur is not None and cur > val:
            val = cur
        if self.waited.get(si, 0) >= val:
            return
        self.eng.wait_ge(self.kb.sems[si], val)
        self.waited[si] = val

    def cur(self):
        return (self.semi, self.cnt, self) if self.cnt > 0 else None


class KB:
    def __init__(self):
        self.nc = bass.Bass("TRN2", target_bir_lowering=False)
        nc = self.nc
        self.sems = []
        self.PE = Eng(self, "pe", nc.tensor)
        self.ACT = Eng(self, "act", nc.scalar)
        self.DVE = Eng(self, "dve", nc.vector)
        self.POOL = Eng(self, "pool", nc.gpsimd)
        self.SP = Eng(self, "sp", nc.sync)
        self.engs = [self.PE, self.ACT, self.DVE, self.POOL, self.SP]
        self.dma_sems = []
        for _ in range(16):
            s = nc.alloc_semaphore()
            self.sems.append(s)
            self.dma_sems.append([len(self.sems) - 1, 0])
        self.dma_rr = 0
        self.dma_slot_val = {}
        self.old_final = []

    def _deps(self, E, reads, writes):
        for b in reads:
            if b.w is not None:
                if b.w[2] is E and E is self.PE:
                    continue
                E.wait(b.w)
        for b in writes:
            if b.w is not None and b.w[2] is not E:
                E.wait(b.w)
            for t in b.rs:
                if t[2] is not E:
                    E.wait(t)

    def _commit(self, tok, reads, writes):
        for b in writes:
            b.w = tok
            b.rs = []
        for b in reads:
            b.rs.append(tok)
            if len(b.rs) > 12:
                best = {}
                for t in b.rs:
                    if t[0] not in best or best[t[0]][1] < t[1]:
                        best[t[0]] = t
                b.rs = list(best.values())

    def op(self, E, fn, reads=(), writes=()):
        if E.cnt >= SEM_ROT:
            self.old_final.append(E.cur())
            E.newsem()
        self._deps(E, reads, writes)
        ins = fn()
        E.cnt += 1
        ins.then_inc(E.sem, 1)
        tok = (E.semi, E.cnt, E)
        self._commit(tok, reads, writes)
        return tok

    def pe(self, fn, reads=(), writes=()):
        return self.op(self.PE, fn, reads, writes)

    def act(self, fn, reads=(), writes=()):
        return self.op(self.ACT, fn, reads, writes)

    def dve(self, fn, reads=(), writes=()):
        return self.op(self.DVE, fn, reads, writes)

    def pool(self, fn, reads=(), writes=()):
        return self.op(self.POOL, fn, reads, writes)

    def dma(self, Q, out, in_, reads=(), writes=()):
        self._deps(Q, reads, writes)
        slot = self.dma_sems[self.dma_rr]
        self.dma_rr = (self.dma_rr + 1) % len(self.dma_sems)
        if slot[1] >= SEM_ROT:
            self.old_final.append((slot[0], slot[1], None))
            s = self.nc.alloc_semaphore()
            self.sems.append(s)
            slot[0] = len(self.sems) - 1
            slot[1] = 0
        Q.eng.dma_start(out=out, in_=in_).then_inc(self.sems[slot[0]], 16)
        slot[1] += 16
        self.dma_slot_val[slot[0]] = slot[1]
        tok = (slot[0], slot[1], None)
        self._commit(tok, reads, writes)
        return tok

    def collective(self, ins, outs, reads, writes, groups):
        Q = self.POOL
        if NO_CC:
            n = ins[0].shape[0]
            return self.dma(Q, outs[0][0:n], ins[0], reads=reads, writes=writes)
        self._deps(Q, reads, writes)
        s = self.nc.alloc_semaphore()
        self.sems.append(s)
        si = len(self.sems) - 1
        Q.eng.collective_compute("AllGather", ALU.bypass, replica_groups=groups,
                                 ins=ins, outs=outs).then_inc(s, 1)
        tok = (si, 1, None)
        self._commit(tok, reads, writes)
        return tok

    def barrier(self):
        toks = [e.cur() for e in self.engs if e.cur() is not None]
        toks += [(s[0], s[1], None) for s in self.dma_sems if s[1] > 0]
        toks += self.old_final
        self.old_final = []
        for e in self.engs:
            for t in toks:
                if t[2] is not e:
                    e.wait(t)


def build_program(nlayers=DEPTH, debug=False):
    kb = KB()
    nc = kb.nc
    PE, ACT, DVE, POOL, SP = kb.PE, kb.ACT, kb.DVE, kb.POOL, kb.SP

    _ctr = [0]

    def sbt(name, shape, dt):
        _ctr[0] += 1
        return nc.sbuf_tensor("%s_%d" % (name, _ctr[0]), list(shape), dt)

    def din(name, shape, dt=F32):
        return nc.dram_tensor(name, list(shape), dt, kind="ExternalInput").ap()

    def dscr(name, shape, dt=BF16):
        return nc.dram_tensor(name, list(shape), dt).ap()

    x_in = din("x_in", [2, S, D])
    p_in = din("p_in", [DEPTH, 2, S, PLE])
    y_out = nc.dram_tensor("y_out", [2, S, D], F32, kind="ExternalOutput").ap()
    g_mix_pre = din("norm_mix_pre", [DEPTH, D])
    g_mix_post = din("norm_mix_post", [DEPTH, D])
    g_ffn_pre = din("norm_ffn_pre", [DEPTH, D])
    g_ffn_post = din("norm_ffn_post", [DEPTH, D])
    g_ple = din("ple_norm", [DEPTH, D])
    mla_w_down = din("mla_w_down", [2, D, 704])
    mla_q_norm = din("mla_q_norm", [2, 384])
    mla_kv_norm = din("mla_kv_norm", [2, 256])
    mla_w_uq = din("mla_w_uq", [2, 384, 1536])
    mla_w_uq_rp = din("mla_w_uq_rp", [2, 384, 512])
    mla_w_ukv = din("mla_w_ukv", [2, 256, 2048])
    mla_w_o = din("mla_w_o", [2, D, D])
    gqa_w_qkv = din("gqa_w_qkv", [1, D, 1536])
    gqa_q_norm = din("gqa_q_norm", [1, 128])
    gqa_k_norm = din("gqa_k_norm", [1, 128])
    gqa_w_o = din("gqa_w_o", [1, D, D])
    na_w_qkv = din("na_w_qkv", [1, D, 3072])
    na_w_o = din("na_w_o", [1, D, D])
    na_tab = din("na_tab", [2, 4, 128, 8 * 4 * 64])
    na_rmask = din("na_rmask", [2, 128, 32, 6])
    na_hsel = din("na_hsel", [128, 8])
    ffn_w_in = din("ffn_w_in", [DEPTH, D, 2 * DFF])
    ffn_conv_w = din("ffn_conv_wT", [DEPTH, 128, 3, 32])
    ffn_conv_b = din("ffn_conv_bT", [DEPTH, 128, 32])
    ffn_w_out = din("ffn_w_out", [DEPTH, DFF, D])
    ple_w_proj = din("ple_w_proj", [DEPTH, PLE, D])
    ple_w_gate = din("ple_w_gate", [DEPTH, D, D])
    ident_in = din("ident", [128, 128])
    halo_sel = din("halo_sel", [8, 2])
    mla_cs_tok = din("mla_cs_tok", [2, S, 64])
    mla_cs_feat = din("mla_cs_feat", [2, 2, 64, S])
    gqa_cs_tok = din("gqa_cs_tok", [2, S, 128])

    xres = nc.dram_tensor("xres", [2, S, D], F32).ap()
    xnT_d = dscr("xnT_d", [2, 8, 128, S])
    xfT_d = dscr("xfT_d", [2, 8, 128, S])
    w_in_bf = dscr("w_in_bf", [D, 2 * DFF])
    w_out_bf = dscr("w_out_bf", [DFF, D])
    cc_in_mla = dscr("cc_in_mla", [256, S])
    cc_out_mla = dscr("cc_out_mla", [4 * 256, S])
    cc_in_mlb = dscr("cc_in_mlb", [64, S])
    cc_out_mlb = dscr("cc_out_mlb", [4 * 64, S])
    cc_in_gk = dscr("cc_in_gk", [256, S])
    cc_out_gk = dscr("cc_out_gk", [4 * 256, S])
    cc_in_gv = dscr("cc_in_gv", [S, 256])
    cc_out_gv = dscr("cc_out_gv", [4 * S, 256])
    cc_in_h = dscr("cc_in_h", [2, D])
    cc_out_h = dscr("cc_out_h", [8, D])
    na_kv_d = dscr("na_kv_d", [2, S, 2048])
    cc_in_na = dscr("cc_in_na", [512, 2048])
    cc_out_na = dscr("cc_out_na", [4 * 512, 2048])
    GROUPS = [[0, 1, 2, 3], [4, 5, 6, 7]]

    B_xres = [[Buf() for _ in range(NT)] for _ in range(2)]
    B_xnT = [[Buf() for _ in range(NT)] for _ in range(2)]
    B_xfT = [[Buf() for _ in range(NT)] for _ in range(2)]
    B_win = Buf()
    B_wout = Buf()
    B_cc = {k: Buf() for k in ["in_mla", "out_mla", "in_mlb", "out_mlb", "in_gk", "out_gk", "in_gv", "out_gv", "in_h", "out_h",
                               "in_na", "out_na", "nakv0", "nakv1"]}

    ident = nc.alloc_sbuf_tensor("identb", [128, 128], BF16).ap()
    ones = nc.alloc_sbuf_tensor("onesb", [128, 128], BF16).ap()
    B_const = Buf()
    PS = nc.alloc_psum_tensor("PS", [128, 8, 512], F32).ap()
    PB = [Buf() for _ in range(8)]
    kb.dma(POOL, ident, ident_in, writes=[B_const])
    kb.dve(lambda: nc.vector.memset(ones, 1.0), writes=[B_const])

    def ps_bf(b0, nb=1):
        return PS[:, b0:b0 + nb, :].rearrange("p b n -> p (b n)").bitcast(BF16)

    def ps_f(b0, nb=1):
        return PS[:, b0:b0 + nb, :].rearrange("p b n -> p (b n)")

    def load_gain(es, name, src_row, n=D):
        t = es.enter_context(sbt(name, [128, n], F32))
        b = Buf()
        kb.dma(SP, t[:], src_row.to_broadcast([128, n]), writes=[b])
        return t, b

    def rms_stats(src_ap, src_bufs, n, scr, stat, stat_b, col, scr_b=None):
        kb.act(lambda: nc.scalar.activation(out=scr[:, 0:n], in_=src_ap, func=AF.Square,
                                            accum_out=stat[:, col:col + 1]),
               reads=src_bufs, writes=[stat_b] + ([scr_b] if scr_b is not None else []))
        kb.act(lambda: nc.scalar.activation(out=stat[:, col + 1:col + 2], in_=stat[:, col:col + 1],
                                            func=AF.Sqrt, scale=1.0 / n, bias=EPS),
               reads=[stat_b], writes=[stat_b])
        kb.dve(lambda: nc.vector.reciprocal(out=stat[:, col + 2:col + 3], in_=stat[:, col + 1:col + 2]),
               reads=[stat_b], writes=[stat_b])
        return stat[:, col + 2:col + 3]

    def transpose_to(src_bf, src_b, nchunk, pbank, dst_fn, dst_bufs, rows=128):
        pv = ps_bf(pbank)
        def f():
            ins = None
            for c in range(nchunk):
                ins = nc.tensor.transpose(pv[:, c * 128:(c + 1) * 128], src_bf[:, c * 128:(c + 1) * 128], ident)
            return ins
        kb.pe(f, reads=[src_b, B_const], writes=[PB[pbank]])
        return pv

    def norm_T_store(es_tiles, x_sb, x_b, gain, gain_b, dst_d, dst_b, seg, t, pbank, extra_row=None):
        scr, scr_b, stat, stat_b, xn, xn_b, xT, xT_b = es_tiles
        rstd = rms_stats(x_sb, [x_b], D, scr, stat, stat_b, 0, scr_b)
        if gain is not None:
            kb.dve(lambda: nc.vector.scalar_tensor_tensor(out=xn, in0=x_sb, scalar=rstd, in1=gain,
                                                          op0=ALU.mult, op1=ALU.mult),
                   reads=[x_b, stat_b, gain_b], writes=[xn_b])
        else:
            kb.dve(lambda: nc.vector.tensor_scalar(out=xn, in0=x_sb, scalar1=rstd, scalar2=None, op0=ALU.mult),
                   reads=[x_b, stat_b], writes=[xn_b])
        if extra_row is not None:
            extra_row(xn, xn_b)
        pv = transpose_to(xn, xn_b, 8, pbank, None, None)
        kb.act(lambda: nc.scalar.activation(out=xT, in_=pv, func=AF.Copy), reads=[PB[pbank]], writes=[xT_b])
        if dst_d is not None:
            kb.dma(POOL, dst_d[seg, :, :, t * 128:(t + 1) * 128].rearrange("c p n -> p c n"),
                   xT.rearrange("p (c n) -> p c n", c=8), reads=[xT_b], writes=[dst_b[seg][t]])

    def alloc_norm_tiles(es, tag):
        scr = es.enter_context(sbt("scr" + tag, [128, D], F32))[:]
        stat = es.enter_context(sbt("stat" + tag, [128, 16], F32))[:]
        xn = es.enter_context(sbt("xn" + tag, [128, D], BF16))[:]
        xT = es.enter_context(sbt("xT" + tag, [128, D], BF16))[:]
        return (scr, Buf(), stat, Buf(), xn, Buf(), xT, Buf())

    def stage_initial():
        with ExitStack() as es:
            gain, gain_b = load_gain(es, "g0", g_mix_pre[0:1, :])
            tl = [alloc_norm_tiles(es, "i%d" % i) for i in range(2)]
            xs = [es.enter_context(sbt("xi%d" % i, [128, D], F32))[:] for i in range(2)]
            xb = [Buf(), Buf()]
            k = 0
            for seg in range(2):
                for t in range(NT):
                    j = k % 2
                    kb.dma(SP, xs[j], x_in[seg, t * 128:(t + 1) * 128, :], writes=[xb[j]])
                    kb.dma(POOL, xres[seg, t * 128:(t + 1) * 128, :], xs[j], reads=[xb[j]], writes=[B_xres[seg][t]])
                    norm_T_store(tl[j], xs[j], xb[j], gain[:], gain_b, xnT_d, B_xnT, seg, t, j)
                    k += 1
        kb.barrier()

    def attn_head(es_at, nk, qparts, kparts, vfn, OT_dst, OT_b, extra_reads):
        PT, PT_b, rec, rec_b = es_at
        nkt = nk // 128
        ngrp = nkt // 2
        for qb in range(S // 512):
            qs = slice(qb * 512, (qb + 1) * 512)

            def scores(g):
                b0 = 2 * (g % 3)
                def f():
                    ins = None
                    for j in range(2):
                        kt = 2 * g + j
                        for pi, ((q_ap, rows), (k_ap, _)) in enumerate(zip(qparts, kparts)):
                            ins = nc.tensor.matmul(PS[:, b0 + j, :], lhsT=k_ap[0:rows, kt * 128:(kt + 1) * 128],
                                                   rhs=q_ap[0:rows, qs], start=(pi == 0),
                                                   stop=(pi == len(qparts) - 1))
                    return ins
                kb.pe(f, reads=extra_reads, writes=[PB[b0], PB[b0 + 1]])

            def expo(g):
                b0 = 2 * (g % 3)
                kb.act(lambda: nc.scalar.activation(out=PT[g % 3], in_=ps_f(b0, 2), func=AF.Exp),
                       reads=[PB[b0], PB[b0 + 1]], writes=[PT_b[g % 3]])

            def pv(g):
                def f():
                    ins = None
                    for j in range(2):
                        kt = 2 * g + j
                        rhs = PT[g % 3][:, j * 512:(j + 1) * 512]
                        nc.tensor.matmul(PS[:, 6, :], lhsT=vfn(kt), rhs=rhs, start=(kt == 0), stop=(kt == nkt - 1))
                        ins = nc.tensor.matmul(PS[:, 7, :], lhsT=ones, rhs=rhs, start=(kt == 0), stop=(kt == nkt - 1))
                    return ins
                kb.pe(f, reads=[PT_b[g % 3], B_const] + extra_reads, writes=[PB[6], PB[7]])

            for g in range(ngrp):
                scores(g)
                expo(g)
                if g > 1:
                    pv(g - 2)
            if ngrp > 1:
                pv(ngrp - 2)
            pv(ngrp - 1)
            kb.dve(lambda: nc.vector.reciprocal(out=rec, in_=PS[:, 7, :]), reads=[PB[7]], writes=[rec_b])
            kb.dve(lambda: nc.vector.tensor_tensor(out=OT_dst[:, qs], in0=PS[:, 6, :], in1=rec, op=ALU.mult),
                   reads=[PB[6], rec_b], writes=[OT_b])

    def alloc_attn_tiles(es):
        PT = [es.enter_context(sbt("PT%d" % i, [128, 1024], BF16))[:] for i in range(3)]
        rec = es.enter_context(sbt("rec", [128, 512], F32))[:]
        return (PT, [Buf(), Buf(), Buf()], rec, Buf())

    def stage_out(es, layer, seg, OT, OT_b, wo_src, nchunk_rows, last_phase_cb=None):
        rows, nch = nchunk_rows
        wo = es.enter_context(sbt("wo", [rows, nch, D], BF16))
        wo_b = Buf()
        kb.dma(POOL, wo[:], wo_src.rearrange("(c p) n -> p c n", p=rows), writes=[wo_b])
        gpost, gpost_b = load_gain(es, "gpost", g_mix_post[layer:layer + 1, :])
        gfpre, gfpre_b = load_gain(es, "gfpre", g_ffn_pre[layer:layer + 1, :])
        tl = [alloc_norm_tiles(es, "c%d" % i) for i in range(2)]
        xs = [es.enter_context(sbt("xc%d" % i, [128, D], F32))[:] for i in range(2)]
        xb = [Buf(), Buf()]
        x1 = [es.enter_context(sbt("x1c%d" % i, [128, D], F32))[:] for i in range(2)]
        x1b = [Buf(), Buf()]
        st2 = es.enter_context(sbt("st2", [128, 16], F32))[:]
        st2_b = Buf()
        for t in range(NT):
            j = t % 2
            ts = slice(t * 128, (t + 1) * 128)
            kb.dma(SP, xs[j], xres[seg, ts, :], reads=[B_xres[seg][t]], writes=[xb[j]])
            b0 = 6 if False else (2 * j)
            def f():
                ins = None
                for half in range(2):
                    for c in range(nch):
                        ins = nc.tensor.matmul(PS[:, b0 + half, :], lhsT=OT[0:rows, c, ts],
                                               rhs=wo[0:rows, c, half * 512:(half + 1) * 512],
                                               start=(c == 0), stop=(c == nch - 1))
                return ins
            kb.pe(f, reads=[OT_b, wo_b], writes=[PB[b0], PB[b0 + 1]])
            scr, scr_b = tl[j][0], tl[j][1]
            rstd = rms_stats(ps_f(b0, 2), [PB[b0], PB[b0 + 1]], D, scr, st2, st2_b, 4 * j, scr_b)
            kb.dve(lambda: nc.vector.scalar_tensor_tensor(out=scr, in0=ps_f(b0, 2), scalar=rstd, in1=gpost[:],
                                                          op0=ALU.mult, op1=ALU.mult),
                   reads=[PB[b0], PB[b0 + 1], st2_b, gpost_b], writes=[scr_b])
            kb.dve(lambda: nc.vector.tensor_tensor(out=x1[j], in0=scr, in1=xs[j], op=ALU.add),
                   reads=[scr_b, xb[j]], writes=[x1b[j]])
            kb.dma(POOL, xres[seg, ts, :], x1[j], reads=[x1b[j]], writes=[B_xres[seg][t]])
            extra = None
            if seg == 1 and t in (0, NT - 1):
                def extra(xn, xn_b, t=t):
                    if t == 0:
                        kb.dma(POOL, cc_in_h[0:1, :], xn[0:1, :], reads=[xn_b], writes=[B_cc["in_h"]])
                    else:
                        kb.dma(POOL, cc_in_h[1:2, :], xn[127:128, :], reads=[xn_b], writes=[B_cc["in_h"]])
            norm_T_store(tl[j], x1[j], x1b[j], gfpre[:], gfpre_b, xfT_d, B_xfT, seg, t, 6 + j, extra_row=extra)
        if seg == 1:
            kb.collective([cc_in_h], [cc_out_h], reads=[B_cc["in_h"]], writes=[B_cc["out_h"]], groups=GROUPS)

    def mixer_mla(layer, seg, j):
        nk = S if seg == 0 else 4 * S
        with ExitStack() as es0, ExitStack() as es:
            OT = es0.enter_context(sbt("OT", [128, 8, S], BF16))[:]
            sb = lambda name, shape, dt=BF16: es.enter_context(sbt(name, shape, dt))[:]
            ckvT = sb("ckvT", [128, 2, nk])
            kropeT = sb("kropeT", [64, nk])
            cqT = sb("cqT", [128, 3, S])
            B_ckv, B_cq, B_OT = Buf(), Buf(), Buf()
            w_uq = sb("w_uq", [128, 3, 1536])
            w_uqr = sb("w_uqr", [128, 3, 512])
            w_ukv = sb("w_ukv", [128, 2, 2048])
            B_w = Buf()
            kb.dma(POOL, w_uq, mla_w_uq[j].rearrange("(c p) n -> p c n", p=128), writes=[B_w])
            kb.dma(POOL, w_uqr, mla_w_uq_rp[j].rearrange("(c p) n -> p c n", p=128), writes=[B_w])
            kb.dma(POOL, w_ukv, mla_w_ukv[j].rearrange("(c p) n -> p c n", p=128), writes=[B_w])
            csf = sb("csf", [64, 2, S], F32)
            B_csf = Buf()
            kb.dma(SP, csf, mla_cs_feat[seg].rearrange("a p n -> p a n"), writes=[B_csf])
            with ExitStack() as es2:
                sb2 = lambda name, shape, dt=BF16: es2.enter_context(sbt(name, shape, dt))[:]
                xnTb = [sb2("xnTb%d" % i, [128, 8, 512]) for i in range(2)]
                B_xnb = [Buf(), Buf()]
                wd = sb2("wd", [128, 8, 704])
                B_wd = Buf()
                kb.dma(POOL, wd, mla_w_down[j].rearrange("(c p) n -> p c n", p=128), writes=[B_wd])
                gq, gq_b = load_gain(es2, "gq", mla_q_norm[j:j + 1, :], 384)
                gkv, gkv_b = load_gain(es2, "gkv", mla_kv_norm[j:j + 1, :], 256)
                cst = sb2("cst", [128, NT, 64], F32)
                B_cst = Buf()
                kb.dma(SP, cst, mla_cs_tok[seg].rearrange("(t p) n -> p t n", p=128), writes=[B_cst])
                scr = sb2("scrA", [128, 512], F32)
                stat = sb2("statA", [128, 16], F32)
                stat_b = Buf()
                dnb = [sb2("dnb%d" % i, [128, 768]) for i in range(2)]
                dnb_b = [Buf(), Buf()]
                tmp = [sb2("tmpA%d" % i, [128, 128], F32) for i in range(2)]
                tmp_b = [Buf(), Buf()]
                for t in range(NT):
                    i = t % 2
                    ts = slice(t * 128, (t + 1) * 128)
                    b0 = 2 * i
                    xb_i = (t // 4) % 2
                    if t % 4 == 0:
                        kb.dma(SP, xnTb[xb_i], xnT_d[seg, :, :, t * 128:t * 128 + 512].rearrange("c p n -> p c n"),
                               reads=B_xnT[seg][t:t + 4], writes=[B_xnb[xb_i]])
                    xnT = xnTb[xb_i]
                    B_xn = B_xnb[xb_i]
                    tl_ = slice((t % 4) * 128, (t % 4 + 1) * 128)
                    def f():
                        ins = None
                        for c in range(8):
                            nc.tensor.matmul(PS[:, b0, 0:384], lhsT=xnT[:, c, tl_], rhs=wd[:, c, 0:384],
                                             start=(c == 0), stop=(c == 7))
                        for c in range(8):
                            ins = nc.tensor.matmul(PS[:, b0 + 1, 0:320], lhsT=xnT[:, c, tl_], rhs=wd[:, c, 384:704],
                                                   start=(c == 0), stop=(c == 7))
                        return ins
                    kb.pe(f, reads=[B_xn, B_wd], writes=[PB[b0], PB[b0 + 1]])
                    rq = rms_stats(PS[:, b0, 0:384], [PB[b0]], 384, scr, stat, stat_b, 8 * i)
                    rkv = rms_stats(PS[:, b0 + 1, 0:256], [PB[b0 + 1]], 256, scr, stat, stat_b, 8 * i + 4)
                    kb.dve(lambda: nc.vector.scalar_tensor_tensor(out=dnb[i][:, 0:384], in0=PS[:, b0, 0:384], scalar=rq,
                                                                  in1=gq[:], op0=ALU.mult, op1=ALU.mult),
                           reads=[PB[b0], stat_b, gq_b], writes=[dnb_b[i]])
                    kb.dve(lambda: nc.vector.scalar_tensor_tensor(out=dnb[i][:, 384:640], in0=PS[:, b0 + 1, 0:256], scalar=rkv,
                                                                  in1=gkv[:], op0=ALU.mult, op1=ALU.mult),
                           reads=[PB[b0 + 1], stat_b, gkv_b], writes=[dnb_b[i]])
                    x1 = PS[:, b0 + 1, 256:288]
                    x2 = PS[:, b0 + 1, 288:320]
                    cs_c = cst[:, t, 0:32]
                    cs_s = cst[:, t, 32:64]
                    tm = tmp[i]
                    kb.dve(lambda: nc.vector.tensor_tensor(out=tm[:, 0:32], in0=x1, in1=cs_c, op=ALU.mult),
                           reads=[PB[b0 + 1], B_cst], writes=[tmp_b[i]])
                    kb.dve(lambda: nc.vector.tensor_tensor(out=tm[:, 32:64], in0=x2, in1=cs_s, op=ALU.mult),
                           reads=[PB[b0 + 1], B_cst], writes=[tmp_b[i]])
                    kb.dve(lambda: nc.vector.tensor_tensor(out=tm[:, 64:96], in0=x2, in1=cs_c, op=ALU.mult),
                           reads=[PB[b0 + 1], B_cst], writes=[tmp_b[i]])
                    kb.dve(lambda: nc.vector.tensor_tensor(out=tm[:, 96:128], in0=x1, in1=cs_s, op=ALU.mult),
                           reads=[PB[b0 + 1], B_cst], writes=[tmp_b[i]])
                    kb.dve(lambda: nc.vector.tensor_tensor(out=dnb[i][:, 640:672], in0=tm[:, 0:32], in1=tm[:, 32:64],
                                                           op=ALU.subtract), reads=[tmp_b[i]], writes=[dnb_b[i]])
                    kb.dve(lambda: nc.vector.tensor_tensor(out=dnb[i][:, 672:704], in0=tm[:, 64:96], in1=tm[:, 96:128],
                                                           op=ALU.add), reads=[tmp_b[i]], writes=[dnb_b[i]])
                    pbank = 4 + i
                    pvw = ps_bf(pbank)
                    def ft():
                        ins = None
                        for c in range(5):
                            ins = nc.tensor.transpose(pvw[:, c * 128:(c + 1) * 128], dnb[i][:, c * 128:(c + 1) * 128], ident)
                        ins = nc.tensor.transpose(pvw[0:64, 640:768], dnb[i][:, 640:704], ident)
                        return ins
                    kb.pe(ft, reads=[dnb_b[i], B_const], writes=[PB[pbank]])
                    off = 0 if seg == 0 else 0
                    kb.act(lambda: nc.scalar.activation(out=cqT[:, :, ts], in_=pvw[:, 0:384].rearrange("p (c n) -> p c n", c=3),
                                                        func=AF.Copy), reads=[PB[pbank]], writes=[B_cq])
                    if seg == 0:
                        kb.act(lambda: nc.scalar.activation(out=ckvT[:, :, ts], in_=pvw[:, 384:640].rearrange("p (c n) -> p c n", c=2),
                                                            func=AF.Copy), reads=[PB[pbank]], writes=[B_ckv])
                        kb.act(lambda: nc.scalar.activation(out=kropeT[:, ts], in_=pvw[0:64, 640:768], func=AF.Copy),
                               reads=[PB[pbank]], writes=[B_ckv])
                    else:
                        kb.act(lambda: nc.scalar.activation(out=ckvT[:, :, ts], in_=pvw[:, 384:640].rearrange("p (c n) -> p c n", c=2),
                                                            func=AF.Copy), reads=[PB[pbank]], writes=[B_ckv])
                        kb.act(lambda: nc.scalar.activation(out=kropeT[:, ts], in_=pvw[0:64, 640:768], func=AF.Copy),
                               reads=[PB[pbank]], writes=[B_ckv])
                if seg == 1:
                    kb.dma(POOL, cc_in_mla.rearrange("(c p) n -> p c n", p=128), ckvT[:, :, 0:S],
                           reads=[B_ckv], writes=[B_cc["in_mla"]])
                    kb.dma(POOL, cc_in_mlb, kropeT[:, 0:S], reads=[B_ckv], writes=[B_cc["in_mlb"]])
                    kb.collective([cc_in_mla], [cc_out_mla], reads=[B_cc["in_mla"]], writes=[B_cc["out_mla"]], groups=GROUPS)
                    kb.collective([cc_in_mlb], [cc_out_mlb], reads=[B_cc["in_mlb"]], writes=[B_cc["out_mlb"]], groups=GROUPS)
                    for r in range(4):
                        kb.dma(SP, ckvT[:, :, r * S:(r + 1) * S],
                               cc_out_mla[r * 256:(r + 1) * 256, :].rearrange("(c p) n -> p c n", p=128),
                               reads=[B_cc["out_mla"]], writes=[B_ckv])
                        kb.dma(SP, kropeT[:, r * S:(r + 1) * S], cc_out_mlb[r * 64:(r + 1) * 64, :],
                               reads=[B_cc["out_mlb"]], writes=[B_ckv])
            kb.barrier()
            with ExitStack() as es3:
                sb3 = lambda name, shape, dt=BF16: es3.enter_context(sbt(name, shape, dt))[:]
                qnT = sb3("qnT", [128, S])
                qrT = sb3("qrT", [64, S])
                KhT = sb3("KhT", [128, nk])
                Vh = sb3("Vh", [128, nk // 128, 128])
                B_q, B_K, B_V = Buf(), Buf(), Buf()
                t1 = sb3("t1", [64, 512], F32)
                t2 = sb3("t2", [64, 512], F32)
                B_t = Buf()
                at = alloc_attn_tiles(es3)
                scale = 192.0 ** -0.5
                for h in range(8):
                    for qb in range(S // 512):
                        qs = slice(qb * 512, (qb + 1) * 512)
                        bq = 4 + (qb % 2)
                        def f():
                            ins = None
                            for c in range(3):
                                ins = nc.tensor.matmul(PS[:, bq, :], lhsT=w_uq[:, c, h * 192:h * 192 + 128], rhs=cqT[:, c, qs],
                                                       start=(c == 0), stop=(c == 2))
                            return ins
                        kb.pe(f, reads=[B_w, B_cq], writes=[PB[bq]])
                        kb.act(lambda: nc.scalar.activation(out=qnT[:, qs], in_=PS[:, bq, :], func=AF.Copy, scale=scale),
                               reads=[PB[bq]], writes=[B_q])
                        def f3():
                            ins = None
                            for c in range(3):
                                ins = nc.tensor.matmul(PS[0:64, bq, :], lhsT=w_uq[:, c, h * 192 + 128:h * 192 + 192], rhs=cqT[:, c, qs],
                                                       start=(c == 0), stop=(c == 2))
                            return ins
                        kb.pe(f3, reads=[B_w, B_cq], writes=[PB[bq]])
                        kb.dve(lambda: nc.vector.tensor_tensor(out=t1, in0=PS[0:64, bq, :], in1=csf[:, 0, qs], op=ALU.mult),
                               reads=[PB[bq], B_csf], writes=[B_t])
                        def f4():
                            ins = None
                            for c in range(3):
                                ins = nc.tensor.matmul(PS[0:64, bq, :], lhsT=w_uqr[:, c, h * 64:(h + 1) * 64], rhs=cqT[:, c, qs],
                                                       start=(c == 0), stop=(c == 2))
                            return ins
                        kb.pe(f4, reads=[B_w, B_cq], writes=[PB[bq]])
                        kb.dve(lambda: nc.vector.tensor_tensor(out=t2, in0=PS[0:64, bq, :], in1=csf[:, 1, qs], op=ALU.mult),
                               reads=[PB[bq], B_csf], writes=[B_t])
                        kb.dve(lambda: nc.vector.tensor_tensor(out=qrT[:, qs], in0=t1, in1=t2, op=ALU.add),
                               reads=[B_t], writes=[B_q])
                    for kbk in range(nk // 512):
                        ks = slice(kbk * 512, (kbk + 1) * 512)
                        bq = 4 + (kbk % 2)
                        def f():
                            ins = None
                            for c in range(2):
                                ins = nc.tensor.matmul(PS[:, bq, :], lhsT=w_ukv[:, c, h * 256:h * 256 + 128], rhs=ckvT[:, c, ks],
                                                       start=(c == 0), stop=(c == 1))
                            return ins
                        kb.pe(f, reads=[B_w, B_ckv], writes=[PB[bq]])
                        kb.act(lambda: nc.scalar.activation(out=KhT[:, ks], in_=PS[:, bq, :], func=AF.Copy),
                               reads=[PB[bq]], writes=[B_K])
                    for kg in range(nk // 512):
                        bq = 4 + (kg % 2)
                        def f():
                            ins = None
                            for jj in range(4):
                                kt = kg * 4 + jj
                                for c in range(2):
                                    ins = nc.tensor.matmul(PS[:, bq, jj * 128:(jj + 1) * 128], lhsT=ckvT[:, c, kt * 128:(kt + 1) * 128],
                                                           rhs=w_ukv[:, c, h * 256 + 128:h * 256 + 256], start=(c == 0), stop=(c == 1))
                            return ins
                        kb.pe(f, reads=[B_w, B_ckv], writes=[PB[bq]])
                        kb.dve(lambda: nc.vector.tensor_copy(out=Vh[:, kg * 4:(kg + 1) * 4, :],
                                                             in_=PS[:, bq, :].rearrange("p (a n) -> p a n", a=4)),
                               reads=[PB[bq]], writes=[B_V])
                    attn_head(at, nk, [(qnT, 128), (qrT, 64)], [(KhT, 128), (kropeT, 64)],
                              lambda kt: Vh[:, kt, :], OT[:, h, :], B_OT, [B_q, B_K, B_V, B_ckv])
            kb.barrier()
            es.close()
            with ExitStack() as es4:
                stage_out(es4, layer, seg, OT, B_OT, mla_w_o[j], (128, 8))
        kb.barrier()

    def mixer_gqa(layer, seg, j):
        nk = S if seg == 0 else 4 * S
        with ExitStack() as es0, ExitStack() as es:
            OT = es0.enter_context(sbt("OTg", [128, 8, S], BF16))[:]
            B_OT = Buf()
            sb = lambda name, shape, dt=BF16: es.enter_context(sbt(name, shape, dt))[:]
            qT = sb("qT", [128, 8, S])
            kT = sb("kT", [128, 2, nk])
            Vall = sb("Vall", [128, nk // 128, 256])
            B_q, B_k, B_v = Buf(), Buf(), Buf()
            with ExitStack() as es2:
                sb2 = lambda name, shape, dt=BF16: es2.enter_context(sbt(name, shape, dt))[:]
                xnTb = [sb2("xnTg%d" % i, [128, 8, 512]) for i in range(2)]
                B_xnb = [Buf(), Buf()]
                wqkv = sb2("wqkv", [128, 8, 1536])
                B_w = Buf()
                kb.dma(POOL, wqkv, gqa_w_qkv[j].rearrange("(c p) n -> p c n", p=128), writes=[B_w])
                gq, gq_b = load_gain(es2, "ggq", gqa_q_norm[j:j + 1, :], 128)
                gk, gk_b = load_gain(es2, "ggk", gqa_k_norm[j:j + 1, :], 128)
                kb.act(lambda: nc.scalar.mul(out=gq[:], in_=gq[:], mul=128.0 ** -0.5), reads=[gq_b], writes=[gq_b])
                cst = sb2("cstg", [128, NT, 128], F32)
                B_cst = Buf()
                kb.dma(SP, cst, gqa_cs_tok[seg].rearrange("(t p) n -> p t n", p=128), writes=[B_cst])
                sq = sb2("sqg", [128, 1280], F32)
                B_sq = Buf()
                stat = sb2("statg", [128, 32], F32)
                stat_b = Buf()
                qn = sb2("qng", [128, 1280], F32)
                B_qn = Buf()
                tm = sb2("tmg", [128, 4, 320], F32)
                B_tm = Buf()
                qkb = [sb2("qkb%d" % i, [128, 1280]) for i in range(2)]
                qkb_b = [Buf(), Buf()]
                for t in range(NT):
                    i = t % 2
                    ts = slice(t * 128, (t + 1) * 128)
                    b0 = 3 * i
                    xb_i = (t // 4) % 2
                    if t % 4 == 0:
                        kb.dma(SP, xnTb[xb_i], xnT_d[seg, :, :, t * 128:t * 128 + 512].rearrange("c p n -> p c n"),
                               reads=B_xnT[seg][t:t + 4], writes=[B_xnb[xb_i]])
                    xnT = xnTb[xb_i]
                    tl_ = slice((t % 4) * 128, (t % 4 + 1) * 128)
                    def f():
                        ins = None
                        for pc in range(3):
                            for c in range(8):
                                ins = nc.tensor.matmul(PS[:, b0 + pc, :], lhsT=xnT[:, c, tl_], rhs=wqkv[:, c, pc * 512:(pc + 1) * 512],
                                                       start=(c == 0), stop=(c == 7))
                        return ins
                    kb.pe(f, reads=[B_xnb[xb_i], B_w], writes=[PB[b0], PB[b0 + 1], PB[b0 + 2]])
                    kb.act(lambda: nc.scalar.activation(out=Vall[:, t, :], in_=PS[:, b0 + 2, 256:512], func=AF.Copy),
                           reads=[PB[b0 + 2]], writes=[B_v])
                    kb.act(lambda: nc.scalar.activation(out=sq[:, 0:1024], in_=ps_f(b0, 2), func=AF.Square),
                           reads=[PB[b0], PB[b0 + 1]], writes=[B_sq])
                    kb.act(lambda: nc.scalar.activation(out=sq[:, 1024:1280], in_=PS[:, b0 + 2, 0:256], func=AF.Square),
                           reads=[PB[b0 + 2]], writes=[B_sq])
                    kb.dve(lambda: nc.vector.tensor_reduce(out=stat[:, 0:10], in_=sq.rearrange("p (h d) -> p h d", d=128),
                                                           axis=mybir.AxisListType.X, op=ALU.add),
                           reads=[B_sq], writes=[stat_b])
                    kb.act(lambda: nc.scalar.activation(out=stat[:, 10:20], in_=stat[:, 0:10], func=AF.Sqrt, scale=1.0 / 128, bias=EPS),
                           reads=[stat_b], writes=[stat_b])
                    kb.dve(lambda: nc.vector.reciprocal(out=stat[:, 20:30], in_=stat[:, 10:20]), reads=[stat_b], writes=[stat_b])
                    for h in range(10):
                        src = PS[:, b0 + h // 4, (h % 4) * 128:(h % 4 + 1) * 128]
                        g_ = gq if h < 8 else gk
                        g_b = gq_b if h < 8 else gk_b
                        kb.dve(lambda: nc.vector.scalar_tensor_tensor(out=qn[:, h * 128:(h + 1) * 128], in0=src, scalar=stat[:, 20 + h:21 + h],
                                                                      in1=g_[:], op0=ALU.mult, op1=ALU.mult),
                               reads=[PB[b0 + h // 4], stat_b, g_b], writes=[B_qn])
                    q3 = qn.rearrange("p (h d) -> p h d", d=128)
                    o3 = qkb[i].rearrange("p (h d) -> p h d", d=128)
                    for part in range(2):
                        o = part * 64
                        x1 = q3[:, :, o:o + 32]
                        x2 = q3[:, :, o + 32:o + 64]
                        cc_ = cst[:, t, o:o + 32].rearrange("p (o n) -> p o n", o=1).to_broadcast([128, 10, 32])
                        ss_ = cst[:, t, o + 32:o + 64].rearrange("p (o n) -> p o n", o=1).to_broadcast([128, 10, 32])
                        tv = [tm[:, k, :].rearrange("p (h n) -> p h n", n=32) for k in range(4)]
                        kb.dve(lambda: nc.vector.tensor_tensor(out=tv[0], in0=x1, in1=cc_, op=ALU.mult), reads=[B_qn, B_cst], writes=[B_tm])
                        kb.dve(lambda: nc.vector.tensor_tensor(out=tv[1], in0=x2, in1=ss_, op=ALU.mult), reads=[B_qn, B_cst], writes=[B_tm])
                        kb.dve(lambda: nc.vector.tensor_tensor(out=tv[2], in0=x2, in1=cc_, op=ALU.mult), reads=[B_qn, B_cst], writes=[B_tm])
                        kb.dve(lambda: nc.vector.tensor_tensor(out=tv[3], in0=x1, in1=ss_, op=ALU.mult), reads=[B_qn, B_cst], writes=[B_tm])
                        kb.dve(lambda: nc.vector.tensor_tensor(out=o3[:, :, o:o + 32], in0=tv[0], in1=tv[1], op=ALU.subtract),
                               reads=[B_tm], writes=[qkb_b[i]])
                        kb.dve(lambda: nc.vector.tensor_tensor(out=o3[:, :, o + 32:o + 64], in0=tv[2], in1=tv[3], op=ALU.add),
                               reads=[B_tm], writes=[qkb_b[i]])
                    def ft():
                        ins = None
                        for c in range(8):
                            ins = nc.tensor.transpose(ps_bf(6)[:, c * 128:(c + 1) * 128], qkb[i][:, c * 128:(c + 1) * 128], ident)
                        for c in range(2):
                            ins = nc.tensor.transpose(ps_bf(7)[:, c * 128:(c + 1) * 128], qkb[i][:, (8 + c) * 128:(9 + c) * 128], ident)
                        return ins
                    kb.pe(ft, reads=[qkb_b[i], B_const], writes=[PB[6], PB[7]])
                    kb.act(lambda: nc.scalar.activation(out=qT[:, :, ts], in_=ps_bf(6).rearrange("p (c n) -> p c n", c=8), func=AF.Copy),
                           reads=[PB[6]], writes=[B_q])
                    kb.act(lambda: nc.scalar.activation(out=kT[:, :, ts], in_=ps_bf(7)[:, 0:256].rearrange("p (c n) -> p c n", c=2), func=AF.Copy),
                           reads=[PB[7]], writes=[B_k])
                if seg == 1:
                    kb.dma(POOL, cc_in_gk.rearrange("(c p) n -> p c n", p=128), kT[:, :, 0:S], reads=[B_k], writes=[B_cc["in_gk"]])
                    kb.dma(POOL, cc_in_gv.rearrange("(t p) n -> p t n", p=128), Vall[:, 0:NT, :], reads=[B_v], writes=[B_cc["in_gv"]])
                    kb.collective([cc_in_gk], [cc_out_gk], reads=[B_cc["in_gk"]], writes=[B_cc["out_gk"]], groups=GROUPS)
                    kb.collective([cc_in_gv], [cc_out_gv], reads=[B_cc["in_gv"]], writes=[B_cc["out_gv"]], groups=GROUPS)
                    for r in range(4):
                        kb.dma(SP, kT[:, :, r * S:(r + 1) * S], cc_out_gk[r * 256:(r + 1) * 256, :].rearrange("(c p) n -> p c n", p=128),
                               reads=[B_cc["out_gk"]], writes=[B_k])
                        kb.dma(SP, Vall[:, r * NT:(r + 1) * NT, :], cc_out_gv[r * S:(r + 1) * S, :].rearrange("(t p) n -> p t n", p=128),
                               reads=[B_cc["out_gv"]], writes=[B_v])
            kb.barrier()
            with ExitStack() as es3:
                at = alloc_attn_tiles(es3)
                for h in range(8):
                    kvh = h // 4
                    attn_head(at, nk, [(qT[:, h, :], 128)], [(kT[:, kvh, :], 128)],
                              lambda kt, kvh=kvh: Vall[:, kt, kvh * 128:(kvh + 1) * 128], OT[:, h, :], B_OT, [B_q, B_k, B_v])
            kb.barrier()
            es.close()
            with ExitStack() as es4:
                stage_out(es4, layer, seg, OT, B_OT, gqa_w_o[j], (128, 8))
        kb.barrier()

    NBT = 2560
    qTn_d = dscr("qTn_d", [2, 8, 128, S])
    kTn_d = dscr("kTn_d", [2, 8, 128, NBT])
    vn_d = dscr("vn_d", [2, NBT, D])
    cc_in_nak = dscr("cc_in_nak", [8 * 128, 512])
    cc_out_nak = dscr("cc_out_nak", [4 * 8 * 128, 512])
    cc_in_nav = dscr("cc_in_nav", [512, D])
    cc_out_nav = dscr("cc_out_nav", [4 * 512, D])
    B_na = {k: Buf() for k in ["q0", "q1", "k0", "k1", "v0", "v1", "ink", "outk", "inv", "outv"]}

    def na_window(lr):
        if lr < 4:
            lo, hi = lr - 4, 7
        elif lr >= 28:
            lo, hi = 24, lr + 3
        else:
            lo, hi = lr - 4, lr + 3
        blo, bhi = lo + 4, hi + 4
        ws = blo - (blo % 2)
        nrows = bhi - ws + 1
        nch = (nrows + 1) // 2
        rho0 = ws - lr + 3
        layout = 0 if rho0 % 2 == 0 else 1
        pi0 = rho0 // 2
        return ws, nch, layout, pi0

    def mixer_na(layer, seg, j):
        Bq, Bk, Bv = B_na["q%d" % seg], B_na["k%d" % seg], B_na["v%d" % seg]
        with ExitStack() as es2:
            sb2 = lambda name, shape, dt=BF16: es2.enter_context(sbt(name, shape, dt))[:]
            wq = sb2("wna", [128, 8, 3072])
            B_w = Buf()
            for k3 in range(3):
                kb.dma(POOL, wq[:, :, k3 * 1024:(k3 + 1) * 1024],
                       na_w_qkv[j, :, k3 * 1024:(k3 + 1) * 1024].rearrange("(c p) n -> p c n", p=128), writes=[B_w])
            xnTb = [sb2("xnTn%d" % i, [128, 8, 512]) for i in range(2)]
            B_xnb = [Buf(), Buf()]
            stg = [sb2("stgn%d" % i, [128, 512]) for i in range(2)]
            stg_b = [Buf(), Buf()]
            vst = [sb2("vstn%d" % i, [128, D]) for i in range(2)]
            vst_b = [Buf(), Buf()]
            zt = sb2("zt", [128, 8, 256])
            B_z = Buf()
            kb.dve(lambda: nc.vector.memset(zt, 0.0), writes=[B_z])
            if seg == 0:
                kb.dma(POOL, kTn_d[seg, :, :, 0:256].rearrange("c p n -> p c n"), zt, reads=[B_z], writes=[Bk])
                kb.dma(POOL, kTn_d[seg, :, :, NBT - 256:NBT].rearrange("c p n -> p c n"), zt, reads=[B_z], writes=[Bk])
                kb.dma(POOL, vn_d[seg, 0:256, :].rearrange("(t p) n -> p t n", p=128), zt.rearrange("p a (b n) -> p (a b) n", b=2)[:, 0:2, :].rearrange("p t n -> p t n") if False else zt[:, 0:8, :].rearrange("p c n -> p (c n)")[:, 0:2048].rearrange("p (t n) -> p t n", t=2),
                       reads=[B_z], writes=[Bv])
                kb.dma(POOL, vn_d[seg, NBT - 256:NBT, :].rearrange("(t p) n -> p t n", p=128),
                       zt[:, 0:8, :].rearrange("p c n -> p (c n)")[:, 0:2048].rearrange("p (t n) -> p t n", t=2),
                       reads=[B_z], writes=[Bv])
            k_ = 0
            for blk in range(4):
                xi = blk % 2
                kb.dma(SP, xnTb[xi], xnT_d[seg, :, :, blk * 512:(blk + 1) * 512].rearrange("c p n -> p c n"),
                       reads=B_xnT[seg][blk * 4:blk * 4 + 4], writes=[B_xnb[xi]])
                xnT = xnTb[xi]
                for m in range(16):
                    i = k_ % 2
                    k_ += 1
                    bq = 4 + i
                    def f():
                        ins = None
                        for c in range(8):
                            ins = nc.tensor.matmul(PS[:, bq, :], lhsT=wq[:, c, m * 128:(m + 1) * 128], rhs=xnT[:, c, :],
                                                   start=(c == 0), stop=(c == 7))
                        return ins
                    kb.pe(f, reads=[B_w, B_xnb[xi]], writes=[PB[bq]])
                    sc_ = 0.125 if m < 8 else 1.0
                    kb.act(lambda: nc.scalar.activation(out=stg[i], in_=PS[:, bq, :], func=AF.Copy, scale=sc_),
                           reads=[PB[bq]], writes=[stg_b[i]])
                    if m < 8:
                        kb.dma(POOL, qTn_d[seg, m, :, blk * 512:(blk + 1) * 512], stg[i], reads=[stg_b[i]], writes=[Bq])
                    else:
                        kb.dma(POOL, kTn_d[seg, m - 8, :, 256 + blk * 512:256 + (blk + 1) * 512], stg[i], reads=[stg_b[i]], writes=[Bk])
                for tt in range(4):
                    t = blk * 4 + tt
                    i = t % 2
                    b0 = 6 if False else (0 + 2 * i)
                    def f():
                        ins = None
                        for half in range(2):
                            for c in range(8):
                                ins = nc.tensor.matmul(PS[:, b0 + half, :], lhsT=xnT[:, c, tt * 128:(tt + 1) * 128],
                                                       rhs=wq[:, c, 2048 + half * 512:2048 + (half + 1) * 512], start=(c == 0), stop=(c == 7))
                        return ins
                    kb.pe(f, reads=[B_w, B_xnb[xi]], writes=[PB[b0], PB[b0 + 1]])
                    kb.act(lambda: nc.scalar.activation(out=vst[i], in_=ps_f(b0, 2), func=AF.Copy), reads=[PB[b0], PB[b0 + 1]], writes=[vst_b[i]])
                    kb.dma(POOL, vn_d[seg, 256 + t * 128:256 + (t + 1) * 128, :], vst[i], reads=[vst_b[i]], writes=[Bv])
            if seg == 1:
                kb.dma(POOL, cc_in_nak[:, 0:256].rearrange("(c p) n -> c p n", p=128), kTn_d[seg, :, :, 256:512], reads=[Bk], writes=[B_na["ink"]])
                kb.dma(POOL, cc_in_nak[:, 256:512].rearrange("(c p) n -> c p n", p=128), kTn_d[seg, :, :, NBT - 512:NBT - 256], reads=[Bk], writes=[B_na["ink"]])
                kb.dma(POOL, cc_in_nav[0:256, :], vn_d[seg, 256:512, :], reads=[Bv], writes=[B_na["inv"]])
                kb.dma(POOL, cc_in_nav[256:512, :], vn_d[seg, NBT - 512:NBT - 256, :], reads=[Bv], writes=[B_na["inv"]])
                kb.collective([cc_in_nak], [cc_out_nak], reads=[B_na["ink"]], writes=[B_na["outk"]], groups=GROUPS)
                kb.collective([cc_in_nav], [cc_out_nav], reads=[B_na["inv"]], writes=[B_na["outv"]], groups=GROUPS)
                hsel = sb2("hsel", [128, 8], F32)
                B_hs = Buf()
                kb.dma(SP, hsel, na_hsel, writes=[B_hs])
                candk = sb2("candk", [128, 4, 8, 512])
                candv = sb2("candv", [128, 4, 4, D])
                B_ck, B_cv = Buf(), Buf()
                for r in range(4):
                    kb.dma(SP, candk[:, r], cc_out_nak[r * 1024:(r + 1) * 1024, :].rearrange("(c p) n -> p c n", p=128),
                           reads=[B_na["outk"]], writes=[B_ck])
                    kb.dma(SP, candv[:, r], cc_out_nav[r * 512:(r + 1) * 512, :].rearrange("(t p) n -> p t n", p=128),
                           reads=[B_na["outv"]], writes=[B_cv])
                hk = sb2("hk", [128, 2, 8, 256])
                hv = sb2("hv", [128, 2, 2, D])
                B_hk, B_hv = Buf(), Buf()
                for side in range(2):
                    ksl = slice(256, 512) if side == 0 else slice(0, 256)
                    vsl = slice(2, 4) if side == 0 else slice(0, 2)
                    for r in range(4):
                        msk = hsel[:, side * 4 + r:side * 4 + r + 1]
                        if r == 0:
                            kb.dve(lambda: nc.vector.tensor_scalar(out=hk[:, side], in0=candk[:, r, :, ksl], scalar1=msk, scalar2=None, op0=ALU.mult),
                                   reads=[B_ck, B_hs], writes=[B_hk])
                            kb.dve(lambda: nc.vector.tensor_scalar(out=hv[:, side], in0=candv[:, r, vsl, :], scalar1=msk, scalar2=None, op0=ALU.mult),
                                   reads=[B_cv, B_hs], writes=[B_hv])
                        else:
                            kb.dve(lambda: nc.vector.scalar_tensor_tensor(out=hk[:, side], in0=candk[:, r, :, ksl], scalar=msk, in1=hk[:, side],
                                                                          op0=ALU.mult, op1=ALU.add), reads=[B_ck, B_hs, B_hk], writes=[B_hk])
                            kb.dve(lambda: nc.vector.scalar_tensor_tensor(out=hv[:, side], in0=candv[:, r, vsl, :], scalar=msk, in1=hv[:, side],
                                                                          op0=ALU.mult, op1=ALU.add), reads=[B_cv, B_hs, B_hv], writes=[B_hv])
                kb.dma(POOL, kTn_d[seg, :, :, 0:256].rearrange("c p n -> p c n"), hk[:, 0], reads=[B_hk], writes=[Bk])
                kb.dma(POOL, kTn_d[seg, :, :, NBT - 256:NBT].rearrange("c p n -> p c n"), hk[:, 1], reads=[B_hk], writes=[Bk])
                kb.dma(POOL, vn_d[seg, 0:256, :].rearrange("(t p) n -> p t n", p=128), hv[:, 0], reads=[B_hv], writes=[Bv])
                kb.dma(POOL, vn_d[seg, NBT - 256:NBT, :].rearrange("(t p) n -> p t n", p=128), hv[:, 1], reads=[B_hv], writes=[Bv])
        kb.barrier()
        with ExitStack() as es0:
            Otok = es0.enter_context(sbt("Otok", [128, NT, D], BF16))[:]
            B_Ot = Buf()
            with ExitStack() as es:
                sb = lambda name, shape, dt=BF16: es.enter_context(sbt(name, shape, dt))[:]
                rmask = sb("rmask", [128, 32, 6], F32)
                B_rm = Buf()
                kb.dma(SP, rmask, na_rmask[seg], writes=[B_rm])
                qg = [sb("qg%d" % i, [64, 4, S]) for i in range(2)]
                kg = [sb("kg%d" % i, [64, 4, NBT]) for i in range(2)]
                vg = [sb("vg%d" % i, [128, 20, 4, 72]) for i in range(2)]
                tabs = [[sb("tab%d_%d" % (i, l), [128, 8, 4, 64], F32) for l in range(2)] for i in range(2)]
                B_g = [Buf(), Buf()]
                for i in range(2):
                    kb.dve(lambda: nc.vector.memset(vg[i][:, :, :, 64:65], 1.0), writes=[B_g[i]])
                scb = [sb("scb%d" % i, [128, 6, 256], F32) for i in range(2)]
                scb_b = [Buf(), Buf()]
                PTn = [sb("PTn%d" % i, [128, 6, 4, 64]) for i in range(2)]
                PTn_b = [Buf(), Buf()]
                rec = sb("recn", [64, 8], F32)
                rec_b = Buf()
                vstg = sb("vstg", [128, 20, 256])
                B_vs = Buf()
                if NA_DBG == 'A':
                    kb.dve(lambda: nc.vector.memset(Otok, 0.0), writes=[B_Ot])
                for G in range(4 if NA_DBG != 'A' else 0):
                    gi = G % 2
                    kb.dma(SP, qg[gi], qTn_d[seg].rearrange("c p n -> (c p) n")[4 * G * 64:(4 * G + 4) * 64, :].rearrange("(h p) n -> p h n", p=64),
                           reads=[Bq], writes=[B_g[gi]])
                    kb.dma(SP, kg[gi], kTn_d[seg].rearrange("c p n -> (c p) n")[4 * G * 64:(4 * G + 4) * 64, :].rearrange("(h p) n -> p h n", p=64),
                           reads=[Bk], writes=[B_g[gi]])
                    for q5 in range(4):
                        kb.dma(SP, vstg[:, q5 * 5:(q5 + 1) * 5, :],
                               vn_d[seg, q5 * 640:(q5 + 1) * 640, 4 * G * 64:(4 * G + 4) * 64].rearrange("(c p) n -> p c n", p=128),
                               reads=[Bv], writes=[B_vs])
                    kb.dve(lambda: nc.vector.tensor_copy(out=vg[gi][:, :, :, 0:64], in_=vstg.rearrange("p c (h d) -> p c h d", d=64)),
                           reads=[B_vs], writes=[B_g[gi]])
                    for l in range(2 if NA_DBG != 'L2' else 0):
                        kb.dma(SP, tabs[gi][l].rearrange("p a h c -> p (a h c)"), na_tab[l, G], writes=[B_g[gi]])
                    if NA_DBG in ('L', 'L2', 'S', 'S0', 'S1'):
                        kb.dve(lambda: nc.vector.memset(Otok, 0.0), writes=[B_Ot])
                    for lr in range(32 if NA_DBG not in ('L', 'L2') else 0):
                        ws, nch, layout, pi0 = na_window(lr)
                        i = lr % 2
                        sb0 = 3 * i
                        ob = 6 + i
                        scv = PS[:, sb0:sb0 + 3, :].rearrange("p b n -> p (b n)")
                        def f():
                            ins = None
                            for ci in range(nch):
                                for hh in range(4):
                                    col = (ci * 4 + hh) * 64
                                    ins = nc.tensor.matmul(scv[:, col:col + 64],
                                                           lhsT=kg[gi][0:64, hh, (ws + 2 * ci) * 64:(ws + 2 * ci) * 64 + 128],
                                                           rhs=qg[gi][0:64, hh, lr * 64:(lr + 1) * 64], start=True, stop=True)
                            return ins
                        kb.pe(f, reads=[B_g[gi]], writes=[PB[sb0], PB[sb0 + 1], PB[sb0 + 2]])
                        n_el = nch * 256
                        if NA_DBG == 'S0':
                            continue
                        kb.dve(lambda: nc.vector.tensor_tensor(out=scb[i].rearrange("p a n -> p (a n)")[:, 0:n_el], in0=scv[:, 0:n_el],
                                                               in1=tabs[gi][layout][:, pi0:pi0 + nch].rearrange("p a h c -> p (a h c)"),
                                                               op=ALU.add),
                               reads=[PB[sb0], PB[sb0 + 1], PB[sb0 + 2], B_g[gi]], writes=[scb_b[i]])
                        if NA_DBG == 'S1':
                            continue
                        for ci in range(nch):
                            kb.act(lambda: nc.scalar.activation(out=PTn[i][:, ci].rearrange("p h c -> p (h c)"), in_=scb[i][:, ci, :],
                                                                func=AF.Exp, bias=rmask[:, lr, ci:ci + 1]),
                                   reads=[scb_b[i], B_rm], writes=[PTn_b[i]])
                        if NA_DBG == 'S':
                            continue
                        ov = PS[0:64, ob, 0:260].rearrange("p (h n) -> p h n", n=65)
                        def f2():
                            ins = None
                            for hh in range(4):
                                for ci in range(nch):
                                    ins = nc.tensor.matmul(ov[:, hh, :], lhsT=PTn[i][:, ci, hh, :], rhs=vg[gi][:, ws // 2 + ci, hh, 0:65],
                                                           start=(ci == 0), stop=(ci == nch - 1))
                            return ins
                        kb.pe(f2, reads=[PTn_b[i], B_g[gi]], writes=[PB[ob]])
                        kb.dve(lambda: nc.vector.reciprocal(out=rec[:, 4 * i:4 * i + 4], in_=ov[:, :, 64]), reads=[PB[ob]], writes=[rec_b])
                        pr = (lr % 2) * 64
                        for hh in range(4):
                            kb.dve(lambda: nc.vector.tensor_scalar(out=Otok[pr:pr + 64, lr // 2, (4 * G + hh) * 64:(4 * G + hh + 1) * 64],
                                                                   in0=ov[:, hh, 0:64], scalar1=rec[:, 4 * i + hh:4 * i + hh + 1],
                                                                   scalar2=None, op0=ALU.mult),
                                   reads=[PB[ob], rec_b], writes=[B_Ot])
            kb.barrier()
            if NA_DBG == 'O':
                with ExitStack() as esd:
                    tcp = esd.enter_context(sbt("tcpo", [128, D], F32))[:]
                    bcp = Buf()
                    for t in range(NT):
                        kb.act(lambda: nc.scalar.activation(out=tcp, in_=Otok[:, t, :], func=AF.Copy), reads=[B_Ot], writes=[bcp])
                        kb.dma(SP, y_out[seg, t * 128:(t + 1) * 128, :], tcp, reads=[bcp], writes=[Buf()])
                kb.barrier()
            with ExitStack() as es4:
                OT = es4.enter_context(sbt("OTn", [128, 8, S], BF16))[:]
                B_OT = Buf()
                for t in range(NT):
                    pbank = t % 2
                    pv = transpose_to(Otok[:, t, :], B_Ot, 8, pbank, None, None)
                    kb.act(lambda: nc.scalar.activation(out=OT[:, :, t * 128:(t + 1) * 128], in_=pv.rearrange("p (c n) -> p c n", c=8),
                                                        func=AF.Copy), reads=[PB[pbank]], writes=[B_OT])
                kb.barrier()
                stage_out(es4, layer, seg, OT, B_OT, na_w_o[j], (128, 8))
        kb.barrier()

    def weights_to_bf16(layer):
        for r in range(8):
            kb.dma(POOL, w_in_bf[r * 128:(r + 1) * 128, :], ffn_w_in[layer, r * 128:(r + 1) * 128, :],
                   reads=[], writes=[B_win])
        for r in range(8):
            kb.dma(POOL, w_out_bf[r * 512:(r + 1) * 512, :], ffn_w_out[layer, r * 512:(r + 1) * 512, :],
                   reads=[], writes=[B_wout])

    def stage_ffn(layer, seg):
        TB = 512
        NB = S // TB
        with ExitStack() as es:
            sb = lambda name, shape, dt=BF16: es.enter_context(sbt(name, shape, dt))[:]
            halo = sb("halo", [128, 8, 2])
            B_halo = Buf()
            if seg == 0:
                kb.dve(lambda: nc.vector.memset(halo, 0.0), writes=[B_halo])
            else:
                cand = sb("cand", [8, D])
                selm = sb("selm", [8, 2])
                B_cand = Buf()
                kb.dma(SP, cand, cc_out_h, reads=[B_cc["out_h"]], writes=[B_cand])
                kb.dma(POOL, selm, halo_sel, writes=[B_cand])
                def f():
                    ins = None
                    for c in range(8):
                        ins = nc.tensor.matmul(PS[:, 0, 2 * c:2 * c + 2], lhsT=cand[:, c * 128:(c + 1) * 128], rhs=selm,
                                               start=True, stop=True)
                    return ins
                kb.pe(f, reads=[B_cand], writes=[PB[0]])
                kb.dve(lambda: nc.vector.tensor_copy(out=halo, in_=PS[:, 0, 0:16].rearrange("p (c n) -> p c n", n=2)),
                       reads=[PB[0]], writes=[B_halo])
            cw = sb("cw", [128, 3, 32], F32)
            cbias = sb("cbias", [128, 32], F32)
            B_cw = Buf()
            kb.dma(SP, cw, ffn_conv_w[layer], writes=[B_cw])
            kb.dma(SP, cbias, ffn_conv_b[layer], writes=[B_cw])
            w_gate = sb("w_gate", [128, 8, D])
            w_proj = sb("w_proj", [128, 2, D])
            B_wg = Buf()
            kb.dma(POOL, w_gate, ple_w_gate[layer].rearrange("(c p) n -> p c n", p=128), writes=[B_wg])
            kb.dma(POOL, w_proj, ple_w_proj[layer].rearrange("(c p) n -> p c n", p=128), writes=[B_wg])
            gfpost, gfpost_b = load_gain(es, "gfpost", g_ffn_post[layer:layer + 1, :])
            gple, gple_b = load_gain(es, "gple", g_ple[layer:layer + 1, :])
            if layer + 1 < nlayers:
                gnext, gnext_b = load_gain(es, "gnext", g_mix_pre[layer + 1:layer + 2, :])
            xfb = [sb("xfb%d" % i, [128, 8, TB + 2]) for i in range(2)]
            xfb_b = [Buf(), Buf()]
            hT = sb("hT", [128, 32, TB])
            B_hT = Buf()
            win = [sb("win%d" % i, [128, 8, 256]) for i in range(3)]
            win_b = [Buf() for _ in range(3)]
            wout = [sb("wout%d" % i, [128, 32, 256]) for i in range(2)]
            wout_b = [Buf(), Buf()]
            gs = [sb("gs%d" % i, [128, TB + 2], F32) for i in range(2)]
            gs_b = [Buf(), Buf()]
            ta = [sb("ta%d" % i, [128, TB], F32) for i in range(2)]
            ta_b = [Buf(), Buf()]
            tb_ = [sb("tb%d" % i, [128, TB], F32) for i in range(2)]
            tb_b = [Buf(), Buf()]
            yblk = [sb("yblk%d" % i, [128, D], F32) for i in range(4)]
            yblk_b = [Buf() for _ in range(4)]
            tl1 = alloc_norm_tiles(es, "f0")
            xs1 = sb("xf0", [128, D], F32)
            xs1_b = Buf()
            x21 = sb("x2f0", [128, D], F32)
            x21_b = Buf()
            pt1 = sb("pt0", [128, PLE])
            pt1_b = Buf()
            pT1 = sb("pT0", [128, 2, 128])
            pT1_b = Buf()
            gate1 = sb("gate0", [128, D], F32)
            gate1_b = Buf()
            st3 = sb("st3", [128, 32], F32)
            st3_b = Buf()
            def rms_steps(src_ap, src_bufs, n, scr, scr_b, col):
                kb.act(lambda: nc.scalar.activation(out=scr[:, 0:n], in_=src_ap, func=AF.Square, accum_out=st3[:, col:col + 1]),
                       reads=src_bufs, writes=[st3_b, scr_b])
                yield
                kb.act(lambda: nc.scalar.activation(out=st3[:, col + 1:col + 2], in_=st3[:, col:col + 1], func=AF.Sqrt, scale=1.0 / n, bias=EPS),
                       reads=[st3_b], writes=[st3_b])
                yield
                kb.dve(lambda: nc.vector.reciprocal(out=st3[:, col + 2:col + 3], in_=st3[:, col + 1:col + 2]), reads=[st3_b], writes=[st3_b])
                yield

            def epi(blk, tt):
                t = blk * (TB // 128) + tt
                ts = slice(t * 128, (t + 1) * 128)
                xs_, xs_b_, x2_, x2_b_, pt_, pt_b_, pT_, pT_b_, gate_, gate_b_ = xs1, xs1_b, x21, x21_b, pt1, pt1_b, pT1, pT1_b, gate1, gate1_b
                tl = tl1
                scr, scr_b = tl[0], tl[1]
                xn, xn_b = tl[4], tl[5]
                xT, xT_b = tl[6], tl[7]
                y = yblk[tt]
                kb.dma(SP, xs_, xres[seg, ts, :], reads=[B_xres[seg][t]], writes=[xs_b_])
                yield
                kb.dma(POOL, pt_, p_in[layer, seg, ts, :], writes=[pt_b_])
                yield
                yield from rms_steps(y, [yblk_b[tt]], D, scr, scr_b, 0)
                kb.dve(lambda: nc.vector.scalar_tensor_tensor(out=scr, in0=y, scalar=st3[:, 2:3], in1=gfpost[:], op0=ALU.mult, op1=ALU.mult),
                       reads=[yblk_b[tt], st3_b, gfpost_b], writes=[scr_b])
                yield
                kb.dve(lambda: nc.vector.tensor_tensor(out=x2_, in0=scr, in1=xs_, op=ALU.add), reads=[scr_b, xs_b_], writes=[x2_b_])
                yield
                yield from rms_steps(x2_, [x2_b_], D, scr, scr_b, 4)
                kb.dve(lambda: nc.vector.tensor_scalar(out=xn, in0=x2_, scalar1=st3[:, 6:7], scalar2=None, op0=ALU.mult),
                       reads=[x2_b_, st3_b], writes=[xn_b])
                yield
                yield 'pe'
                pv = transpose_to(xn, xn_b, 8, 6, None, None)
                yield
                kb.act(lambda: nc.scalar.activation(out=xT, in_=pv, func=AF.Copy), reads=[PB[6]], writes=[xT_b])
                yield
                def fp():
                    ins = None
                    for c in range(2):
                        ins = nc.tensor.transpose(ps_bf(7)[:, c * 128:(c + 1) * 128], pt_[:, c * 128:(c + 1) * 128], ident)
                    return ins
                kb.pe(fp, reads=[pt_b_, B_const], writes=[PB[7]])
                yield
                kb.act(lambda: nc.scalar.activation(out=pT_, in_=ps_bf(7)[:, 0:256].rearrange("p (c n) -> p c n", c=2), func=AF.Copy),
                       reads=[PB[7]], writes=[pT_b_])
                yield
                for half in range(2):
                    bkx = 6 + half
                    hs_ = slice(half * 512, (half + 1) * 512)
                    def fg():
                        ins = None
                        for c in range(8):
                            ins = nc.tensor.matmul(PS[:, bkx, :], lhsT=xT[:, c * 128:(c + 1) * 128], rhs=w_gate[:, c, hs_],
                                                   start=(c == 0), stop=(c == 7))
                        return ins
                    if half == 0:
                        yield 'pe'
                    kb.pe(fg, reads=[xT_b, B_wg], writes=[PB[bkx]])
                    yield
                    kb.act(lambda: nc.scalar.activation(out=gate_[:, hs_], in_=PS[:, bkx, :], func=AF.Sigmoid),
                           reads=[PB[bkx]], writes=[gate_b_])
                    yield
                for half in range(2):
                    bkx = 6 + half
                    hs_ = slice(half * 512, (half + 1) * 512)
                    def fe():
                        ins = None
                        for c in range(2):
                            ins = nc.tensor.matmul(PS[:, bkx, :], lhsT=pT_[:, c, :], rhs=w_proj[:, c, hs_], start=(c == 0), stop=(c == 1))
                        return ins
                    if half == 0:
                        yield 'pe'
                    kb.pe(fe, reads=[pT_b_, B_wg], writes=[PB[bkx]])
                    yield
                    kb.dve(lambda: nc.vector.tensor_tensor(out=gate_[:, hs_], in0=gate_[:, hs_], in1=PS[:, bkx, :], op=ALU.mult),
                           reads=[PB[bkx], gate_b_], writes=[gate_b_])
                    yield
                yield from rms_steps(gate_, [gate_b_], D, scr, scr_b, 8)
                kb.dve(lambda: nc.vector.scalar_tensor_tensor(out=scr, in0=gate_, scalar=st3[:, 10:11], in1=gple[:], op0=ALU.mult, op1=ALU.mult),
                       reads=[gate_b_, st3_b, gple_b], writes=[scr_b])
                yield
                kb.dve(lambda: nc.vector.tensor_tensor(out=xs_, in0=scr, in1=x2_, op=ALU.add), reads=[scr_b, x2_b_], writes=[xs_b_])
                yield
                if layer + 1 < nlayers:
                    kb.dma(POOL, xres[seg, ts, :], xs_, reads=[xs_b_], writes=[B_xres[seg][t]])
                    yield
                    yield from rms_steps(xs_, [xs_b_], D, scr, scr_b, 12)
                    kb.dve(lambda: nc.vector.scalar_tensor_tensor(out=xn, in0=xs_, scalar=st3[:, 14:15], in1=gnext[:], op0=ALU.mult, op1=ALU.mult),
                           reads=[xs_b_, st3_b, gnext_b], writes=[xn_b])
                    yield
                    yield 'pe'
                    pv2 = transpose_to(xn, xn_b, 8, 6, None, None)
                    yield
                    kb.act(lambda: nc.scalar.activation(out=xT, in_=pv2, func=AF.Copy), reads=[PB[6]], writes=[xT_b])
                    yield
                    kb.dma(POOL, xnT_d[seg, :, :, t * 128:(t + 1) * 128].rearrange("c p n -> p c n"),
                           xT.rearrange("p (c n) -> p c n", c=8), reads=[xT_b], writes=[B_xnT[seg][t]])
                    yield
                else:
                    kb.dma(POOL, y_out[seg, ts, :], xs_, reads=[xs_b_], writes=[B_xres[seg][t]])
                    yield

            pending = iter(())
            wk = 0
            wo_k = 0
            for blk in range(NB):
                c0 = blk * TB
                xi = blk % 2
                xfT = xfb[xi]
                B_xf = xfb_b[xi]
                lo = c0 - 1 if blk > 0 else c0
                hi = c0 + TB + 1 if blk < NB - 1 else c0 + TB
                kb.dma(SP, xfT[:, :, (lo - (c0 - 1)):(hi - (c0 - 1))], xfT_d[seg, :, :, lo:hi].rearrange("c p n -> p c n"),
                       reads=B_xfT[seg], writes=[B_xf])
                if blk == 0:
                    kb.dve(lambda: nc.vector.tensor_copy(out=xfT[:, :, 0:1], in_=halo[:, :, 0:1]), reads=[B_halo], writes=[B_xf])
                if blk == NB - 1:
                    kb.dve(lambda: nc.vector.tensor_copy(out=xfT[:, :, TB + 1:TB + 2], in_=halo[:, :, 1:2]), reads=[B_halo], writes=[B_xf])
                for ch in range(32):
                    wi = wk % 3
                    wk += 1
                    kb.dma(SP, win[wi][:, :, 0:128], w_in_bf[:, ch * 128:(ch + 1) * 128].rearrange("(c p) n -> p c n", p=128),
                           reads=[B_win], writes=[win_b[wi]])
                    kb.dma(SP, win[wi][:, :, 128:256],
                           w_in_bf[:, DFF + ch * 128:DFF + (ch + 1) * 128].rearrange("(c p) n -> p c n", p=128),
                           reads=[B_win], writes=[win_b[wi]])
                    i = ch % 2
                    bg, bh, bu = 3 * i, 3 * i + 1, 3 * i + 2
                    def f():
                        ins = None
                        for c in range(8):
                            nc.tensor.matmul(PS[:, bg, :], lhsT=win[wi][:, c, 0:128], rhs=xfT[:, c, 0:TB],
                                             start=(c == 0), stop=(c == 7))
                        for c in range(8):
                            nc.tensor.matmul(PS[:, bh, 0:2], lhsT=win[wi][:, c, 0:128], rhs=xfT[:, c, TB:TB + 2],
                                             start=(c == 0), stop=(c == 7))
                        for c in range(8):
                            ins = nc.tensor.matmul(PS[:, bu, :], lhsT=win[wi][:, c, 128:256], rhs=xfT[:, c, 1:1 + TB],
                                                   start=(c == 0), stop=(c == 7))
                        return ins
                    kb.pe(f, reads=[win_b[wi], B_xf], writes=[PB[bg], PB[bh], PB[bu]])
                    kb.act(lambda: nc.scalar.activation(out=gs[i][:, 0:TB], in_=PS[:, bg, :], func=AF.Copy),
                           reads=[PB[bg]], writes=[gs_b[i]])
                    kb.act(lambda: nc.scalar.activation(out=gs[i][:, TB:TB + 2], in_=PS[:, bh, 0:2], func=AF.Copy),
                           reads=[PB[bh]], writes=[gs_b[i]])
                    kb.dve(lambda: nc.vector.tensor_scalar(out=ta[i], in0=gs[i][:, 1:TB + 1], scalar1=cw[:, 1, ch:ch + 1],
                                                           scalar2=cbias[:, ch:ch + 1], op0=ALU.mult, op1=ALU.add),
                           reads=[gs_b[i], B_cw], writes=[ta_b[i]])
                    kb.dve(lambda: nc.vector.scalar_tensor_tensor(out=tb_[i], in0=gs[i][:, 0:TB], scalar=cw[:, 0, ch:ch + 1],
                                                                  in1=ta[i], op0=ALU.mult, op1=ALU.add),
                           reads=[gs_b[i], B_cw, ta_b[i]], writes=[tb_b[i]])
                    kb.dve(lambda: nc.vector.scalar_tensor_tensor(out=ta[i], in0=gs[i][:, 2:TB + 2], scalar=cw[:, 2, ch:ch + 1],
                                                                  in1=tb_[i], op0=ALU.mult, op1=ALU.add),
                           reads=[gs_b[i], B_cw, tb_b[i]], writes=[ta_b[i]])
                    kb.act(lambda: nc.scalar.activation(out=tb_[i], in_=ta[i], func=AF.Gelu_apprx_tanh),
                           reads=[ta_b[i]], writes=[tb_b[i]])
                    kb.dve(lambda: nc.vector.tensor_tensor(out=hT[:, ch, :], in0=tb_[i], in1=PS[:, bu, :], op=ALU.mult),
                           reads=[tb_b[i], PB[bu]], writes=[B_hT])
                    cnt_ = 0
                    while cnt_ < 7:
                        v_ = next(pending, 'end')
                        if v_ == 'end':
                            break
                        if v_ == 'pe':
                            if cnt_ > 0:
                                break
                            continue
                        cnt_ += 1
                for _ in pending:
                    pass
                for q4 in range(4):
                    wo = wo_k % 2
                    wo_k += 1
                    kb.dma(SP, wout[wo], w_out_bf[:, q4 * 256:(q4 + 1) * 256].rearrange("(c p) n -> p c n", p=128),
                           reads=[B_wout], writes=[wout_b[wo]])
                    for tt in range(TB // 128):
                        by = 6 + (tt % 2)
                        def f():
                            ins = None
                            for c in range(32):
                                ins = nc.tensor.matmul(PS[:, by, 0:256], lhsT=hT[:, c, tt * 128:(tt + 1) * 128], rhs=wout[wo][:, c, :],
                                                       start=(c == 0), stop=(c == 31))
                            return ins
                        kb.pe(f, reads=[B_hT, wout_b[wo]], writes=[PB[by]])
                        kb.act(lambda: nc.scalar.activation(out=yblk[tt][:, q4 * 256:(q4 + 1) * 256], in_=PS[:, by, 0:256], func=AF.Copy),
                               reads=[PB[by]], writes=[yblk_b[tt]])
                pending = itertools.chain(*[epi(blk, tt) for tt in range(TB // 128)])
            for _ in pending:
                pass
        kb.barrier()

    stage_initial()
    for layer in range(nlayers):
        kind, j = layer % 3, layer // 3
        for seg in (1, 0):
            if kind == 0:
                mixer_mla(layer, seg, j)
            elif kind == 1:
                mixer_gqa(layer, seg, j)
            else:
                mixer_na(layer, seg, j)
            if seg == 1:
                weights_to_bf16(layer)
        if STOP_MIX and layer == nlayers - 1 and NA_DBG == 'O':
            break
        if STOP_MIX and layer == nlayers - 1:
            with ExitStack() as esd:
                tcp = esd.enter_context(sbt("tcp", [128, D], F32))[:]
                bcp = Buf()
                for seg in (0, 1):
                    for t in range(NT):
                        kb.dma(SP, tcp, xres[seg, t * 128:(t + 1) * 128, :], reads=[B_xres[seg][t]], writes=[bcp])
                        kb.dma(SP, y_out[seg, t * 128:(t + 1) * 128, :], tcp, reads=[bcp], writes=[B_xres[seg][t]])
            break
        for seg in (0, 1):
            stage_ffn(layer, seg)
    kb.barrier()
    return nc


def _na_window(lr):
    if lr < 4:
        lo, hi = lr - 4, 7
    elif lr >= 28:
        lo, hi = 24, lr + 3
    else:
        lo, hi = lr - 4, lr + 3
    blo, bhi = lo + 4, hi + 4
    ws = blo - (blo % 2)
    nrows = bhi - ws + 1
    nch = (nrows + 1) // 2
    rho0 = ws - lr + 3
    layout = 0 if rho0 % 2 == 0 else 1
    pi0 = rho0 // 2
    return ws, nch, layout, pi0


def _rope_tables(pos, dim):
    inv = 1.0 / (10000.0 ** (np.arange(0, dim, 2, dtype=np.float32) / np.float32(dim)))
    ang = pos.astype(np.float32)[:, None] * inv[None, :].astype(np.float32)
    return np.cos(ang).astype(np.float32), np.sin(ang).astype(np.float32)


_NC_CACHE = {}


def kernel(_nlayers=DEPTH, **inputs):
    inp = {k: np.ascontiguousarray(np.asarray(v)) for k, v in inputs.items()}
    if _nlayers not in _NC_CACHE:
        _NC_CACHE[_nlayers] = build_program(_nlayers)
    nc = _NC_CACHE[_nlayers]
    f32 = np.float32
    w_uq = inp["mla_w_uq"]
    rp = np.empty((2, 384, 512), f32)
    for h in range(8):
        base = h * 192 + 128
        idx = base + (np.arange(64) + 32) % 64
        rp[:, :, h * 64:(h + 1) * 64] = w_uq[:, :, idx]
    shared = {k: inp[k] for k in ["norm_mix_pre", "norm_mix_post", "norm_ffn_pre", "norm_ffn_post", "ple_norm",
                                  "mla_w_down", "mla_q_norm", "mla_kv_norm", "mla_w_uq", "mla_w_ukv", "mla_w_o",
                                  "gqa_w_qkv", "gqa_q_norm", "gqa_k_norm", "gqa_w_o", "na_w_qkv", "na_w_o",
                                  "ffn_w_in", "ffn_w_out", "ple_w_proj", "ple_w_gate"]}
    shared["ffn_conv_wT"] = np.ascontiguousarray(inp["ffn_conv_w"].reshape(DEPTH, 3, 32, 128).transpose(0, 3, 1, 2))
    shared["ffn_conv_bT"] = np.ascontiguousarray(inp["ffn_conv_b"].reshape(DEPTH, 32, 128).transpose(0, 2, 1))
    shared["mla_w_uq_rp"] = rp
    shared["ident"] = np.eye(128, dtype=f32)
    rpb = inp["na_rpb"][0]
    cq = np.arange(64)
    c0 = np.clip(cq - 8, 0, 48)
    kcs = np.arange(64)
    valid = (kcs[:, None] >= c0[None, :]) & (kcs[:, None] < c0[None, :] + 16)
    relc = np.clip(kcs[:, None] - cq[None, :] + 15, 0, 30)
    T2 = np.full((16, 17, 64, 64), NEG, f32)
    for rho in range(15):
        T2[:, rho] = np.where(valid[None], rpb[:, rho][:, relc], f32(NEG))
    na_tab = np.empty((2, 128, 8, 16, 64), f32)
    for layout in range(2):
        for k in range(8):
            for jj in range(2):
                rho = 2 * k + jj + layout
                na_tab[layout, jj * 64:(jj + 1) * 64, k] = T2[:, rho].transpose(1, 0, 2)
    shared["na_tab"] = np.ascontiguousarray(
        na_tab.reshape(2, 128, 8, 4, 4, 64).transpose(0, 3, 1, 2, 4, 5).reshape(2, 4, 128, 8 * 4 * 64))
    in_maps = []
    scale = f32(192.0 ** -0.5)
    for core in range(8):
        grp, qtr = core // 4, core % 4
        m = dict(shared)
        m["x_in"] = np.stack([inp["x_prompt"][core], inp["x_sample"][grp, qtr * S:(qtr + 1) * S]], 0)
        m["p_in"] = np.stack([inp["p_prompt"][:, core], inp["p_sample"][:, grp, qtr * S:(qtr + 1) * S]], 1)
        cs_tok = np.zeros((2, S, 64), f32)
        cs_feat = np.zeros((2, 2, 64, S), f32)
        gq_tok = np.zeros((2, S, 128), f32)
        for seg in range(2):
            pos = np.arange(S) + (0 if seg == 0 else qtr * S)
            c, s = _rope_tables(pos, 64)
            cs_tok[seg, :, 0:32] = c
            cs_tok[seg, :, 32:64] = s
            cs_feat[seg, 0] = np.concatenate([c, c], 1).T * scale
            cs_feat[seg, 1] = np.concatenate([-s, s], 1).T * scale
            rc, rs = _rope_tables(pos // 64, 64)
            cc, cs_ = _rope_tables(pos % 64, 64)
            gq_tok[seg] = np.concatenate([rc, rs, cc, cs_], 1)
        m["mla_cs_tok"] = cs_tok
        m["mla_cs_feat"] = cs_feat
        m["gqa_cs_tok"] = gq_tok
        hs = np.zeros((8, 2), f32)
        if qtr > 0:
            hs[2 * (qtr - 1) + 1, 0] = 1.0
        if qtr < 3:
            hs[2 * (qtr + 1), 1] = 1.0
        m["halo_sel"] = hs
        rm = np.full((2, 128, 32, 6), NEG, f32)
        for seg in range(2):
            base_row = 0 if seg == 0 else qtr * 32
            R = 32 if seg == 0 else 128
            for lr in range(32):
                ws, nch, layout, pi0 = _na_window(lr)
                rq = base_row + lr
                r0 = min(max(rq - 4, 0), R - 8)
                for ci in range(nch):
                    for jj in range(2):
                        rg = base_row + (ws + 2 * ci + jj - 4)
                        if r0 <= rg <= r0 + 7:
                            rm[seg, jj * 64:(jj + 1) * 64, lr, ci] = 0.0
        m["na_rmask"] = rm
        hsel = np.zeros((128, 8), f32)
        if qtr > 0:
            hsel[:, qtr - 1] = 1.0
        if qtr < 3:
            hsel[:, 4 + qtr + 1] = 1.0
        m["na_hsel"] = hsel
        in_maps.append(m)
    res = run_bass_kernel_spmd(nc, in_maps, core_ids=list(range(8)))
    outs = [r["y_out"] for r in res.results]
    y_prompt = np.stack([outs[c][0] for c in range(8)], 0).astype(f32)
    y_sample = np.stack([np.concatenate([outs[g * 4 + q][1] for q in range(4)], 0) for g in range(2)], 0).astype(f32)
    return (y_prompt, y_sample)
```

```python
import numpy as np
import ml_dtypes
from contextlib import ExitStack
import itertools
import concourse.bass as bass
import concourse.mybir as mybir
from concourse.bass_utils import run_bass_kernel_spmd

F32 = mybir.dt.float32
BF16 = mybir.dt.bfloat16
AF = mybir.ActivationFunctionType
ALU = mybir.AluOpType

D = 1024
S = 2048
NT = S // 128
DEPTH = 4
DFF = 4096
PLE = 256
EPS = 1e-6
NEG = -30000.0
SEM_ROT = 6000
NO_CC = False
NA_DBG = ''
STOP_MIX = False


class Buf:
    __slots__ = ("w", "rs")

    def __init__(self):
        self.w = None
        self.rs = []


class Eng:
    def __init__(self, kb, name, eng):
        self.kb = kb
        self.name = name
        self.eng = eng
        self.sem = None
        self.semi = -1
        self.cnt = 0
        self.waited = {}
        self.newsem()

    def newsem(self):
        self.sem = self.kb.nc.alloc_semaphore()
        self.kb.sems.append(self.sem)
        self.semi = len(self.kb.sems) - 1
        self.cnt = 0

    def wait(self, tok):
        si, val, _ = tok
        cur = self.kb.dma_slot_val.get(si)
        if cur is not None and cur > val:
            val = cur
        if self.waited.get(si, 0) >= val:
            return
        self.eng.wait_ge(self.kb.sems[si], val)
        self.waited[si] = val

    def cur(self):
        return (self.semi, self.cnt, self) if self.cnt > 0 else None


class KB:
    def __init__(self):
        self.nc = bass.Bass("TRN2", target_bir_lowering=False)
        nc = self.nc
        self.sems = []
        self.PE = Eng(self, "pe", nc.tensor)
        self.ACT = Eng(self, "act", nc.scalar)
        self.DVE = Eng(self, "dve", nc.vector)
        self.POOL = Eng(self, "pool", nc.gpsimd)
        self.SP = Eng(self, "sp", nc.sync)
        self.engs = [self.PE, self.ACT, self.DVE, self.POOL, self.SP]
        self.dma_sems = []
        for _ in range(16):
            s = nc.alloc_semaphore()
            self.sems.append(s)
            self.dma_sems.append([len(self.sems) - 1, 0])
        self.dma_rr = 0
        self.dma_slot_val = {}
        self.old_final = []

    def _deps(self, E, reads, writes):
        for b in reads:
            if b.w is not None:
                if b.w[2] is E and E is self.PE:
                    continue
                E.wait(b.w)
        for b in writes:
            if b.w is not None and b.w[2] is not E:
                E.wait(b.w)
            for t in b.rs:
                if t[2] is not E:
                    E.wait(t)

    def _commit(self, tok, reads, writes):
        for b in writes:
            b.w = tok
            b.rs = []
        for b in reads:
            b.rs.append(tok)
            if len(b.rs) > 12:
                best = {}
                for t in b.rs:
                    if t[0] not in best or best[t[0]][1] < t[1]:
                        best[t[0]] = t
                b.rs = list(best.values())

    def op(self, E, fn, reads=(), writes=()):
        if E.cnt >= SEM_ROT:
            self.old_final.append(E.cur())
            E.newsem()
        self._deps(E, reads, writes)
        ins = fn()
        E.cnt += 1
        ins.then_inc(E.sem, 1)
        tok = (E.semi, E.cnt, E)
        self._commit(tok, reads, writes)
        return tok

    def pe(self, fn, reads=(), writes=()):
        return self.op(self.PE, fn, reads, writes)

    def act(self, fn, reads=(), writes=()):
        return self.op(self.ACT, fn, reads, writes)

    def dve(self, fn, reads=(), writes=()):
        return self.op(self.DVE, fn, reads, writes)

    def pool(self, fn, reads=(), writes=()):
        return self.op(self.POOL, fn, reads, writes)

    def dma(self, Q, out, in_, reads=(), writes=()):
        self._deps(Q, reads, writes)
        slot = self.dma_sems[self.dma_rr]
        self.dma_rr = (self.dma_rr + 1) % len(self.dma_sems)
        if slot[1] >= SEM_ROT:
            self.old_final.append((slot[0], slot[1], None))
            s = self.nc.alloc_semaphore()
            self.sems.append(s)
            slot[0] = len(self.sems) - 1
            slot[1] = 0
        Q.eng.dma_start(out=out, in_=in_).then_inc(self.sems[slot[0]], 16)
        slot[1] += 16
        self.dma_slot_val[slot[0]] = slot[1]
        tok = (slot[0], slot[1], None)
        self._commit(tok, reads, writes)
        return tok

    def collective(self, ins, outs, reads, writes, groups):
        Q = self.POOL
        if NO_CC:
            n = ins[0].shape[0]
            return self.dma(Q, outs[0][0:n], ins[0], reads=reads, writes=writes)
        self._deps(Q, reads, writes)
        s = self.nc.alloc_semaphore()
        self.sems.append(s)
        si = len(self.sems) - 1
        Q.eng.collective_compute("AllGather", ALU.bypass, replica_groups=groups,
                                 ins=ins, outs=outs).then_inc(s, 1)
        tok = (si, 1, None)
        self._commit(tok, reads, writes)
        return tok

    def barrier(self):
        toks = [e.cur() for e in self.engs if e.cur() is not None]
        toks += [(s[0], s[1], None) for s in self.dma_sems if s[1] > 0]
        toks += self.old_final
        self.old_final = []
        for e in self.engs:
            for t in toks:
                if t[2] is not e:
                    e.wait(t)


def build_program(nlayers=DEPTH, debug=False):
    kb = KB()
    nc = kb.nc
    PE, ACT, DVE, POOL, SP = kb.PE, kb.ACT, kb.DVE, kb.POOL, kb.SP

    _ctr = [0]

    def sbt(name, shape, dt):
        _ctr[0] += 1
        return nc.sbuf_tensor("%s_%d" % (name, _ctr[0]), list(shape), dt)

    def din(name, shape, dt=F32):
        return nc.dram_tensor(name, list(shape), dt, kind="ExternalInput").ap()

    def dscr(name, shape, dt=BF16):
        return nc.dram_tensor(name, list(shape), dt).ap()

    x_in = din("x_in", [2, S, D])
    p_in = din("p_in", [DEPTH, 2, S, PLE])
    y_out = nc.dram_tensor("y_out", [2, S, D], F32, kind="ExternalOutput").ap()
    g_mix_pre = din("norm_mix_pre", [DEPTH, D])
    g_mix_post = din("norm_mix_post", [DEPTH, D])
    g_ffn_pre = din("norm_ffn_pre", [DEPTH, D])
    g_ffn_post = din("norm_ffn_post", [DEPTH, D])
    g_ple = din("ple_norm", [DEPTH, D])
    mla_w_down = din("mla_w_down", [2, D, 704])
    mla_q_norm = din("mla_q_norm", [2, 384])
    mla_kv_norm = din("mla_kv_norm", [2, 256])
    mla_w_uq = din("mla_w_uq", [2, 384, 1536])
    mla_w_uq_rp = din("mla_w_uq_rp", [2, 384, 512])
    mla_w_ukv = din("mla_w_ukv", [2, 256, 2048])
    mla_w_o = din("mla_w_o", [2, D, D])
    gqa_w_qkv = din("gqa_w_qkv", [1, D, 1536])
    gqa_q_norm = din("gqa_q_norm", [1, 128])
    gqa_k_norm = din("gqa_k_norm", [1, 128])
    gqa_w_o = din("gqa_w_o", [1, D, D])
    na_w_qkv = din("na_w_qkv", [1, D, 3072])
    na_w_o = din("na_w_o", [1, D, D])
    na_tab = din("na_tab", [2, 4, 128, 8 * 4 * 64])
    na_rmask = din("na_rmask", [2, 128, 32, 6])
    na_hsel = din("na_hsel", [128, 8])
    ffn_w_in = din("ffn_w_in", [DEPTH, D, 2 * DFF])
    ffn_conv_w = din("ffn_conv_wT", [DEPTH, 128, 3, 32])
    ffn_conv_b = din("ffn_conv_bT", [DEPTH, 128, 32])
    ffn_w_out = din("ffn_w_out", [DEPTH, DFF, D])
    ple_w_proj = din("ple_w_proj", [DEPTH, PLE, D])
    ple_w_gate = din("ple_w_gate", [DEPTH, D, D])
    ident_in = din("ident", [128, 128])
    halo_sel = din("halo_sel", [8, 2])
    mla_cs_tok = din("mla_cs_tok", [2, S, 64])
    mla_cs_feat = din("mla_cs_feat", [2, 2, 64, S])
    gqa_cs_tok = din("gqa_cs_tok", [2, S, 128])

    xres = nc.dram_tensor("xres", [2, S, D], F32).ap()
    xnT_d = dscr("xnT_d", [2, 8, 128, S])
    xfT_d = dscr("xfT_d", [2, 8, 128, S])
    w_in_bf = dscr("w_in_bf", [D, 2 * DFF])
    w_out_bf = dscr("w_out_bf", [DFF, D])
    cc_in_mla = dscr("cc_in_mla", [256, S])
    cc_out_mla = dscr("cc_out_mla", [4 * 256, S])
    cc_in_mlb = dscr("cc_in_mlb", [64, S])
    cc_out_mlb = dscr("cc_out_mlb", [4 * 64, S])
    cc_in_gk = dscr("cc_in_gk", [256, S])
    cc_out_gk = dscr("cc_out_gk", [4 * 256, S])
    cc_in_gv = dscr("cc_in_gv", [S, 256])
    cc_out_gv = dscr("cc_out_gv", [4 * S, 256])
    cc_in_h = dscr("cc_in_h", [2, D])
    cc_out_h = dscr("cc_out_h", [8, D])
    na_kv_d = dscr("na_kv_d", [2, S, 2048])
    cc_in_na = dscr("cc_in_na", [512, 2048])
    cc_out_na = dscr("cc_out_na", [4 * 512, 2048])
    GROUPS = [[0, 1, 2, 3], [4, 5, 6, 7]]

    B_xres = [[Buf() for _ in range(NT)] for _ in range(2)]
    B_xnT = [[Buf() for _ in range(NT)] for _ in range(2)]
    B_xfT = [[Buf() for _ in range(NT)] for _ in range(2)]
    B_win = Buf()
    B_wout = Buf()
    B_cc = {k: Buf() for k in ["in_mla", "out_mla", "in_mlb", "out_mlb", "in_gk", "out_gk", "in_gv", "out_gv", "in_h", "out_h",
                               "in_na", "out_na", "nakv0", "nakv1"]}

    ident = nc.alloc_sbuf_tensor("identb", [128, 128], BF16).ap()
    ones = nc.alloc_sbuf_tensor("onesb", [128, 128], BF16).ap()
    B_const = Buf()
    PS = nc.alloc_psum_tensor("PS", [128, 8, 512], F32).ap()
    PB = [Buf() for _ in range(8)]
    kb.dma(POOL, ident, ident_in, writes=[B_const])
    kb.dve(lambda: nc.vector.memset(ones, 1.0), writes=[B_const])

    def ps_bf(b0, nb=1):
        return PS[:, b0:b0 + nb, :].rearrange("p b n -> p (b n)").bitcast(BF16)

    def ps_f(b0, nb=1):
        return PS[:, b0:b0 + nb, :].rearrange("p b n -> p (b n)")

    def load_gain(es, name, src_row, n=D):
        t = es.enter_context(sbt(name, [128, n], F32))
        b = Buf()
        kb.dma(SP, t[:], src_row.to_broadcast([128, n]), writes=[b])
        return t, b

    def rms_stats(src_ap, src_bufs, n, scr, stat, stat_b, col, scr_b=None):
        kb.act(lambda: nc.scalar.activation(out=scr[:, 0:n], in_=src_ap, func=AF.Square,
                                            accum_out=stat[:, col:col + 1]),
               reads=src_bufs, writes=[stat_b] + ([scr_b] if scr_b is not None else []))
        kb.act(lambda: nc.scalar.activation(out=stat[:, col + 1:col + 2], in_=stat[:, col:col + 1],
                                            func=AF.Sqrt, scale=1.0 / n, bias=EPS),
               reads=[stat_b], writes=[stat_b])
        kb.dve(lambda: nc.vector.reciprocal(out=stat[:, col + 2:col + 3], in_=stat[:, col + 1:col + 2]),
               reads=[stat_b], writes=[stat_b])
        return stat[:, col + 2:col + 3]

    def transpose_to(src_bf, src_b, nchunk, pbank, dst_fn, dst_bufs, rows=128):
        pv = ps_bf(pbank)
        def f():
            ins = None
            for c in range(nchunk):
                ins = nc.tensor.transpose(pv[:, c * 128:(c + 1) * 128], src_bf[:, c * 128:(c + 1) * 128], ident)
            return ins
        kb.pe(f, reads=[src_b, B_const], writes=[PB[pbank]])
        return pv

    def norm_T_store(es_tiles, x_sb, x_b, gain, gain_b, dst_d, dst_b, seg, t, pbank, extra_row=None):
        scr, scr_b, stat, stat_b, xn, xn_b, xT, xT_b = es_tiles
        rstd = rms_stats(x_sb, [x_b], D, scr, stat, stat_b, 0, scr_b)
        if gain is not None:
            kb.dve(lambda: nc.vector.scalar_tensor_tensor(out=xn, in0=x_sb, scalar=rstd, in1=gain,
                                                          op0=ALU.mult, op1=ALU.mult),
                   reads=[x_b, stat_b, gain_b], writes=[xn_b])
        else:
            kb.dve(lambda: nc.vector.tensor_scalar(out=xn, in0=x_sb, scalar1=rstd, scalar2=None, op0=ALU.mult),
                   reads=[x_b, stat_b], writes=[xn_b])
        if extra_row is not None:
            extra_row(xn, xn_b)
        pv = transpose_to(xn, xn_b, 8, pbank, None, None)
        kb.act(lambda: nc.scalar.activation(out=xT, in_=pv, func=AF.Copy), reads=[PB[pbank]], writes=[xT_b])
        if dst_d is not None:
            kb.dma(POOL, dst_d[seg, :, :, t * 128:(t + 1) * 128].rearrange("c p n -> p c n"),
                   xT.rearrange("p (c n) -> p c n", c=8), reads=[xT_b], writes=[dst_b[seg][t]])

    def alloc_norm_tiles(es, tag):
        scr = es.enter_context(sbt("scr" + tag, [128, D], F32))[:]
        stat = es.enter_context(sbt("stat" + tag, [128, 16], F32))[:]
        xn = es.enter_context(sbt("xn" + tag, [128, D], BF16))[:]
        xT = es.enter_context(sbt("xT" + tag, [128, D], BF16))[:]
        return (scr, Buf(), stat, Buf(), xn, Buf(), xT, Buf())

    def stage_initial():
        with ExitStack() as es:
            gain, gain_b = load_gain(es, "g0", g_mix_pre[0:1, :])
            tl = [alloc_norm_tiles(es, "i%d" % i) for i in range(2)]
            xs = [es.enter_context(sbt("xi%d" % i, [128, D], F32))[:] for i in range(2)]
            xb = [Buf(), Buf()]
            k = 0
            for seg in range(2):
                for t in range(NT):
                    j = k % 2
                    kb.dma(SP, xs[j], x_in[seg, t * 128:(t + 1) * 128, :], writes=[xb[j]])
                    kb.dma(POOL, xres[seg, t * 128:(t + 1) * 128, :], xs[j], reads=[xb[j]], writes=[B_xres[seg][t]])
                    norm_T_store(tl[j], xs[j], xb[j], gain[:], gain_b, xnT_d, B_xnT, seg, t, j)
                    k += 1
        kb.barrier()

    def attn_head(es_at, nk, qparts, kparts, vfn, OT_dst, OT_b, extra_reads):
        PT, PT_b, rec, rec_b = es_at
        nkt = nk // 128
        ngrp = nkt // 2
        for qb in range(S // 512):
            qs = slice(qb * 512, (qb + 1) * 512)

            def scores(g):
                b0 = 2 * (g % 3)
                def f():
                    ins = None
                    for j in range(2):
                        kt = 2 * g + j
                        for pi, ((q_ap, rows), (k_ap, _)) in enumerate(zip(qparts, kparts)):
                            ins = nc.tensor.matmul(PS[:, b0 + j, :], lhsT=k_ap[0:rows, kt * 128:(kt + 1) * 128],
                                                   rhs=q_ap[0:rows, qs], start=(pi == 0),
                                                   stop=(pi == len(qparts) - 1))
                    return ins
                kb.pe(f, reads=extra_reads, writes=[PB[b0], PB[b0 + 1]])

            def expo(g):
                b0 = 2 * (g % 3)
                kb.act(lambda: nc.scalar.activation(out=PT[g % 3], in_=ps_f(b0, 2), func=AF.Exp),
                       reads=[PB[b0], PB[b0 + 1]], writes=[PT_b[g % 3]])

            def pv(g):
                def f():
                    ins = None
                    for j in range(2):
                        kt = 2 * g + j
                        rhs = PT[g % 3][:, j * 512:(j + 1) * 512]
                        nc.tensor.matmul(PS[:, 6, :], lhsT=vfn(kt), rhs=rhs, start=(kt == 0), stop=(kt == nkt - 1))
                        ins = nc.tensor.matmul(PS[:, 7, :], lhsT=ones, rhs=rhs, start=(kt == 0), stop=(kt == nkt - 1))
                    return ins
                kb.pe(f, reads=[PT_b[g % 3], B_const] + extra_reads, writes=[PB[6], PB[7]])

            for g in range(ngrp):
                scores(g)
                expo(g)
                if g > 1:
                    pv(g - 2)
            if ngrp > 1:
                pv(ngrp - 2)
            pv(ngrp - 1)
            kb.dve(lambda: nc.vector.reciprocal(out=rec, in_=PS[:, 7, :]), reads=[PB[7]], writes=[rec_b])
            kb.dve(lambda: nc.vector.tensor_tensor(out=OT_dst[:, qs], in0=PS[:, 6, :], in1=rec, op=ALU.mult),
                   reads=[PB[6], rec_b], writes=[OT_b])

    def alloc_attn_tiles(es):
        PT = [es.enter_context(sbt("PT%d" % i, [128, 1024], BF16))[:] for i in range(3)]
        rec = es.enter_context(sbt("rec", [128, 512], F32))[:]
        return (PT, [Buf(), Buf(), Buf()], rec, Buf())

    def stage_out(es, layer, seg, OT, OT_b, wo_src, nchunk_rows, last_phase_cb=None):
        rows, nch = nchunk_rows
        wo = es.enter_context(sbt("wo", [rows, nch, D], BF16))
        wo_b = Buf()
        kb.dma(POOL, wo[:], wo_src.rearrange("(c p) n -> p c n", p=rows), writes=[wo_b])
        gpost, gpost_b = load_gain(es, "gpost", g_mix_post[layer:layer + 1, :])
        gfpre, gfpre_b = load_gain(es, "gfpre", g_ffn_pre[layer:layer + 1, :])
        tl = [alloc_norm_tiles(es, "c%d" % i) for i in range(2)]
        xs = [es.enter_context(sbt("xc%d" % i, [128, D], F32))[:] for i in range(2)]
        xb = [Buf(), Buf()]
        x1 = [es.enter_context(sbt("x1c%d" % i, [128, D], F32))[:] for i in range(2)]
        x1b = [Buf(), Buf()]
        st2 = es.enter_context(sbt("st2", [128, 16], F32))[:]
        st2_b = Buf()
        for t in range(NT):
            j = t % 2
            ts = slice(t * 128, (t + 1) * 128)
            kb.dma(SP, xs[j], xres[seg, ts, :], reads=[B_xres[seg][t]], writes=[xb[j]])
            b0 = 6 if False else (2 * j)
            def f():
                ins = None
                for half in range(2):
                    for c in range(nch):
                        ins = nc.tensor.matmul(PS[:, b0 + half, :], lhsT=OT[0:rows, c, ts],
                                               rhs=wo[0:rows, c, half * 512:(half + 1) * 512],
                                               start=(c == 0), stop=(c == nch - 1))
                return ins
            kb.pe(f, reads=[OT_b, wo_b], writes=[PB[b0], PB[b0 + 1]])
            scr, scr_b = tl[j][0], tl[j][1]
            rstd = rms_stats(ps_f(b0, 2), [PB[b0], PB[b0 + 1]], D, scr, st2, st2_b, 4 * j, scr_b)
            kb.dve(lambda: nc.vector.scalar_tensor_tensor(out=scr, in0=ps_f(b0, 2), scalar=rstd, in1=gpost[:],
                                                          op0=ALU.mult, op1=ALU.mult),
                   reads=[PB[b0], PB[b0 + 1], st2_b, gpost_b], writes=[scr_b])
            kb.dve(lambda: nc.vector.tensor_tensor(out=x1[j], in0=scr, in1=xs[j], op=ALU.add),
                   reads=[scr_b, xb[j]], writes=[x1b[j]])
            kb.dma(POOL, xres[seg, ts, :], x1[j], reads=[x1b[j]], writes=[B_xres[seg][t]])
            extra = None
            if seg == 1 and t in (0, NT - 1):
                def extra(xn, xn_b, t=t):
                    if t == 0:
                        kb.dma(POOL, cc_in_h[0:1, :], xn[0:1, :], reads=[xn_b], writes=[B_cc["in_h"]])
                    else:
                        kb.dma(POOL, cc_in_h[1:2, :], xn[127:128, :], reads=[xn_b], writes=[B_cc["in_h"]])
            norm_T_store(tl[j], x1[j], x1b[j], gfpre[:], gfpre_b, xfT_d, B_xfT, seg, t, 6 + j, extra_row=extra)
        if seg == 1:
            kb.collective([cc_in_h], [cc_out_h], reads=[B_cc["in_h"]], writes=[B_cc["out_h"]], groups=GROUPS)

    def mixer_mla(layer, seg, j):
        nk = S if seg == 0 else 4 * S
        with ExitStack() as es0, ExitStack() as es:
            OT = es0.enter_context(sbt("OT", [128, 8, S], BF16))[:]
            sb = lambda name, shape, dt=BF16: es.enter_context(sbt(name, shape, dt))[:]
            ckvT = sb("ckvT", [128, 2, nk])
            kropeT = sb("kropeT", [64, nk])
            cqT = sb("cqT", [128, 3, S])
            B_ckv, B_cq, B_OT = Buf(), Buf(), Buf()
            w_uq = sb("w_uq", [128, 3, 1536])
            w_uqr = sb("w_uqr", [128, 3, 512])
            w_ukv = sb("w_ukv", [128, 2, 2048])
            B_w = Buf()
            kb.dma(POOL, w_uq, mla_w_uq[j].rearrange("(c p) n -> p c n", p=128), writes=[B_w])
            kb.dma(POOL, w_uqr, mla_w_uq_rp[j].rearrange("(c p) n -> p c n", p=128), writes=[B_w])
            kb.dma(POOL, w_ukv, mla_w_ukv[j].rearrange("(c p) n -> p c n", p=128), writes=[B_w])
            csf = sb("csf", [64, 2, S], F32)
            B_csf = Buf()
            kb.dma(SP, csf, mla_cs_feat[seg].rearrange("a p n -> p a n"), writes=[B_csf])
            with ExitStack() as es2:
                sb2 = lambda name, shape, dt=BF16: es2.enter_context(sbt(name, shape, dt))[:]
                xnTb = [sb2("xnTb%d" % i, [128, 8, 512]) for i in range(2)]
                B_xnb = [Buf(), Buf()]
                wd = sb2("wd", [128, 8, 704])
                B_wd = Buf()
                kb.dma(POOL, wd, mla_w_down[j].rearrange("(c p) n -> p c n", p=128), writes=[B_wd])
                gq, gq_b = load_gain(es2, "gq", mla_q_norm[j:j + 1, :], 384)
                gkv, gkv_b = load_gain(es2, "gkv", mla_kv_norm[j:j + 1, :], 256)
                cst = sb2("cst", [128, NT, 64], F32)
                B_cst = Buf()
                kb.dma(SP, cst, mla_cs_tok[seg].rearrange("(t p) n -> p t n", p=128), writes=[B_cst])
                scr = sb2("scrA", [128, 512], F32)
                stat = sb2("statA", [128, 16], F32)
                stat_b = Buf()
                dnb = [sb2("dnb%d" % i, [128, 768]) for i in range(2)]
                dnb_b = [Buf(), Buf()]
                tmp = [sb2("tmpA%d" % i, [128, 128], F32) for i in range(2)]
                tmp_b = [Buf(), Buf()]
                for t in range(NT):
                    i = t % 2
                    ts = slice(t * 128, (t + 1) * 128)
                    b0 = 2 * i
                    xb_i = (t // 4) % 2
                    if t % 4 == 0:
                        kb.dma(SP, xnTb[xb_i], xnT_d[seg, :, :, t * 128:t * 128 + 512].rearrange("c p n -> p c n"),
                               reads=B_xnT[seg][t:t + 4], writes=[B_xnb[xb_i]])
                    xnT = xnTb[xb_i]
                    B_xn = B_xnb[xb_i]
                    tl_ = slice((t % 4) * 128, (t % 4 + 1) * 128)
                    def f():
                        ins = None
                        for c in range(8):
                            nc.tensor.matmul(PS[:, b0, 0:384], lhsT=xnT[:, c, tl_], rhs=wd[:, c, 0:384],
                                             start=(c == 0), stop=(c == 7))
                        for c in range(8):
                            ins = nc.tensor.matmul(PS[:, b0 + 1, 0:320], lhsT=xnT[:, c, tl_], rhs=wd[:, c, 384:704],
                                                   start=(c == 0), stop=(c == 7))
                        return ins
                    kb.pe(f, reads=[B_xn, B_wd], writes=[PB[b0], PB[b0 + 1]])
                    rq = rms_stats(PS[:, b0, 0:384], [PB[b0]], 384, scr, stat, stat_b, 8 * i)
                    rkv = rms_stats(PS[:, b0 + 1, 0:256], [PB[b0 + 1]], 256, scr, stat, stat_b, 8 * i + 4)
                    kb.dve(lambda: nc.vector.scalar_tensor_tensor(out=dnb[i][:, 0:384], in0=PS[:, b0, 0:384], scalar=rq,
                                                                  in1=gq[:], op0=ALU.mult, op1=ALU.mult),
                           reads=[PB[b0], stat_b, gq_b], writes=[dnb_b[i]])
                    kb.dve(lambda: nc.vector.scalar_tensor_tensor(out=dnb[i][:, 384:640], in0=PS[:, b0 + 1, 0:256], scalar=rkv,
                                                                  in1=gkv[:], op0=ALU.mult, op1=ALU.mult),
                           reads=[PB[b0 + 1], stat_b, gkv_b], writes=[dnb_b[i]])
                    x1 = PS[:, b0 + 1, 256:288]
                    x2 = PS[:, b0 + 1, 288:320]
                    cs_c = cst[:, t, 0:32]
                    cs_s = cst[:, t, 32:64]
                    tm = tmp[i]
                    kb.dve(lambda: nc.vector.tensor_tensor(out=tm[:, 0:32], in0=x1, in1=cs_c, op=ALU.mult),
                           reads=[PB[b0 + 1], B_cst], writes=[tmp_b[i]])
                    kb.dve(lambda: nc.vector.tensor_tensor(out=tm[:, 32:64], in0=x2, in1=cs_s, op=ALU.mult),
                           reads=[PB[b0 + 1], B_cst], writes=[tmp_b[i]])
                    kb.dve(lambda: nc.vector.tensor_tensor(out=tm[:, 64:96], in0=x2, in1=cs_c, op=ALU.mult),
                           reads=[PB[b0 + 1], B_cst], writes=[tmp_b[i]])
                    kb.dve(lambda: nc.vector.tensor_tensor(out=tm[:, 96:128], in0=x1, in1=cs_s, op=ALU.mult),
                           reads=[PB[b0 + 1], B_cst], writes=[tmp_b[i]])
                    kb.dve(lambda: nc.vector.tensor_tensor(out=dnb[i][:, 640:672], in0=tm[:, 0:32], in1=tm[:, 32:64],
                                                           op=ALU.subtract), reads=[tmp_b[i]], writes=[dnb_b[i]])
                    kb.dve(lambda: nc.vector.tensor_tensor(out=dnb[i][:, 672:704], in0=tm[:, 64:96], in1=tm[:, 96:128],
                                                           op=ALU.add), reads=[tmp_b[i]], writes=[dnb_b[i]])
                    pbank = 4 + i
                    pvw = ps_bf(pbank)
                    def ft():
                        ins = None
                        for c in range(5):
                            ins = nc.tensor.transpose(pvw[:, c * 128:(c + 1) * 128], dnb[i][:, c * 128:(c + 1) * 128], ident)
                        ins = nc.tensor.transpose(pvw[0:64, 640:768], dnb[i][:, 640:704], ident)
                        return ins
                    kb.pe(ft, reads=[dnb_b[i], B_const], writes=[PB[pbank]])
                    off = 0 if seg == 0 else 0
                    kb.act(lambda: nc.scalar.activation(out=cqT[:, :, ts], in_=pvw[:, 0:384].rearrange("p (c n) -> p c n", c=3),
                                                        func=AF.Copy), reads=[PB[pbank]], writes=[B_cq])
                    if seg == 0:
                        kb.act(lambda: nc.scalar.activation(out=ckvT[:, :, ts], in_=pvw[:, 384:640].rearrange("p (c n) -> p c n", c=2),
                                                            func=AF.Copy), reads=[PB[pbank]], writes=[B_ckv])
                        kb.act(lambda: nc.scalar.activation(out=kropeT[:, ts], in_=pvw[0:64, 640:768], func=AF.Copy),
                               reads=[PB[pbank]], writes=[B_ckv])
                    else:
                        kb.act(lambda: nc.scalar.activation(out=ckvT[:, :, ts], in_=pvw[:, 384:640].rearrange("p (c n) -> p c n", c=2),
                                                            func=AF.Copy), reads=[PB[pbank]], writes=[B_ckv])
                        kb.act(lambda: nc.scalar.activation(out=kropeT[:, ts], in_=pvw[0:64, 640:768], func=AF.Copy),
                               reads=[PB[pbank]], writes=[B_ckv])
                if seg == 1:
                    kb.dma(POOL, cc_in_mla.rearrange("(c p) n -> p c n", p=128), ckvT[:, :, 0:S],
                           reads=[B_ckv], writes=[B_cc["in_mla"]])
                    kb.dma(POOL, cc_in_mlb, kropeT[:, 0:S], reads=[B_ckv], writes=[B_cc["in_mlb"]])
                    kb.collective([cc_in_mla], [cc_out_mla], reads=[B_cc["in_mla"]], writes=[B_cc["out_mla"]], groups=GROUPS)
                    kb.collective([cc_in_mlb], [cc_out_mlb], reads=[B_cc["in_mlb"]], writes=[B_cc["out_mlb"]], groups=GROUPS)
                    for r in range(4):
                        kb.dma(SP, ckvT[:, :, r * S:(r + 1) * S],
                               cc_out_mla[r * 256:(r + 1) * 256, :].rearrange("(c p) n -> p c n", p=128),
                               reads=[B_cc["out_mla"]], writes=[B_ckv])
                        kb.dma(SP, kropeT[:, r * S:(r + 1) * S], cc_out_mlb[r * 64:(r + 1) * 64, :],
                               reads=[B_cc["out_mlb"]], writes=[B_ckv])
            kb.barrier()
            with ExitStack() as es3:
                sb3 = lambda name, shape, dt=BF16: es3.enter_context(sbt(name, shape, dt))[:]
                qnT = sb3("qnT", [128, S])
                qrT = sb3("qrT", [64, S])
                KhT = sb3("KhT", [128, nk])
                Vh = sb3("Vh", [128, nk // 128, 128])
                B_q, B_K, B_V = Buf(), Buf(), Buf()
                t1 = sb3("t1", [64, 512], F32)
                t2 = sb3("t2", [64, 512], F32)
                B_t = Buf()
                at = alloc_attn_tiles(es3)
                scale = 192.0 ** -0.5
                for h in range(8):
                    for qb in range(S // 512):
                        qs = slice(qb * 512, (qb + 1) * 512)
                        bq = 4 + (qb % 2)
                        def f():
                            ins = None
                            for c in range(3):
                                ins = nc.tensor.matmul(PS[:, bq, :], lhsT=w_uq[:, c, h * 192:h * 192 + 128], rhs=cqT[:, c, qs],
                                                       start=(c == 0), stop=(c == 2))
                            return ins
                        kb.pe(f, reads=[B_w, B_cq], writes=[PB[bq]])
                        kb.act(lambda: nc.scalar.activation(out=qnT[:, qs], in_=PS[:, bq, :], func=AF.Copy, scale=scale),
                               reads=[PB[bq]], writes=[B_q])
                        def f3():
                            ins = None
                            for c in range(3):
                                ins = nc.tensor.matmul(PS[0:64, bq, :], lhsT=w_uq[:, c, h * 192 + 128:h * 192 + 192], rhs=cqT[:, c, qs],
                                                       start=(c == 0), stop=(c == 2))
                            return ins
                        kb.pe(f3, reads=[B_w, B_cq], writes=[PB[bq]])
                        kb.dve(lambda: nc.vector.tensor_tensor(out=t1, in0=PS[0:64, bq, :], in1=csf[:, 0, qs], op=ALU.mult),
                               reads=[PB[bq], B_csf], writes=[B_t])
                        def f4():
                            ins = None
                            for c in range(3):
                                ins = nc.tensor.matmul(PS[0:64, bq, :], lhsT=w_uqr[:, c, h * 64:(h + 1) * 64], rhs=cqT[:, c, qs],
                                                       start=(c == 0), stop=(c == 2))
                            return ins
                        kb.pe(f4, reads=[B_w, B_cq], writes=[PB[bq]])
                        kb.dve(lambda: nc.vector.tensor_tensor(out=t2, in0=PS[0:64, bq, :], in1=csf[:, 1, qs], op=ALU.mult),
                               reads=[PB[bq], B_csf], writes=[B_t])
                        kb.dve(lambda: nc.vector.tensor_tensor(out=qrT[:, qs], in0=t1, in1=t2, op=ALU.add),
                               reads=[B_t], writes=[B_q])
                    for kbk in range(nk // 512):
                        ks = slice(kbk * 512, (kbk + 1) * 512)
                        bq = 4 + (kbk % 2)
                        def f():
                            ins = None
                            for c in range(2):
                                ins = nc.tensor.matmul(PS[:, bq, :], lhsT=w_ukv[:, c, h * 256:h * 256 + 128], rhs=ckvT[:, c, ks],
                                                       start=(c == 0), stop=(c == 1))
                            return ins
                        kb.pe(f, reads=[B_w, B_ckv], writes=[PB[bq]])
                        kb.act(lambda: nc.scalar.activation(out=KhT[:, ks], in_=PS[:, bq, :], func=AF.Copy),
                               reads=[PB[bq]], writes=[B_K])
                    for kg in range(nk // 512):
                        bq = 4 + (kg % 2)
                        def f():
                            ins = None
                            for jj in range(4):
                                kt = kg * 4 + jj
                                for c in range(2):
                                    ins = nc.tensor.matmul(PS[:, bq, jj * 128:(jj + 1) * 128], lhsT=ckvT[:, c, kt * 128:(kt + 1) * 128],
                                                           rhs=w_ukv[:, c, h * 256 + 128:h * 256 + 256], start=(c == 0), stop=(c == 1))
                            return ins
                        kb.pe(f, reads=[B_w, B_ckv], writes=[PB[bq]])
                        kb.dve(lambda: nc.vector.tensor_copy(out=Vh[:, kg * 4:(kg + 1) * 4, :],
                                                             in_=PS[:, bq, :].rearrange("p (a n) -> p a n", a=4)),
                               reads=[PB[bq]], writes=[B_V])
                    attn_head(at, nk, [(qnT, 128), (qrT, 64)], [(KhT, 128), (kropeT, 64)],
                              lambda kt: Vh[:, kt, :], OT[:, h, :], B_OT, [B_q, B_K, B_V, B_ckv])
            kb.barrier()
            es.close()
            with ExitStack() as es4:
                stage_out(es4, layer, seg, OT, B_OT, mla_w_o[j], (128, 8))
        kb.barrier()

    def mixer_gqa(layer, seg, j):
        nk = S if seg == 0 else 4 * S
        with ExitStack() as es0, ExitStack() as es:
            OT = es0.enter_context(sbt("OTg", [128, 8, S], BF16))[:]
            B_OT = Buf()
            sb = lambda name, shape, dt=BF16: es.enter_context(sbt(name, shape, dt))[:]
            qT = sb("qT", [128, 8, S])
            kT = sb("kT", [128, 2, nk])
            Vall = sb("Vall", [128, nk // 128, 256])
            B_q, B_k, B_v = Buf(), Buf(), Buf()
            with ExitStack() as es2:
                sb2 = lambda name, shape, dt=BF16: es2.enter_context(sbt(name, shape, dt))[:]
                xnTb = [sb2("xnTg%d" % i, [128, 8, 512]) for i in range(2)]
                B_xnb = [Buf(), Buf()]
                wqkv = sb2("wqkv", [128, 8, 1536])
                B_w = Buf()
                kb.dma(POOL, wqkv, gqa_w_qkv[j].rearrange("(c p) n -> p c n", p=128), writes=[B_w])
                gq, gq_b = load_gain(es2, "ggq", gqa_q_norm[j:j + 1, :], 128)
                gk, gk_b = load_gain(es2, "ggk", gqa_k_norm[j:j + 1, :], 128)
                kb.act(lambda: nc.scalar.mul(out=gq[:], in_=gq[:], mul=128.0 ** -0.5), reads=[gq_b], writes=[gq_b])
                cst = sb2("cstg", [128, NT, 128], F32)
                B_cst = Buf()
                kb.dma(SP, cst, gqa_cs_tok[seg].rearrange("(t p) n -> p t n", p=128), writes=[B_cst])
                sq = sb2("sqg", [128, 1280], F32)
                B_sq = Buf()
                stat = sb2("statg", [128, 32], F32)
                stat_b = Buf()
                qn = sb2("qng", [128, 1280], F32)
                B_qn = Buf()
                tm = sb2("tmg", [128, 4, 320], F32)
                B_tm = Buf()
                qkb = [sb2("qkb%d" % i, [128, 1280]) for i in range(2)]
                qkb_b = [Buf(), Buf()]
                for t in range(NT):
                    i = t % 2
                    ts = slice(t * 128, (t + 1) * 128)
                    b0 = 3 * i
                    xb_i = (t // 4) % 2
                    if t % 4 == 0:
                        kb.dma(SP, xnTb[xb_i], xnT_d[seg, :, :, t * 128:t * 128 + 512].rearrange("c p n -> p c n"),
                               reads=B_xnT[seg][t:t + 4], writes=[B_xnb[xb_i]])
                    xnT = xnTb[xb_i]
                    tl_ = slice((t % 4) * 128, (t % 4 + 1) * 128)
                    def f():
                        ins = None
                        for pc in range(3):
                            for c in range(8):
                                ins = nc.tensor.matmul(PS[:, b0 + pc, :], lhsT=xnT[:, c, tl_], rhs=wqkv[:, c, pc * 512:(pc + 1) * 512],
                                                       start=(c == 0), stop=(c == 7))
                        return ins
                    kb.pe(f, reads=[B_xnb[xb_i], B_w], writes=[PB[b0], PB[b0 + 1], PB[b0 + 2]])
                    kb.act(lambda: nc.scalar.activation(out=Vall[:, t, :], in_=PS[:, b0 + 2, 256:512], func=AF.Copy),
                           reads=[PB[b0 + 2]], writes=[B_v])
                    kb.act(lambda: nc.scalar.activation(out=sq[:, 0:1024], in_=ps_f(b0, 2), func=AF.Square),
                           reads=[PB[b0], PB[b0 + 1]], writes=[B_sq])
                    kb.act(lambda: nc.scalar.activation(out=sq[:, 1024:1280], in_=PS[:, b0 + 2, 0:256], func=AF.Square),
                           reads=[PB[b0 + 2]], writes=[B_sq])
                    kb.dve(lambda: nc.vector.tensor_reduce(out=stat[:, 0:10], in_=sq.rearrange("p (h d) -> p h d", d=128),
                                                           axis=mybir.AxisListType.X, op=ALU.add),
                           reads=[B_sq], writes=[stat_b])
                    kb.act(lambda: nc.scalar.activation(out=stat[:, 10:20], in_=stat[:, 0:10], func=AF.Sqrt, scale=1.0 / 128, bias=EPS),
                           reads=[stat_b], writes=[stat_b])
                    kb.dve(lambda: nc.vector.reciprocal(out=stat[:, 20:30], in_=stat[:, 10:20]), reads=[stat_b], writes=[stat_b])
                    for h in range(10):
                        src = PS[:, b0 + h // 4, (h % 4) * 128:(h % 4 + 1) * 128]
                        g_ = gq if h < 8 else gk
                        g_b = gq_b if h < 8 else gk_b
                        kb.dve(lambda: nc.vector.scalar_tensor_tensor(out=qn[:, h * 128:(h + 1) * 128], in0=src, scalar=stat[:, 20 + h:21 + h],
                                                                      in1=g_[:], op0=ALU.mult, op1=ALU.mult),
                               reads=[PB[b0 + h // 4], stat_b, g_b], writes=[B_qn])
                    q3 = qn.rearrange("p (h d) -> p h d", d=128)
                    o3 = qkb[i].rearrange("p (h d) -> p h d", d=128)
                    for part in range(2):
                        o = part * 64
                        x1 = q3[:, :, o:o + 32]
                        x2 = q3[:, :, o + 32:o + 64]
                        cc_ = cst[:, t, o:o + 32].rearrange("p (o n) -> p o n", o=1).to_broadcast([128, 10, 32])
                        ss_ = cst[:, t, o + 32:o + 64].rearrange("p (o n) -> p o n", o=1).to_broadcast([128, 10, 32])
                        tv = [tm[:, k, :].rearrange("p (h n) -> p h n", n=32) for k in range(4)]
                        kb.dve(lambda: nc.vector.tensor_tensor(out=tv[0], in0=x1, in1=cc_, op=ALU.mult), reads=[B_qn, B_cst], writes=[B_tm])
                        kb.dve(lambda: nc.vector.tensor_tensor(out=tv[1], in0=x2, in1=ss_, op=ALU.mult), reads=[B_qn, B_cst], writes=[B_tm])
                        kb.dve(lambda: nc.vector.tensor_tensor(out=tv[2], in0=x2, in1=cc_, op=ALU.mult), reads=[B_qn, B_cst], writes=[B_tm])
                        kb.dve(lambda: nc.vector.tensor_tensor(out=tv[3], in0=x1, in1=ss_, op=ALU.mult), reads=[B_qn, B_cst], writes=[B_tm])
                        kb.dve(lambda: nc.vector.tensor_tensor(out=o3[:, :, o:o + 32], in0=tv[0], in1=tv[1], op=ALU.subtract),
                               reads=[B_tm], writes=[qkb_b[i]])
                        kb.dve(lambda: nc.vector.tensor_tensor(out=o3[:, :, o + 32:o + 64], in0=tv[2], in1=tv[3], op=ALU.add),
                               reads=[B_tm], writes=[qkb_b[i]])
                    def ft():
                        ins = None
                        for c in range(8):
                            ins = nc.tensor.transpose(ps_bf(6)[:, c * 128:(c + 1) * 128], qkb[i][:, c * 128:(c + 1) * 128], ident)
                        for c in range(2):
                            ins = nc.tensor.transpose(ps_bf(7)[:, c * 128:(c + 1) * 128], qkb[i][:, (8 + c) * 128:(9 + c) * 128], ident)
                        return ins
                    kb.pe(ft, reads=[qkb_b[i], B_const], writes=[PB[6], PB[7]])
                    kb.act(lambda: nc.scalar.activation(out=qT[:, :, ts], in_=ps_bf(6).rearrange("p (c n) -> p c n", c=8), func=AF.Copy),
                           reads=[PB[6]], writes=[B_q])
                    kb.act(lambda: nc.scalar.activation(out=kT[:, :, ts], in_=ps_bf(7)[:, 0:256].rearrange("p (c n) -> p c n", c=2), func=AF.Copy),
                           reads=[PB[7]], writes=[B_k])
                if seg == 1:
                    kb.dma(POOL, cc_in_gk.rearrange("(c p) n -> p c n", p=128), kT[:, :, 0:S], reads=[B_k], writes=[B_cc["in_gk"]])
                    kb.dma(POOL, cc_in_gv.rearrange("(t p) n -> p t n", p=128), Vall[:, 0:NT, :], reads=[B_v], writes=[B_cc["in_gv"]])
                    kb.collective([cc_in_gk], [cc_out_gk], reads=[B_cc["in_gk"]], writes=[B_cc["out_gk"]], groups=GROUPS)
                    kb.collective([cc_in_gv], [cc_out_gv], reads=[B_cc["in_gv"]], writes=[B_cc["out_gv"]], groups=GROUPS)
                    for r in range(4):
                        kb.dma(SP, kT[:, :, r * S:(r + 1) * S], cc_out_gk[r * 256:(r + 1) * 256, :].rearrange("(c p) n -> p c n", p=128),
                               reads=[B_cc["out_gk"]], writes=[B_k])
                        kb.dma(SP, Vall[:, r * NT:(r + 1) * NT, :], cc_out_gv[r * S:(r + 1) * S, :].rearrange("(t p) n -> p t n", p=128),
                               reads=[B_cc["out_gv"]], writes=[B_v])
            kb.barrier()
            with ExitStack() as es3:
                at = alloc_attn_tiles(es3)
                for h in range(8):
                    kvh = h // 4
                    attn_head(at, nk, [(qT[:, h, :], 128)], [(kT[:, kvh, :], 128)],
                              lambda kt, kvh=kvh: Vall[:, kt, kvh * 128:(kvh + 1) * 128], OT[:, h, :], B_OT, [B_q, B_k, B_v])
            kb.barrier()
            es.close()
            with ExitStack() as es4:
                stage_out(es4, layer, seg, OT, B_OT, gqa_w_o[j], (128, 8))
        kb.barrier()

    NBT = 2560
    qTn_d = dscr("qTn_d", [2, 8, 128, S])
    kTn_d = dscr("kTn_d", [2, 8, 128, NBT])
    vn_d = dscr("vn_d", [2, NBT, D])
    cc_in_nak = dscr("cc_in_nak", [8 * 128, 512])
    cc_out_nak = dscr("cc_out_nak", [4 * 8 * 128, 512])
    cc_in_nav = dscr("cc_in_nav", [512, D])
    cc_out_nav = dscr("cc_out_nav", [4 * 512, D])
    B_na = {k: Buf() for k in ["q0", "q1", "k0", "k1", "v0", "v1", "ink", "outk", "inv", "outv"]}

    def na_window(lr):
        if lr < 4:
            lo, hi = lr - 4, 7
        elif lr >= 28:
            lo, hi = 24, lr + 3
        else:
            lo, hi = lr - 4, lr + 3
        blo, bhi = lo + 4, hi + 4
        ws = blo - (blo % 2)
        nrows = bhi - ws + 1
        nch = (nrows + 1) // 2
        rho0 = ws - lr + 3
        layout = 0 if rho0 % 2 == 0 else 1
        pi0 = rho0 // 2
        return ws, nch, layout, pi0

    def mixer_na(layer, seg, j):
        Bq, Bk, Bv = B_na["q%d" % seg], B_na["k%d" % seg], B_na["v%d" % seg]
        with ExitStack() as es2:
            sb2 = lambda name, shape, dt=BF16: es2.enter_context(sbt(name, shape, dt))[:]
            wq = sb2("wna", [128, 8, 3072])
            B_w = Buf()
            for k3 in range(3):
                kb.dma(POOL, wq[:, :, k3 * 1024:(k3 + 1) * 1024],
                       na_w_qkv[j, :, k3 * 1024:(k3 + 1) * 1024].rearrange("(c p) n -> p c n", p=128), writes=[B_w])
            xnTb = [sb2("xnTn%d" % i, [128, 8, 512]) for i in range(2)]
            B_xnb = [Buf(), Buf()]
            stg = [sb2("stgn%d" % i, [128, 512]) for i in range(2)]
            stg_b = [Buf(), Buf()]
            vst = [sb2("vstn%d" % i, [128, D]) for i in range(2)]
            vst_b = [Buf(), Buf()]
            zt = sb2("zt", [128, 8, 256])
            B_z = Buf()
            kb.dve(lambda: nc.vector.memset(zt, 0.0), writes=[B_z])
            if seg == 0:
                kb.dma(POOL, kTn_d[seg, :, :, 0:256].rearrange("c p n -> p c n"), zt, reads=[B_z], writes=[Bk])
                kb.dma(POOL, kTn_d[seg, :, :, NBT - 256:NBT].rearrange("c p n -> p c n"), zt, reads=[B_z], writes=[Bk])
                kb.dma(POOL, vn_d[seg, 0:256, :].rearrange("(t p) n -> p t n", p=128), zt.rearrange("p a (b n) -> p (a b) n", b=2)[:, 0:2, :].rearrange("p t n -> p t n") if False else zt[:, 0:8, :].rearrange("p c n -> p (c n)")[:, 0:2048].rearrange("p (t n) -> p t n", t=2),
                       reads=[B_z], writes=[Bv])
                kb.dma(POOL, vn_d[seg, NBT - 256:NBT, :].rearrange("(t p) n -> p t n", p=128),
                       zt[:, 0:8, :].rearrange("p c n -> p (c n)")[:, 0:2048].rearrange("p (t n) -> p t n", t=2),
                       reads=[B_z], writes=[Bv])
            k_ = 0
            for blk in range(4):
                xi = blk % 2
                kb.dma(SP, xnTb[xi], xnT_d[seg, :, :, blk * 512:(blk + 1) * 512].rearrange("c p n -> p c n"),
                       reads=B_xnT[seg][blk * 4:blk * 4 + 4], writes=[B_xnb[xi]])
                xnT = xnTb[xi]
                for m in range(16):
                    i = k_ % 2
                    k_ += 1
                    bq = 4 + i
                    def f():
                        ins = None
                        for c in range(8):
                            ins = nc.tensor.matmul(PS[:, bq, :], lhsT=wq[:, c, m * 128:(m + 1) * 128], rhs=xnT[:, c, :],
                                                   start=(c == 0), stop=(c == 7))
                        return ins
                    kb.pe(f, reads=[B_w, B_xnb[xi]], writes=[PB[bq]])
                    sc_ = 0.125 if m < 8 else 1.0
                    kb.act(lambda: nc.scalar.activation(out=stg[i], in_=PS[:, bq, :], func=AF.Copy, scale=sc_),
                           reads=[PB[bq]], writes=[stg_b[i]])
                    if m < 8:
                        kb.dma(POOL, qTn_d[seg, m, :, blk * 512:(blk + 1) * 512], stg[i], reads=[stg_b[i]], writes=[Bq])
                    else:
                        kb.dma(POOL, kTn_d[seg, m - 8, :, 256 + blk * 512:256 + (blk + 1) * 512], stg[i], reads=[stg_b[i]], writes=[Bk])
                for tt in range(4):
                    t = blk * 4 + tt
                    i = t % 2
                    b0 = 6 if False else (0 + 2 * i)
                    def f():
                        ins = None
                        for half in range(2):
                            for c in range(8):
                                ins = nc.tensor.matmul(PS[:, b0 + half, :], lhsT=xnT[:, c, tt * 128:(tt + 1) * 128],
                                                       rhs=wq[:, c, 2048 + half * 512:2048 + (half + 1) * 512], start=(c == 0), stop=(c == 7))
                        return ins
                    kb.pe(f, reads=[B_w, B_xnb[xi]], writes=[PB[b0], PB[b0 + 1]])
                    kb.act(lambda: nc.scalar.activation(out=vst[i], in_=ps_f(b0, 2), func=AF.Copy), reads=[PB[b0], PB[b0 + 1]], writes=[vst_b[i]])
                    kb.dma(POOL, vn_d[seg, 256 + t * 128:256 + (t + 1) * 128, :], vst[i], reads=[vst_b[i]], writes=[Bv])
            if seg == 1:
                kb.dma(POOL, cc_in_nak[:, 0:256].rearrange("(c p) n -> c p n", p=128), kTn_d[seg, :, :, 256:512], reads=[Bk], writes=[B_na["ink"]])
                kb.dma(POOL, cc_in_nak[:, 256:512].rearrange("(c p) n -> c p n", p=128), kTn_d[seg, :, :, NBT - 512:NBT - 256], reads=[Bk], writes=[B_na["ink"]])
                kb.dma(POOL, cc_in_nav[0:256, :], vn_d[seg, 256:512, :], reads=[Bv], writes=[B_na["inv"]])
                kb.dma(POOL, cc_in_nav[256:512, :], vn_d[seg, NBT - 512:NBT - 256, :], reads=[Bv], writes=[B_na["inv"]])
                kb.collective([cc_in_nak], [cc_out_nak], reads=[B_na["ink"]], writes=[B_na["outk"]], groups=GROUPS)
                kb.collective([cc_in_nav], [cc_out_nav], reads=[B_na["inv"]], writes=[B_na["outv"]], groups=GROUPS)
                hsel = sb2("hsel", [128, 8], F32)
                B_hs = Buf()
                kb.dma(SP, hsel, na_hsel, writes=[B_hs])
                candk = sb2("candk", [128, 4, 8, 512])
                candv = sb2("candv", [128, 4, 4, D])
                B_ck, B_cv = Buf(), Buf()
                for r in range(4):
                    kb.dma(SP, candk[:, r], cc_out_nak[r * 1024:(r + 1) * 1024, :].rearrange("(c p) n -> p c n", p=128),
                           reads=[B_na["outk"]], writes=[B_ck])
                    kb.dma(SP, candv[:, r], cc_out_nav[r * 512:(r + 1) * 512, :].rearrange("(t p) n -> p t n", p=128),
                           reads=[B_na["outv"]], writes=[B_cv])
                hk = sb2("hk", [128, 2, 8, 256])
                hv = sb2("hv", [128, 2, 2, D])
                B_hk, B_hv = Buf(), Buf()
                for side in range(2):
                    ksl = slice(256, 512) if side == 0 else slice(0, 256)
                    vsl = slice(2, 4) if side == 0 else slice(0, 2)
                    for r in range(4):
                        msk = hsel[:, side * 4 + r:side * 4 + r + 1]
                        if r == 0:
                            kb.dve(lambda: nc.vector.tensor_scalar(out=hk[:, side], in0=candk[:, r, :, ksl], scalar1=msk, scalar2=None, op0=ALU.mult),
                                   reads=[B_ck, B_hs], writes=[B_hk])
                            kb.dve(lambda: nc.vector.tensor_scalar(out=hv[:, side], in0=candv[:, r, vsl, :], scalar1=msk, scalar2=None, op0=ALU.mult),
                                   reads=[B_cv, B_hs], writes=[B_hv])
                        else:
                            kb.dve(lambda: nc.vector.scalar_tensor_tensor(out=hk[:, side], in0=candk[:, r, :, ksl], scalar=msk, in1=hk[:, side],
                                                                          op0=ALU.mult, op1=ALU.add), reads=[B_ck, B_hs, B_hk], writes=[B_hk])
                            kb.dve(lambda: nc.vector.scalar_tensor_tensor(out=hv[:, side], in0=candv[:, r, vsl, :], scalar=msk, in1=hv[:, side],
                                                                          op0=ALU.mult, op1=ALU.add), reads=[B_cv, B_hs, B_hv], writes=[B_hv])
                kb.dma(POOL, kTn_d[seg, :, :, 0:256].rearrange("c p n -> p c n"), hk[:, 0], reads=[B_hk], writes=[Bk])
                kb.dma(POOL, kTn_d[seg, :, :, NBT - 256:NBT].rearrange("c p n -> p c n"), hk[:, 1], reads=[B_hk], writes=[Bk])
                kb.dma(POOL, vn_d[seg, 0:256, :].rearrange("(t p) n -> p t n", p=128), hv[:, 0], reads=[B_hv], writes=[Bv])
                kb.dma(POOL, vn_d[seg, NBT - 256:NBT, :].rearrange("(t p) n -> p t n", p=128), hv[:, 1], reads=[B_hv], writes=[Bv])
        kb.barrier()
        with ExitStack() as es0:
            Otok = es0.enter_context(sbt("Otok", [128, NT, D], BF16))[:]
            B_Ot = Buf()
            with ExitStack() as es:
                sb = lambda name, shape, dt=BF16: es.enter_context(sbt(name, shape, dt))[:]
                rmask = sb("rmask", [128, 32, 6], F32)
                B_rm = Buf()
                kb.dma(SP, rmask, na_rmask[seg], writes=[B_rm])
                qg = [sb("qg%d" % i, [64, 4, S]) for i in range(2)]
                kg = [sb("kg%d" % i, [64, 4, NBT]) for i in range(2)]
                vg = [sb("vg%d" % i, [128, 20, 4, 72]) for i in range(2)]
                tabs = [[sb("tab%d_%d" % (i, l), [128, 8, 4, 64], F32) for l in range(2)] for i in range(2)]
                B_g = [Buf(), Buf()]
                for i in range(2):
                    kb.dve(lambda: nc.vector.memset(vg[i][:, :, :, 64:65], 1.0), writes=[B_g[i]])
                scb = [sb("scb%d" % i, [128, 6, 256], F32) for i in range(2)]
                scb_b = [Buf(), Buf()]
                PTn = [sb("PTn%d" % i, [128, 6, 4, 64]) for i in range(2)]
                PTn_b = [Buf(), Buf()]
                rec = sb("recn", [64, 8], F32)
                rec_b = Buf()
                vstg = sb("vstg", [128, 20, 256])
                B_vs = Buf()
                if NA_DBG == 'A':
                    kb.dve(lambda: nc.vector.memset(Otok, 0.0), writes=[B_Ot])
                for G in range(4 if NA_DBG != 'A' else 0):
                    gi = G % 2
                    kb.dma(SP, qg[gi], qTn_d[seg].rearrange("c p n -> (c p) n")[4 * G * 64:(4 * G + 4) * 64, :].rearrange("(h p) n -> p h n", p=64),
                           reads=[Bq], writes=[B_g[gi]])
                    kb.dma(SP, kg[gi], kTn_d[seg].rearrange("c p n -> (c p) n")[4 * G * 64:(4 * G + 4) * 64, :].rearrange("(h p) n -> p h n", p=64),
                           reads=[Bk], writes=[B_g[gi]])
                    for q5 in range(4):
                        kb.dma(SP, vstg[:, q5 * 5:(q5 + 1) * 5, :],
                               vn_d[seg, q5 * 640:(q5 + 1) * 640, 4 * G * 64:(4 * G + 4) * 64].rearrange("(c p) n -> p c n", p=128),
                               reads=[Bv], writes=[B_vs])
                    kb.dve(lambda: nc.vector.tensor_copy(out=vg[gi][:, :, :, 0:64], in_=vstg.rearrange("p c (h d) -> p c h d", d=64)),
                           reads=[B_vs], writes=[B_g[gi]])
                    for l in range(2 if NA_DBG != 'L2' else 0):
                        kb.dma(SP, tabs[gi][l].rearrange("p a h c -> p (a h c)"), na_tab[l, G], writes=[B_g[gi]])
                    if NA_DBG in ('L', 'L2', 'S', 'S0', 'S1'):
                        kb.dve(lambda: nc.vector.memset(Otok, 0.0), writes=[B_Ot])
                    for lr in range(32 if NA_DBG not in ('L', 'L2') else 0):
                        ws, nch, layout, pi0 = na_window(lr)
                        i = lr % 2
                        sb0 = 3 * i
                        ob = 6 + i
                        scv = PS[:, sb0:sb0 + 3, :].rearrange("p b n -> p (b n)")
                        def f():
                            ins = None
                            for ci in range(nch):
                                for hh in range(4):
                                    col = (ci * 4 + hh) * 64
                                    ins = nc.tensor.matmul(scv[:, col:col + 64],
                                                           lhsT=kg[gi][0:64, hh, (ws + 2 * ci) * 64:(ws + 2 * ci) * 64 + 128],
                                                           rhs=qg[gi][0:64, hh, lr * 64:(lr + 1) * 64], start=True, stop=True)
                            return ins
                        kb.pe(f, reads=[B_g[gi]], writes=[PB[sb0], PB[sb0 + 1], PB[sb0 + 2]])
                        n_el = nch * 256
                        if NA_DBG == 'S0':
                            continue
                        kb.dve(lambda: nc.vector.tensor_tensor(out=scb[i].rearrange("p a n -> p (a n)")[:, 0:n_el], in0=scv[:, 0:n_el],
                                                               in1=tabs[gi][layout][:, pi0:pi0 + nch].rearrange("p a h c -> p (a h c)"),
                                                               op=ALU.add),
                               reads=[PB[sb0], PB[sb0 + 1], PB[sb0 + 2], B_g[gi]], writes=[scb_b[i]])
                        if NA_DBG == 'S1':
                            continue
                        for ci in range(nch):
                            kb.act(lambda: nc.scalar.activation(out=PTn[i][:, ci].rearrange("p h c -> p (h c)"), in_=scb[i][:, ci, :],
                                                                func=AF.Exp, bias=rmask[:, lr, ci:ci + 1]),
                                   reads=[scb_b[i], B_rm], writes=[PTn_b[i]])
                        if NA_DBG == 'S':
                            continue
                        ov = PS[0:64, ob, 0:260].rearrange("p (h n) -> p h n", n=65)
                        def f2():
                            ins = None
                            for hh in range(4):
                                for ci in range(nch):
                                    ins = nc.tensor.matmul(ov[:, hh, :], lhsT=PTn[i][:, ci, hh, :], rhs=vg[gi][:, ws // 2 + ci, hh, 0:65],
                                                           start=(ci == 0), stop=(ci == nch - 1))
                            return ins
                        kb.pe(f2, reads=[PTn_b[i], B_g[gi]], writes=[PB[ob]])
                        kb.dve(lambda: nc.vector.reciprocal(out=rec[:, 4 * i:4 * i + 4], in_=ov[:, :, 64]), reads=[PB[ob]], writes=[rec_b])
                        pr = (lr % 2) * 64
                        for hh in range(4):
                            kb.dve(lambda: nc.vector.tensor_scalar(out=Otok[pr:pr + 64, lr // 2, (4 * G + hh) * 64:(4 * G + hh + 1) * 64],
                                                                   in0=ov[:, hh, 0:64], scalar1=rec[:, 4 * i + hh:4 * i + hh + 1],
                                                                   scalar2=None, op0=ALU.mult),
                                   reads=[PB[ob], rec_b], writes=[B_Ot])
            kb.barrier()
            if NA_DBG == 'O':
                with ExitStack() as esd:
                    tcp = esd.enter_context(sbt("tcpo", [128, D], F32))[:]
                    bcp = Buf()
                    for t in range(NT):
                        kb.act(lambda: nc.scalar.activation(out=tcp, in_=Otok[:, t, :], func=AF.Copy), reads=[B_Ot], writes=[bcp])
                        kb.dma(SP, y_out[seg, t * 128:(t + 1) * 128, :], tcp, reads=[bcp], writes=[Buf()])
                kb.barrier()
            with ExitStack() as es4:
                OT = es4.enter_context(sbt("OTn", [128, 8, S], BF16))[:]
                B_OT = Buf()
                for t in range(NT):
                    pbank = t % 2
                    pv = transpose_to(Otok[:, t, :], B_Ot, 8, pbank, None, None)
                    kb.act(lambda: nc.scalar.activation(out=OT[:, :, t * 128:(t + 1) * 128], in_=pv.rearrange("p (c n) -> p c n", c=8),
                                                        func=AF.Copy), reads=[PB[pbank]], writes=[B_OT])
                kb.barrier()
                stage_out(es4, layer, seg, OT, B_OT, na_w_o[j], (128, 8))
        kb.barrier()

    def weights_to_bf16(layer):
        for r in range(8):
            kb.dma(POOL, w_in_bf[r * 128:(r + 1) * 128, :], ffn_w_in[layer, r * 128:(r + 1) * 128, :],
                   reads=[], writes=[B_win])
        for r in range(8):
            kb.dma(POOL, w_out_bf[r * 512:(r + 1) * 512, :], ffn_w_out[layer, r * 512:(r + 1) * 512, :],
                   reads=[], writes=[B_wout])

    def stage_ffn(layer, seg):
        TB = 512
        NB = S // TB
        with ExitStack() as es:
            sb = lambda name, shape, dt=BF16: es.enter_context(sbt(name, shape, dt))[:]
            halo = sb("halo", [128, 8, 2])
            B_halo = Buf()
            if seg == 0:
                kb.dve(lambda: nc.vector.memset(halo, 0.0), writes=[B_halo])
            else:
                cand = sb("cand", [8, D])
                selm = sb("selm", [8, 2])
                B_cand = Buf()
                kb.dma(SP, cand, cc_out_h, reads=[B_cc["out_h"]], writes=[B_cand])
                kb.dma(POOL, selm, halo_sel, writes=[B_cand])
                def f():
                    ins = None
                    for c in range(8):
                        ins = nc.tensor.matmul(PS[:, 0, 2 * c:2 * c + 2], lhsT=cand[:, c * 128:(c + 1) * 128], rhs=selm,
                                               start=True, stop=True)
                    return ins
                kb.pe(f, reads=[B_cand], writes=[PB[0]])
                kb.dve(lambda: nc.vector.tensor_copy(out=halo, in_=PS[:, 0, 0:16].rearrange("p (c n) -> p c n", n=2)),
                       reads=[PB[0]], writes=[B_halo])
            cw = sb("cw", [128, 3, 32], F32)
            cbias = sb("cbias", [128, 32], F32)
            B_cw = Buf()
            kb.dma(SP, cw, ffn_conv_w[layer], writes=[B_cw])
            kb.dma(SP, cbias, ffn_conv_b[layer], writes=[B_cw])
            w_gate = sb("w_gate", [128, 8, D])
            w_proj = sb("w_proj", [128, 2, D])
            B_wg = Buf()
            kb.dma(POOL, w_gate, ple_w_gate[layer].rearrange("(c p) n -> p c n", p=128), writes=[B_wg])
            kb.dma(POOL, w_proj, ple_w_proj[layer].rearrange("(c p) n -> p c n", p=128), writes=[B_wg])
            gfpost, gfpost_b = load_gain(es, "gfpost", g_ffn_post[layer:layer + 1, :])
            gple, gple_b = load_gain(es, "gple", g_ple[layer:layer + 1, :])
            if layer + 1 < nlayers:
                gnext, gnext_b = load_gain(es, "gnext", g_mix_pre[layer + 1:layer + 2, :])
            xfb = [sb("xfb%d" % i, [128, 8, TB + 2]) for i in range(2)]
            xfb_b = [Buf(), Buf()]
            hT = sb("hT", [128, 32, TB])
            B_hT = Buf()
            win = [sb("win%d" % i, [128, 8, 256]) for i in range(3)]
            win_b = [Buf() for _ in range(3)]
            wout = [sb("wout%d" % i, [128, 32, 256]) for i in range(2)]
            wout_b = [Buf(), Buf()]
            gs = [sb("gs%d" % i, [128, TB + 2], F32) for i in range(2)]
            gs_b = [Buf(), Buf()]
            ta = [sb("ta%d" % i, [128, TB], F32) for i in range(2)]
            ta_b = [Buf(), Buf()]
            tb_ = [sb("tb%d" % i, [128, TB], F32) for i in range(2)]
            tb_b = [Buf(), Buf()]
            yblk = [sb("yblk%d" % i, [128, D], F32) for i in range(4)]
            yblk_b = [Buf() for _ in range(4)]
            ESET = []
            for e_ in range(2):
                ESET.append(dict(tl=alloc_norm_tiles(es, "f%d" % e_), xs=sb("xf%d" % e_, [128, D], F32), xs_b=Buf(),
                                 x2=sb("x2f%d" % e_, [128, D], F32), x2_b=Buf(), pt=sb("pt%d" % e_, [128, PLE]), pt_b=Buf(),
                                 pT=sb("pT%d" % e_, [128, 2, 128]), pT_b=Buf(), gate=sb("gate%d" % e_, [128, D], F32), gate_b=Buf(),
                                 st=sb("st3%d" % e_, [128, 32], F32), st_b=Buf()))
            def rms_steps(src_ap, src_bufs, n, scr, scr_b, col, st3, st3_b):
                kb.act(lambda: nc.scalar.activation(out=scr[:, 0:n], in_=src_ap, func=AF.Square, accum_out=st3[:, col:col + 1]),
                       reads=src_bufs, writes=[st3_b, scr_b])
                yield
                kb.act(lambda: nc.scalar.activation(out=st3[:, col + 1:col + 2], in_=st3[:, col:col + 1], func=AF.Sqrt, scale=1.0 / n, bias=EPS),
                       reads=[st3_b], writes=[st3_b])
                yield
                kb.dve(lambda: nc.vector.reciprocal(out=st3[:, col + 2:col + 3], in_=st3[:, col + 1:col + 2]), reads=[st3_b], writes=[st3_b])
                yield

            def epi(blk, tt):
                t = blk * (TB // 128) + tt
                ts = slice(t * 128, (t + 1) * 128)
                E_ = ESET[tt % 2]
                xs_, xs_b_, x2_, x2_b_, pt_, pt_b_, pT_, pT_b_, gate_, gate_b_ = (E_["xs"], E_["xs_b"], E_["x2"], E_["x2_b"], E_["pt"], E_["pt_b"],
                                                                              E_["pT"], E_["pT_b"], E_["gate"], E_["gate_b"])
                tl = E_["tl"]
                st3, st3_b = E_["st"], E_["st_b"]
                BK = 4 * (tt % 2)
                scr, scr_b = tl[0], tl[1]
                xn, xn_b = tl[4], tl[5]
                xT, xT_b = tl[6], tl[7]
                y = yblk[tt]
                kb.dma(SP, xs_, xres[seg, ts, :], reads=[B_xres[seg][t]], writes=[xs_b_])
                yield
                kb.dma(POOL, pt_, p_in[layer, seg, ts, :], writes=[pt_b_])
                yield
                yield from rms_steps(y, [yblk_b[tt]], D, scr, scr_b, 0, st3, st3_b)
                kb.dve(lambda: nc.vector.scalar_tensor_tensor(out=scr, in0=y, scalar=st3[:, 2:3], in1=gfpost[:], op0=ALU.mult, op1=ALU.mult),
                       reads=[yblk_b[tt], st3_b, gfpost_b], writes=[scr_b])
                yield
                kb.dve(lambda: nc.vector.tensor_tensor(out=x2_, in0=scr, in1=xs_, op=ALU.add), reads=[scr_b, xs_b_], writes=[x2_b_])
                yield
                yield from rms_steps(x2_, [x2_b_], D, scr, scr_b, 4, st3, st3_b)
                kb.dve(lambda: nc.vector.tensor_scalar(out=xn, in0=x2_, scalar1=st3[:, 6:7], scalar2=None, op0=ALU.mult),
                       reads=[x2_b_, st3_b], writes=[xn_b])
                yield
                yield 'pe'
                pv = transpose_to(xn, xn_b, 8, BK, None, None)
                yield
                kb.act(lambda: nc.scalar.activation(out=xT, in_=pv, func=AF.Copy), reads=[PB[BK]], writes=[xT_b])
                yield
                def fp():
                    ins = None
                    for c in range(2):
                        ins = nc.tensor.transpose(ps_bf(BK + 1)[:, c * 128:(c + 1) * 128], pt_[:, c * 128:(c + 1) * 128], ident)
                    return ins
                kb.pe(fp, reads=[pt_b_, B_const], writes=[PB[BK + 1]])
                yield
                kb.act(lambda: nc.scalar.activation(out=pT_, in_=ps_bf(BK + 1)[:, 0:256].rearrange("p (c n) -> p c n", c=2), func=AF.Copy),
                       reads=[PB[BK + 1]], writes=[pT_b_])
                yield
                for half in range(2):
                    bkx = BK + 2 + half
                    hs_ = slice(half * 512, (half + 1) * 512)
                    def fg():
                        ins = None
                        for c in range(8):
                            ins = nc.tensor.matmul(PS[:, bkx, :], lhsT=xT[:, c * 128:(c + 1) * 128], rhs=w_gate[:, c, hs_],
                                                   start=(c == 0), stop=(c == 7))
                        return ins
                    if half == 0:
                        yield 'pe'
                    kb.pe(fg, reads=[xT_b, B_wg], writes=[PB[bkx]])
                    yield
                    kb.act(lambda: nc.scalar.activation(out=gate_[:, hs_], in_=PS[:, bkx, :], func=AF.Sigmoid),
                           reads=[PB[bkx]], writes=[gate_b_])
                    yield
                for half in range(2):
                    bkx = BK + 2 + half
                    hs_ = slice(half * 512, (half + 1) * 512)
                    def fe():
                        ins = None
                        for c in range(2):
                            ins = nc.tensor.matmul(PS[:, bkx, :], lhsT=pT_[:, c, :], rhs=w_proj[:, c, hs_], start=(c == 0), stop=(c == 1))
                        return ins
                    if half == 0:
                        yield 'pe'
                    kb.pe(fe, reads=[pT_b_, B_wg], writes=[PB[bkx]])
                    yield
                    kb.dve(lambda: nc.vector.tensor_tensor(out=gate_[:, hs_], in0=gate_[:, hs_], in1=PS[:, bkx, :], op=ALU.mult),
                           reads=[PB[bkx], gate_b_], writes=[gate_b_])
                    yield
                yield from rms_steps(gate_, [gate_b_], D, scr, scr_b, 8, st3, st3_b)
                kb.dve(lambda: nc.vector.scalar_tensor_tensor(out=scr, in0=gate_, scalar=st3[:, 10:11], in1=gple[:], op0=ALU.mult, op1=ALU.mult),
                       reads=[gate_b_, st3_b, gple_b], writes=[scr_b])
                yield
                kb.dve(lambda: nc.vector.tensor_tensor(out=xs_, in0=scr, in1=x2_, op=ALU.add), reads=[scr_b, x2_b_], writes=[xs_b_])
                yield
                if layer + 1 < nlayers:
                    kb.dma(POOL, xres[seg, ts, :], xs_, reads=[xs_b_], writes=[B_xres[seg][t]])
                    yield
                    yield from rms_steps(xs_, [xs_b_], D, scr, scr_b, 12, st3, st3_b)
                    kb.dve(lambda: nc.vector.scalar_tensor_tensor(out=xn, in0=xs_, scalar=st3[:, 14:15], in1=gnext[:], op0=ALU.mult, op1=ALU.mult),
                           reads=[xs_b_, st3_b, gnext_b], writes=[xn_b])
                    yield
                    yield 'pe'
                    pv2 = transpose_to(xn, xn_b, 8, BK, None, None)
                    yield
                    kb.act(lambda: nc.scalar.activation(out=xT, in_=pv2, func=AF.Copy), reads=[PB[BK]], writes=[xT_b])
                    yield
                    kb.dma(POOL, xnT_d[seg, :, :, t * 128:(t + 1) * 128].rearrange("c p n -> p c n"),
                           xT.rearrange("p (c n) -> p c n", c=8), reads=[xT_b], writes=[B_xnT[seg][t]])
                    yield
                else:
                    kb.dma(POOL, y_out[seg, ts, :], xs_, reads=[xs_b_], writes=[B_xres[seg][t]])
                    yield

            pending = iter(())
            wk = 0
            wo_k = 0
            for blk in range(NB):
                c0 = blk * TB
                xi = blk % 2
                xfT = xfb[xi]
                B_xf = xfb_b[xi]
                lo = c0 - 1 if blk > 0 else c0
                hi = c0 + TB + 1 if blk < NB - 1 else c0 + TB
                kb.dma(SP, xfT[:, :, (lo - (c0 - 1)):(hi - (c0 - 1))], xfT_d[seg, :, :, lo:hi].rearrange("c p n -> p c n"),
                       reads=B_xfT[seg], writes=[B_xf])
                if blk == 0:
                    kb.dve(lambda: nc.vector.tensor_copy(out=xfT[:, :, 0:1], in_=halo[:, :, 0:1]), reads=[B_halo], writes=[B_xf])
                if blk == NB - 1:
                    kb.dve(lambda: nc.vector.tensor_copy(out=xfT[:, :, TB + 1:TB + 2], in_=halo[:, :, 1:2]), reads=[B_halo], writes=[B_xf])
                for ch in range(32):
                    wi = wk % 3
                    wk += 1
                    kb.dma(SP, win[wi][:, :, 0:128], w_in_bf[:, ch * 128:(ch + 1) * 128].rearrange("(c p) n -> p c n", p=128),
                           reads=[B_win], writes=[win_b[wi]])
                    kb.dma(SP, win[wi][:, :, 128:256],
                           w_in_bf[:, DFF + ch * 128:DFF + (ch + 1) * 128].rearrange("(c p) n -> p c n", p=128),
                           reads=[B_win], writes=[win_b[wi]])
                    i = ch % 2
                    bg, bh, bu = 3 * i, 3 * i + 1, 3 * i + 2
                    def f():
                        ins = None
                        for c in range(8):
                            nc.tensor.matmul(PS[:, bg, :], lhsT=win[wi][:, c, 0:128], rhs=xfT[:, c, 0:TB],
                                             start=(c == 0), stop=(c == 7))
                        for c in range(8):
                            nc.tensor.matmul(PS[:, bh, 0:2], lhsT=win[wi][:, c, 0:128], rhs=xfT[:, c, TB:TB + 2],
                                             start=(c == 0), stop=(c == 7))
                        for c in range(8):
                            ins = nc.tensor.matmul(PS[:, bu, :], lhsT=win[wi][:, c, 128:256], rhs=xfT[:, c, 1:1 + TB],
                                                   start=(c == 0), stop=(c == 7))
                        return ins
                    kb.pe(f, reads=[win_b[wi], B_xf], writes=[PB[bg], PB[bh], PB[bu]])
                    kb.act(lambda: nc.scalar.activation(out=gs[i][:, 0:TB], in_=PS[:, bg, :], func=AF.Copy),
                           reads=[PB[bg]], writes=[gs_b[i]])
                    kb.act(lambda: nc.scalar.activation(out=gs[i][:, TB:TB + 2], in_=PS[:, bh, 0:2], func=AF.Copy),
                           reads=[PB[bh]], writes=[gs_b[i]])
                    kb.dve(lambda: nc.vector.tensor_scalar(out=ta[i], in0=gs[i][:, 1:TB + 1], scalar1=cw[:, 1, ch:ch + 1],
                                                           scalar2=cbias[:, ch:ch + 1], op0=ALU.mult, op1=ALU.add),
                           reads=[gs_b[i], B_cw], writes=[ta_b[i]])
                    kb.dve(lambda: nc.vector.scalar_tensor_tensor(out=tb_[i], in0=gs[i][:, 0:TB], scalar=cw[:, 0, ch:ch + 1],
                                                                  in1=ta[i], op0=ALU.mult, op1=ALU.add),
                           reads=[gs_b[i], B_cw, ta_b[i]], writes=[tb_b[i]])
                    kb.dve(lambda: nc.vector.scalar_tensor_tensor(out=ta[i], in0=gs[i][:, 2:TB + 2], scalar=cw[:, 2, ch:ch + 1],
                                                                  in1=tb_[i], op0=ALU.mult, op1=ALU.add),
                           reads=[gs_b[i], B_cw, tb_b[i]], writes=[ta_b[i]])
                    kb.act(lambda: nc.scalar.activation(out=tb_[i], in_=ta[i], func=AF.Gelu_apprx_tanh),
                           reads=[ta_b[i]], writes=[tb_b[i]])
                    kb.dve(lambda: nc.vector.tensor_tensor(out=hT[:, ch, :], in0=tb_[i], in1=PS[:, bu, :], op=ALU.mult),
                           reads=[tb_b[i], PB[bu]], writes=[B_hT])
                for q4 in range(4):
                    wo = wo_k % 2
                    wo_k += 1
                    kb.dma(SP, wout[wo], w_out_bf[:, q4 * 256:(q4 + 1) * 256].rearrange("(c p) n -> p c n", p=128),
                           reads=[B_wout], writes=[wout_b[wo]])
                    for tt in range(TB // 128):
                        by = 6 + (tt % 2)
                        def f():
                            ins = None
                            for c in range(32):
                                ins = nc.tensor.matmul(PS[:, by, 0:256], lhsT=hT[:, c, tt * 128:(tt + 1) * 128], rhs=wout[wo][:, c, :],
                                                       start=(c == 0), stop=(c == 31))
                            return ins
                        kb.pe(f, reads=[B_hT, wout_b[wo]], writes=[PB[by]])
                        kb.act(lambda: nc.scalar.activation(out=yblk[tt][:, q4 * 256:(q4 + 1) * 256], in_=PS[:, by, 0:256], func=AF.Copy),
                               reads=[PB[by]], writes=[yblk_b[tt]])
                for pair in range(TB // 256):
                    gens = [epi(blk, 2 * pair), epi(blk, 2 * pair + 1)]
                    alive = [True, True]
                    while any(alive):
                        for gi_ in range(2):
                            if alive[gi_]:
                                v_ = next(gens[gi_], 'end')
                                while v_ == 'pe':
                                    v_ = next(gens[gi_], 'end')
                                if v_ == 'end':
                                    alive[gi_] = False
        kb.barrier()

    stage_initial()
    for layer in range(nlayers):
        kind, j = layer % 3, layer // 3
        for seg in (1, 0):
            if kind == 0:
                mixer_mla(layer, seg, j)
            elif kind == 1:
                mixer_gqa(layer, seg, j)
            else:
                mixer_na(layer, seg, j)
            if seg == 1:
                weights_to_bf16(layer)
        if STOP_MIX and layer == nlayers - 1 and NA_DBG == 'O':
            break
        if STOP_MIX and layer == nlayers - 1:
            with ExitStack() as esd:
                tcp = esd.enter_context(sbt("tcp", [128, D], F32))[:]
                bcp = Buf()
                for seg in (0, 1):
                    for t in range(NT):
                        kb.dma(SP, tcp, xres[seg, t * 128:(t + 1) * 128, :], reads=[B_xres[seg][t]], writes=[bcp])
                        kb.dma(SP, y_out[seg, t * 128:(t + 1) * 128, :], tcp, reads=[bcp], writes=[B_xres[seg][t]])
            break
        for seg in (0, 1):
            stage_ffn(layer, seg)
    kb.barrier()
    return nc


def _na_window(lr):
    if lr < 4:
        lo, hi = lr - 4, 7
    elif lr >= 28:
        lo, hi = 24, lr + 3
    else:
        lo, hi = lr - 4, lr + 3
    blo, bhi = lo + 4, hi + 4
    ws = blo - (blo % 2)
    nrows = bhi - ws + 1
    nch = (nrows + 1) // 2
    rho0 = ws - lr + 3
    layout = 0 if rho0 % 2 == 0 else 1
    pi0 = rho0 // 2
    return ws, nch, layout, pi0


def _rope_tables(pos, dim):
    inv = 1.0 / (10000.0 ** (np.arange(0, dim, 2, dtype=np.float32) / np.float32(dim)))
    ang = pos.astype(np.float32)[:, None] * inv[None, :].astype(np.float32)
    return np.cos(ang).astype(np.float32), np.sin(ang).astype(np.float32)


_NC_CACHE = {}


def kernel(_nlayers=DEPTH, **inputs):
    inp = {k: np.ascontiguousarray(np.asarray(v)) for k, v in inputs.items()}
    if _nlayers not in _NC_CACHE:
        _NC_CACHE[_nlayers] = build_program(_nlayers)
    nc = _NC_CACHE[_nlayers]
    f32 = np.float32
    w_uq = inp["mla_w_uq"]
    rp = np.empty((2, 384, 512), f32)
    for h in range(8):
        base = h * 192 + 128
        idx = base + (np.arange(64) + 32) % 64
        rp[:, :, h * 64:(h + 1) * 64] = w_uq[:, :, idx]
    shared = {k: inp[k] for k in ["norm_mix_pre", "norm_mix_post", "norm_ffn_pre", "norm_ffn_post", "ple_norm",
                                  "mla_w_down", "mla_q_norm", "mla_kv_norm", "mla_w_uq", "mla_w_ukv", "mla_w_o",
                                  "gqa_w_qkv", "gqa_q_norm", "gqa_k_norm", "gqa_w_o", "na_w_qkv", "na_w_o",
                                  "ffn_w_in", "ffn_w_out", "ple_w_proj", "ple_w_gate"]}
    shared["ffn_conv_wT"] = np.ascontiguousarray(inp["ffn_conv_w"].reshape(DEPTH, 3, 32, 128).transpose(0, 3, 1, 2))
    shared["ffn_conv_bT"] = np.ascontiguousarray(inp["ffn_conv_b"].reshape(DEPTH, 32, 128).transpose(0, 2, 1))
    shared["mla_w_uq_rp"] = rp
    shared["ident"] = np.eye(128, dtype=f32)
    rpb = inp["na_rpb"][0]
    cq = np.arange(64)
    c0 = np.clip(cq - 8, 0, 48)
    kcs = np.arange(64)
    valid = (kcs[:, None] >= c0[None, :]) & (kcs[:, None] < c0[None, :] + 16)
    relc = np.clip(kcs[:, None] - cq[None, :] + 15, 0, 30)
    T2 = np.full((16, 17, 64, 64), NEG, f32)
    for rho in range(15):
        T2[:, rho] = np.where(valid[None], rpb[:, rho][:, relc], f32(NEG))
    na_tab = np.empty((2, 128, 8, 16, 64), f32)
    for layout in range(2):
        for k in range(8):
            for jj in range(2):
                rho = 2 * k + jj + layout
                na_tab[layout, jj * 64:(jj + 1) * 64, k] = T2[:, rho].transpose(1, 0, 2)
    shared["na_tab"] = np.ascontiguousarray(
        na_tab.reshape(2, 128, 8, 4, 4, 64).transpose(0, 3, 1, 2, 4, 5).reshape(2, 4, 128, 8 * 4 * 64))
    in_maps = []
    scale = f32(192.0 ** -0.5)
    for core in range(8):
        grp, qtr = core // 4, core % 4
        m = dict(shared)
        m["x_in"] = np.stack([inp["x_prompt"][core], inp["x_sample"][grp, qtr * S:(qtr + 1) * S]], 0)
        m["p_in"] = np.stack([inp["p_prompt"][:, core], inp["p_sample"][:, grp, qtr * S:(qtr + 1) * S]], 1)
        cs_tok = np.zeros((2, S, 64), f32)
        cs_feat = np.zeros((2, 2, 64, S), f32)
        gq_tok = np.zeros((2, S, 128), f32)
        for seg in range(2):
            pos = np.arange(S) + (0 if seg == 0 else qtr * S)
            c, s = _rope_tables(pos, 64)
            cs_tok[seg, :, 0:32] = c
            cs_tok[seg, :, 32:64] = s
            cs_feat[seg, 0] = np.concatenate([c, c], 1).T * scale
            cs_feat[seg, 1] = np.concatenate([-s, s], 1).T * scale
            rc, rs = _rope_tables(pos // 64, 64)
            cc, cs_ = _rope_tables(pos % 64, 64)
            gq_tok[seg] = np.concatenate([rc, rs, cc, cs_], 1)
        m["mla_cs_tok"] = cs_tok
        m["mla_cs_feat"] = cs_feat
        m["gqa_cs_tok"] = gq_tok
        hs = np.zeros((8, 2), f32)
        if qtr > 0:
            hs[2 * (qtr - 1) + 1, 0] = 1.0
        if qtr < 3:
            hs[2 * (qtr + 1), 1] = 1.0
        m["halo_sel"] = hs
        rm = np.full((2, 128, 32, 6), NEG, f32)
        for seg in range(2):
            base_row = 0 if seg == 0 else qtr * 32
            R = 32 if seg == 0 else 128
            for lr in range(32):
                ws, nch, layout, pi0 = _na_window(lr)
                rq = base_row + lr
                r0 = min(max(rq - 4, 0), R - 8)
                for ci in range(nch):
                    for jj in range(2):
                        rg = base_row + (ws + 2 * ci + jj - 4)
                        if r0 <= rg <= r0 + 7:
                            rm[seg, jj * 64:(jj + 1) * 64, lr, ci] = 0.0
        m["na_rmask"] = rm
        hsel = np.zeros((128, 8), f32)
        if qtr > 0:
            hsel[:, qtr - 1] = 1.0
        if qtr < 3:
            hsel[:, 4 + qtr + 1] = 1.0
        m["na_hsel"] = hsel
        in_maps.append(m)
    res = run_bass_kernel_spmd(nc, in_maps, core_ids=list(range(8)))
    outs = [r["y_out"] for r in res.results]
    y_prompt = np.stack([outs[c][0] for c in range(8)], 0).astype(f32)
    y_sample = np.stack([np.concatenate([outs[g * 4 + q][1] for q in range(4)], 0) for g in range(2)], 0).astype(f32)
    return (y_prompt, y_sample)
```

```python
import numpy as np
import ml_dtypes
from contextlib import ExitStack
import itertools
import concourse.bass as bass
import concourse.mybir as mybir
from concourse.bass_utils import run_bass_kernel_spmd

F32 = mybir.dt.float32
BF16 = mybir.dt.bfloat16
AF = mybir.ActivationFunctionType
ALU = mybir.AluOpType

D = 1024
S = 2048
NT = S // 128
DEPTH = 4
DFF = 4096
PLE = 256
EPS = 1e-6
NEG = -30000.0
SEM_ROT = 6000
NO_CC = False
NA_DBG = ''
STOP_MIX = False


class Buf:
    __slots__ = ("w", "rs")

    def __init__(self):
        self.w = None
        self.rs = []


class Eng:
    def __init__(self, kb, name, eng):
        self.kb = kb
        self.name = name
        self.eng = eng
        self.sem = None
        self.semi = -1
        self.cnt = 0
        self.waited = {}
        self.newsem()

    def newsem(self):
        self.sem = self.kb.nc.alloc_semaphore()
        self.kb.sems.append(self.sem)
        self.semi = len(self.kb.sems) - 1
        self.cnt = 0

    def wait(self, tok):
        si, val, _ = tok
        cur = self.kb.dma_slot_val.get(si)
        if cur is not None and cur > val:
            val = cur
        if self.waited.get(si, 0) >= val:
            return
        self.eng.wait_ge(self.kb.sems[si], val)
        self.waited[si] = val

    def cur(self):
        return (self.semi, self.cnt, self) if self.cnt > 0 else None


class KB:
    def __init__(self):
        self.nc = bass.Bass("TRN2", target_bir_lowering=False)
        nc = self.nc
        self.sems = []
        self.PE = Eng(self, "pe", nc.tensor)
        self.ACT = Eng(self, "act", nc.scalar)
        self.DVE = Eng(self, "dve", nc.vector)
        self.POOL = Eng(self, "pool", nc.gpsimd)
        self.SP = Eng(self, "sp", nc.sync)
        self.engs = [self.PE, self.ACT, self.DVE, self.POOL, self.SP]
        self.dma_pools = {}
        self.dma_sems = []
        for qn in ("sp", "pool"):
            pool_ = []
            for _ in range(10):
                s = nc.alloc_semaphore()
                self.sems.append(s)
                pool_.append([len(self.sems) - 1, 0])
            self.dma_pools[qn] = [pool_, 0]
            self.dma_sems.extend(pool_)
        self.dma_slot_val = {}
        self.old_final = []

    def _deps(self, E, reads, writes):
        for b in reads:
            if b.w is not None:
                if b.w[2] is E and E is self.PE:
                    continue
                E.wait(b.w)
        for b in writes:
            if b.w is not None and b.w[2] is not E:
                E.wait(b.w)
            for t in b.rs:
                if t[2] is not E:
                    E.wait(t)

    def _commit(self, tok, reads, writes):
        for b in writes:
            b.w = tok
            b.rs = []
        for b in reads:
            b.rs.append(tok)
            if len(b.rs) > 12:
                best = {}
                for t in b.rs:
                    if t[0] not in best or best[t[0]][1] < t[1]:
                        best[t[0]] = t
                b.rs = list(best.values())

    def op(self, E, fn, reads=(), writes=()):
        if E.cnt >= SEM_ROT:
            self.old_final.append(E.cur())
            E.newsem()
        self._deps(E, reads, writes)
        ins = fn()
        E.cnt += 1
        ins.then_inc(E.sem, 1)
        tok = (E.semi, E.cnt, E)
        self._commit(tok, reads, writes)
        return tok

    def pe(self, fn, reads=(), writes=()):
        return self.op(self.PE, fn, reads, writes)

    def act(self, fn, reads=(), writes=()):
        return self.op(self.ACT, fn, reads, writes)

    def dve(self, fn, reads=(), writes=()):
        return self.op(self.DVE, fn, reads, writes)

    def pool(self, fn, reads=(), writes=()):
        return self.op(self.POOL, fn, reads, writes)

    def dma(self, Q, out, in_, reads=(), writes=()):
        self._deps(Q, reads, writes)
        pr_ = self.dma_pools[Q.name]
        slot = pr_[0][pr_[1]]
        pr_[1] = (pr_[1] + 1) % len(pr_[0])
        if slot[1] >= SEM_ROT:
            self.old_final.append((slot[0], slot[1], None))
            s = self.nc.alloc_semaphore()
            self.sems.append(s)
            slot[0] = len(self.sems) - 1
            slot[1] = 0
        Q.eng.dma_start(out=out, in_=in_).then_inc(self.sems[slot[0]], 16)
        slot[1] += 16
        self.dma_slot_val[slot[0]] = slot[1]
        tok = (slot[0], slot[1], None)
        self._commit(tok, reads, writes)
        return tok

    def collective(self, ins, outs, reads, writes, groups):
        Q = self.POOL
        if NO_CC:
            n = ins[0].shape[0]
            return self.dma(Q, outs[0][0:n], ins[0], reads=reads, writes=writes)
        self._deps(Q, reads, writes)
        s = self.nc.alloc_semaphore()
        self.sems.append(s)
        si = len(self.sems) - 1
        Q.eng.collective_compute("AllGather", ALU.bypass, replica_groups=groups,
                                 ins=ins, outs=outs).then_inc(s, 1)
        tok = (si, 1, None)
        self._commit(tok, reads, writes)
        return tok

    def barrier(self):
        toks = [e.cur() for e in self.engs if e.cur() is not None]
        toks += [(s[0], s[1], None) for s in self.dma_sems if s[1] > 0]
        toks += self.old_final
        self.old_final = []
        for e in self.engs:
            for t in toks:
                if t[2] is not e:
                    e.wait(t)


def build_program(nlayers=DEPTH, debug=False):
    kb = KB()
    nc = kb.nc
    PE, ACT, DVE, POOL, SP = kb.PE, kb.ACT, kb.DVE, kb.POOL, kb.SP

    _ctr = [0]

    def sbt(name, shape, dt):
        _ctr[0] += 1
        return nc.sbuf_tensor("%s_%d" % (name, _ctr[0]), list(shape), dt)

    def din(name, shape, dt=F32):
        return nc.dram_tensor(name, list(shape), dt, kind="ExternalInput").ap()

    def dscr(name, shape, dt=BF16):
        return nc.dram_tensor(name, list(shape), dt).ap()

    x_in = din("x_in", [2, S, D])
    p_in = din("p_in", [DEPTH, 2, S, PLE])
    y_out = nc.dram_tensor("y_out", [2, S, D], F32, kind="ExternalOutput").ap()
    g_mix_pre = din("norm_mix_pre", [DEPTH, D])
    g_mix_post = din("norm_mix_post", [DEPTH, D])
    g_ffn_pre = din("norm_ffn_pre", [DEPTH, D])
    g_ffn_post = din("norm_ffn_post", [DEPTH, D])
    g_ple = din("ple_norm", [DEPTH, D])
    mla_w_down = din("mla_w_down", [2, D, 704])
    mla_q_norm = din("mla_q_norm", [2, 384])
    mla_kv_norm = din("mla_kv_norm", [2, 256])
    mla_w_uq = din("mla_w_uq", [2, 384, 1536])
    mla_w_uq_rp = din("mla_w_uq_rp", [2, 384, 512])
    mla_w_ukv = din("mla_w_ukv", [2, 256, 2048])
    mla_w_o = din("mla_w_o", [2, D, D])
    gqa_w_qkv = din("gqa_w_qkv", [1, D, 1536])
    gqa_q_norm = din("gqa_q_norm", [1, 128])
    gqa_k_norm = din("gqa_k_norm", [1, 128])
    gqa_w_o = din("gqa_w_o", [1, D, D])
    na_w_qkv = din("na_w_qkv", [1, D, 3072])
    na_w_o = din("na_w_o", [1, D, D])
    na_tab = din("na_tab", [2, 4, 128, 8 * 4 * 64])
    na_rmask = din("na_rmask", [2, 128, 32, 6])
    na_hsel = din("na_hsel", [128, 8])
    ffn_w_in = din("ffn_w_in", [DEPTH, D, 2 * DFF])
    ffn_conv_w = din("ffn_conv_wT", [DEPTH, 128, 3, 32])
    ffn_conv_b = din("ffn_conv_bT", [DEPTH, 128, 32])
    ffn_w_out = din("ffn_w_out", [DEPTH, DFF, D])
    ple_w_proj = din("ple_w_proj", [DEPTH, PLE, D])
    ple_w_gate = din("ple_w_gate", [DEPTH, D, D])
    ident_in = din("ident", [128, 128])
    halo_sel = din("halo_sel", [8, 2])
    mla_cs_tok = din("mla_cs_tok", [2, S, 64])
    mla_cs_feat = din("mla_cs_feat", [2, 2, 64, S])
    gqa_cs_tok = din("gqa_cs_tok", [2, S, 128])

    xres = nc.dram_tensor("xres", [2, S, D], F32).ap()
    xnT_d = dscr("xnT_d", [2, 8, 128, S])
    xfT_d = dscr("xfT_d", [2, 8, 128, S])
    w_in_bf = dscr("w_in_bf", [D, 2 * DFF])
    w_out_bf = dscr("w_out_bf", [DFF, D])
    cc_in_mla = dscr("cc_in_mla", [256, S])
    cc_out_mla = dscr("cc_out_mla", [4 * 256, S])
    cc_in_mlb = dscr("cc_in_mlb", [64, S])
    cc_out_mlb = dscr("cc_out_mlb", [4 * 64, S])
    cc_in_gk = dscr("cc_in_gk", [256, S])
    cc_out_gk = dscr("cc_out_gk", [4 * 256, S])
    cc_in_gv = dscr("cc_in_gv", [S, 256])
    cc_out_gv = dscr("cc_out_gv", [4 * S, 256])
    cc_in_h = dscr("cc_in_h", [2, D])
    cc_out_h = dscr("cc_out_h", [8, D])
    na_kv_d = dscr("na_kv_d", [2, S, 2048])
    cc_in_na = dscr("cc_in_na", [512, 2048])
    cc_out_na = dscr("cc_out_na", [4 * 512, 2048])
    GROUPS = [[0, 1, 2, 3], [4, 5, 6, 7]]

    B_xres = [[Buf() for _ in range(NT)] for _ in range(2)]
    B_xnT = [[Buf() for _ in range(NT)] for _ in range(2)]
    B_xfT = [[Buf() for _ in range(NT)] for _ in range(2)]
    B_win = Buf()
    B_wout = Buf()
    B_cc = {k: Buf() for k in ["in_mla", "out_mla", "in_mlb", "out_mlb", "in_gk", "out_gk", "in_gv", "out_gv", "in_h", "out_h",
                               "in_na", "out_na", "nakv0", "nakv1"]}

    ident = nc.alloc_sbuf_tensor("identb", [128, 128], BF16).ap()
    ones = nc.alloc_sbuf_tensor("onesb", [128, 128], BF16).ap()
    B_const = Buf()
    PS = nc.alloc_psum_tensor("PS", [128, 8, 512], F32).ap()
    PB = [Buf() for _ in range(8)]
    kb.dma(POOL, ident, ident_in, writes=[B_const])
    kb.dve(lambda: nc.vector.memset(ones, 1.0), writes=[B_const])

    def ps_bf(b0, nb=1):
        return PS[:, b0:b0 + nb, :].rearrange("p b n -> p (b n)").bitcast(BF16)

    def ps_f(b0, nb=1):
        return PS[:, b0:b0 + nb, :].rearrange("p b n -> p (b n)")

    def load_gain(es, name, src_row, n=D):
        t = es.enter_context(sbt(name, [128, n], F32))
        b = Buf()
        kb.dma(SP, t[:], src_row.to_broadcast([128, n]), writes=[b])
        return t, b

    def rms_stats(src_ap, src_bufs, n, scr, stat, stat_b, col, scr_b=None):
        kb.act(lambda: nc.scalar.activation(out=scr[:, 0:n], in_=src_ap, func=AF.Square,
                                            accum_out=stat[:, col:col + 1]),
               reads=src_bufs, writes=[stat_b] + ([scr_b] if scr_b is not None else []))
        kb.act(lambda: nc.scalar.activation(out=stat[:, col + 1:col + 2], in_=stat[:, col:col + 1],
                                            func=AF.Sqrt, scale=1.0 / n, bias=EPS),
               reads=[stat_b], writes=[stat_b])
        kb.dve(lambda: nc.vector.reciprocal(out=stat[:, col + 2:col + 3], in_=stat[:, col + 1:col + 2]),
               reads=[stat_b], writes=[stat_b])
        return stat[:, col + 2:col + 3]

    def transpose_to(src_bf, src_b, nchunk, pbank, dst_fn, dst_bufs, rows=128):
        pv = ps_bf(pbank)
        def f():
            ins = None
            for c in range(nchunk):
                ins = nc.tensor.transpose(pv[:, c * 128:(c + 1) * 128], src_bf[:, c * 128:(c + 1) * 128], ident)
            return ins
        kb.pe(f, reads=[src_b, B_const], writes=[PB[pbank]])
        return pv

    def norm_T_store(es_tiles, x_sb, x_b, gain, gain_b, dst_d, dst_b, seg, t, pbank, extra_row=None):
        scr, scr_b, stat, stat_b, xn, xn_b, xT, xT_b = es_tiles
        rstd = rms_stats(x_sb, [x_b], D, scr, stat, stat_b, 0, scr_b)
        if gain is not None:
            kb.dve(lambda: nc.vector.scalar_tensor_tensor(out=xn, in0=x_sb, scalar=rstd, in1=gain,
                                                          op0=ALU.mult, op1=ALU.mult),
                   reads=[x_b, stat_b, gain_b], writes=[xn_b])
        else:
            kb.dve(lambda: nc.vector.tensor_scalar(out=xn, in0=x_sb, scalar1=rstd, scalar2=None, op0=ALU.mult),
                   reads=[x_b, stat_b], writes=[xn_b])
        if extra_row is not None:
            extra_row(xn, xn_b)
        pv = transpose_to(xn, xn_b, 8, pbank, None, None)
        kb.act(lambda: nc.scalar.activation(out=xT, in_=pv, func=AF.Copy), reads=[PB[pbank]], writes=[xT_b])
        if dst_d is not None:
            kb.dma(POOL, dst_d[seg, :, :, t * 128:(t + 1) * 128].rearrange("c p n -> p c n"),
                   xT.rearrange("p (c n) -> p c n", c=8), reads=[xT_b], writes=[dst_b[seg][t]])

    def alloc_norm_tiles(es, tag):
        scr = es.enter_context(sbt("scr" + tag, [128, D], F32))[:]
        stat = es.enter_context(sbt("stat" + tag, [128, 16], F32))[:]
        xn = es.enter_context(sbt("xn" + tag, [128, D], BF16))[:]
        xT = es.enter_context(sbt("xT" + tag, [128, D], BF16))[:]
        return (scr, Buf(), stat, Buf(), xn, Buf(), xT, Buf())

    def stage_initial():
        with ExitStack() as es:
            gain, gain_b = load_gain(es, "g0", g_mix_pre[0:1, :])
            tl = [alloc_norm_tiles(es, "i%d" % i) for i in range(2)]
            xs = [es.enter_context(sbt("xi%d" % i, [128, D], F32))[:] for i in range(2)]
            xb = [Buf(), Buf()]
            k = 0
            for seg in range(2):
                for t in range(NT):
                    j = k % 2
                    kb.dma(SP, xs[j], x_in[seg, t * 128:(t + 1) * 128, :], writes=[xb[j]])
                    kb.dma(POOL, xres[seg, t * 128:(t + 1) * 128, :], xs[j], reads=[xb[j]], writes=[B_xres[seg][t]])
                    norm_T_store(tl[j], xs[j], xb[j], gain[:], gain_b, xnT_d, B_xnT, seg, t, j)
                    k += 1
        kb.barrier()

    def attn_head(es_at, nk, qparts, kparts, vfn, OT_dst, OT_b, extra_reads):
        PT, PT_b, rec, rec_b = es_at
        nkt = nk // 128
        ngrp = nkt // 2
        for qb in range(S // 512):
            qs = slice(qb * 512, (qb + 1) * 512)

            def scores(g):
                b0 = 2 * (g % 3)
                def f():
                    ins = None
                    for j in range(2):
                        kt = 2 * g + j
                        for pi, ((q_ap, rows), (k_ap, _)) in enumerate(zip(qparts, kparts)):
                            ins = nc.tensor.matmul(PS[:, b0 + j, :], lhsT=k_ap[0:rows, kt * 128:(kt + 1) * 128],
                                                   rhs=q_ap[0:rows, qs], start=(pi == 0),
                                                   stop=(pi == len(qparts) - 1))
                    return ins
                kb.pe(f, reads=extra_reads, writes=[PB[b0], PB[b0 + 1]])

            def expo(g):
                b0 = 2 * (g % 3)
                kb.act(lambda: nc.scalar.activation(out=PT[g % 3], in_=ps_f(b0, 2), func=AF.Exp),
                       reads=[PB[b0], PB[b0 + 1]], writes=[PT_b[g % 3]])

            def pv(g):
                def f():
                    ins = None
                    for j in range(2):
                        kt = 2 * g + j
                        rhs = PT[g % 3][:, j * 512:(j + 1) * 512]
                        nc.tensor.matmul(PS[:, 6, :], lhsT=vfn(kt), rhs=rhs, start=(kt == 0), stop=(kt == nkt - 1))
                        ins = nc.tensor.matmul(PS[:, 7, :], lhsT=ones, rhs=rhs, start=(kt == 0), stop=(kt == nkt - 1))
                    return ins
                kb.pe(f, reads=[PT_b[g % 3], B_const] + extra_reads, writes=[PB[6], PB[7]])

            for g in range(ngrp):
                scores(g)
                expo(g)
                if g > 1:
                    pv(g - 2)
            if ngrp > 1:
                pv(ngrp - 2)
            pv(ngrp - 1)
            kb.dve(lambda: nc.vector.reciprocal(out=rec, in_=PS[:, 7, :]), reads=[PB[7]], writes=[rec_b])
            kb.dve(lambda: nc.vector.tensor_tensor(out=OT_dst[:, qs], in0=PS[:, 6, :], in1=rec, op=ALU.mult),
                   reads=[PB[6], rec_b], writes=[OT_b])

    def alloc_attn_tiles(es):
        PT = [es.enter_context(sbt("PT%d" % i, [128, 1024], BF16))[:] for i in range(3)]
        rec = es.enter_context(sbt("rec", [128, 512], F32))[:]
        return (PT, [Buf(), Buf(), Buf()], rec, Buf())

    def stage_out(es, layer, seg, OT, OT_b, wo_src, nchunk_rows, last_phase_cb=None):
        rows, nch = nchunk_rows
        wo = es.enter_context(sbt("wo", [rows, nch, D], BF16))
        wo_b = Buf()
        kb.dma(POOL, wo[:], wo_src.rearrange("(c p) n -> p c n", p=rows), writes=[wo_b])
        gpost, gpost_b = load_gain(es, "gpost", g_mix_post[layer:layer + 1, :])
        gfpre, gfpre_b = load_gain(es, "gfpre", g_ffn_pre[layer:layer + 1, :])
        tl = [alloc_norm_tiles(es, "c%d" % i) for i in range(2)]
        xs = [es.enter_context(sbt("xc%d" % i, [128, D], F32))[:] for i in range(2)]
        xb = [Buf(), Buf()]
        x1 = [es.enter_context(sbt("x1c%d" % i, [128, D], F32))[:] for i in range(2)]
        x1b = [Buf(), Buf()]
        st2 = es.enter_context(sbt("st2", [128, 16], F32))[:]
        st2_b = Buf()
        for t in range(NT):
            j = t % 2
            ts = slice(t * 128, (t + 1) * 128)
            kb.dma(SP, xs[j], xres[seg, ts, :], reads=[B_xres[seg][t]], writes=[xb[j]])
            b0 = 6 if False else (2 * j)
            def f():
                ins = None
                for half in range(2):
                    for c in range(nch):
                        ins = nc.tensor.matmul(PS[:, b0 + half, :], lhsT=OT[0:rows, c, ts],
                                               rhs=wo[0:rows, c, half * 512:(half + 1) * 512],
                                               start=(c == 0), stop=(c == nch - 1))
                return ins
            kb.pe(f, reads=[OT_b, wo_b], writes=[PB[b0], PB[b0 + 1]])
            scr, scr_b = tl[j][0], tl[j][1]
            rstd = rms_stats(ps_f(b0, 2), [PB[b0], PB[b0 + 1]], D, scr, st2, st2_b, 4 * j, scr_b)
            kb.dve(lambda: nc.vector.scalar_tensor_tensor(out=scr, in0=ps_f(b0, 2), scalar=rstd, in1=gpost[:],
                                                          op0=ALU.mult, op1=ALU.mult),
                   reads=[PB[b0], PB[b0 + 1], st2_b, gpost_b], writes=[scr_b])
            kb.dve(lambda: nc.vector.tensor_tensor(out=x1[j], in0=scr, in1=xs[j], op=ALU.add),
                   reads=[scr_b, xb[j]], writes=[x1b[j]])
            kb.dma(POOL, xres[seg, ts, :], x1[j], reads=[x1b[j]], writes=[B_xres[seg][t]])
            extra = None
            if seg == 1 and t in (0, NT - 1):
                def extra(xn, xn_b, t=t):
                    if t == 0:
                        kb.dma(POOL, cc_in_h[0:1, :], xn[0:1, :], reads=[xn_b], writes=[B_cc["in_h"]])
                    else:
                        kb.dma(POOL, cc_in_h[1:2, :], xn[127:128, :], reads=[xn_b], writes=[B_cc["in_h"]])
            norm_T_store(tl[j], x1[j], x1b[j], gfpre[:], gfpre_b, xfT_d, B_xfT, seg, t, 6 + j, extra_row=extra)
        if seg == 1:
            kb.collective([cc_in_h], [cc_out_h], reads=[B_cc["in_h"]], writes=[B_cc["out_h"]], groups=GROUPS)

    def mixer_mla(layer, seg, j):
        nk = S if seg == 0 else 4 * S
        with ExitStack() as es0, ExitStack() as es:
            OT = es0.enter_context(sbt("OT", [128, 8, S], BF16))[:]
            sb = lambda name, shape, dt=BF16: es.enter_context(sbt(name, shape, dt))[:]
            ckvT = sb("ckvT", [128, 2, nk])
            kropeT = sb("kropeT", [64, nk])
            cqT = sb("cqT", [128, 3, S])
            B_ckv, B_cq, B_OT = Buf(), Buf(), Buf()
            w_uq = sb("w_uq", [128, 3, 1536])
            w_uqr = sb("w_uqr", [128, 3, 512])
            w_ukv = sb("w_ukv", [128, 2, 2048])
            B_w = Buf()
            kb.dma(POOL, w_uq, mla_w_uq[j].rearrange("(c p) n -> p c n", p=128), writes=[B_w])
            kb.dma(POOL, w_uqr, mla_w_uq_rp[j].rearrange("(c p) n -> p c n", p=128), writes=[B_w])
            kb.dma(POOL, w_ukv, mla_w_ukv[j].rearrange("(c p) n -> p c n", p=128), writes=[B_w])
            csf = sb("csf", [64, 2, S], F32)
            B_csf = Buf()
            kb.dma(SP, csf, mla_cs_feat[seg].rearrange("a p n -> p a n"), writes=[B_csf])
            with ExitStack() as es2:
                sb2 = lambda name, shape, dt=BF16: es2.enter_context(sbt(name, shape, dt))[:]
                xnTb = [sb2("xnTb%d" % i, [128, 8, 512]) for i in range(2)]
                B_xnb = [Buf(), Buf()]
                wd = sb2("wd", [128, 8, 704])
                B_wd = Buf()
                kb.dma(POOL, wd, mla_w_down[j].rearrange("(c p) n -> p c n", p=128), writes=[B_wd])
                gq, gq_b = load_gain(es2, "gq", mla_q_norm[j:j + 1, :], 384)
                gkv, gkv_b = load_gain(es2, "gkv", mla_kv_norm[j:j + 1, :], 256)
                cst = sb2("cst", [128, NT, 64], F32)
                B_cst = Buf()
                kb.dma(SP, cst, mla_cs_tok[seg].rearrange("(t p) n -> p t n", p=128), writes=[B_cst])
                scr = sb2("scrA", [128, 512], F32)
                stat = sb2("statA", [128, 16], F32)
                stat_b = Buf()
                dnb = [sb2("dnb%d" % i, [128, 768]) for i in range(2)]
                dnb_b = [Buf(), Buf()]
                tmp = [sb2("tmpA%d" % i, [128, 128], F32) for i in range(2)]
                tmp_b = [Buf(), Buf()]
                for t in range(NT):
                    i = t % 2
                    ts = slice(t * 128, (t + 1) * 128)
                    b0 = 2 * i
                    xb_i = (t // 4) % 2
                    if t % 4 == 0:
                        kb.dma(SP, xnTb[xb_i], xnT_d[seg, :, :, t * 128:t * 128 + 512].rearrange("c p n -> p c n"),
                               reads=B_xnT[seg][t:t + 4], writes=[B_xnb[xb_i]])
                    xnT = xnTb[xb_i]
                    B_xn = B_xnb[xb_i]
                    tl_ = slice((t % 4) * 128, (t % 4 + 1) * 128)
                    def f():
                        ins = None
                        for c in range(8):
                            nc.tensor.matmul(PS[:, b0, 0:384], lhsT=xnT[:, c, tl_], rhs=wd[:, c, 0:384],
                                             start=(c == 0), stop=(c == 7))
                        for c in range(8):
                            ins = nc.tensor.matmul(PS[:, b0 + 1, 0:320], lhsT=xnT[:, c, tl_], rhs=wd[:, c, 384:704],
                                                   start=(c == 0), stop=(c == 7))
                        return ins
                    kb.pe(f, reads=[B_xn, B_wd], writes=[PB[b0], PB[b0 + 1]])
                    rq = rms_stats(PS[:, b0, 0:384], [PB[b0]], 384, scr, stat, stat_b, 8 * i)
                    rkv = rms_stats(PS[:, b0 + 1, 0:256], [PB[b0 + 1]], 256, scr, stat, stat_b, 8 * i + 4)
                    kb.dve(lambda: nc.vector.scalar_tensor_tensor(out=dnb[i][:, 0:384], in0=PS[:, b0, 0:384], scalar=rq,
                                                                  in1=gq[:], op0=ALU.mult, op1=ALU.mult),
                           reads=[PB[b0], stat_b, gq_b], writes=[dnb_b[i]])
                    kb.dve(lambda: nc.vector.scalar_tensor_tensor(out=dnb[i][:, 384:640], in0=PS[:, b0 + 1, 0:256], scalar=rkv,
                                                                  in1=gkv[:], op0=ALU.mult, op1=ALU.mult),
                           reads=[PB[b0 + 1], stat_b, gkv_b], writes=[dnb_b[i]])
                    x1 = PS[:, b0 + 1, 256:288]
                    x2 = PS[:, b0 + 1, 288:320]
                    cs_c = cst[:, t, 0:32]
                    cs_s = cst[:, t, 32:64]
                    tm = tmp[i]
                    kb.dve(lambda: nc.vector.tensor_tensor(out=tm[:, 0:32], in0=x1, in1=cs_c, op=ALU.mult),
                           reads=[PB[b0 + 1], B_cst], writes=[tmp_b[i]])
                    kb.dve(lambda: nc.vector.tensor_tensor(out=tm[:, 32:64], in0=x2, in1=cs_s, op=ALU.mult),
                           reads=[PB[b0 + 1], B_cst], writes=[tmp_b[i]])
                    kb.dve(lambda: nc.vector.tensor_tensor(out=tm[:, 64:96], in0=x2, in1=cs_c, op=ALU.mult),
                           reads=[PB[b0 + 1], B_cst], writes=[tmp_b[i]])
                    kb.dve(lambda: nc.vector.tensor_tensor(out=tm[:, 96:128], in0=x1, in1=cs_s, op=ALU.mult),
                           reads=[PB[b0 + 1], B_cst], writes=[tmp_b[i]])
                    kb.dve(lambda: nc.vector.tensor_tensor(out=dnb[i][:, 640:672], in0=tm[:, 0:32], in1=tm[:, 32:64],
                                                           op=ALU.subtract), reads=[tmp_b[i]], writes=[dnb_b[i]])
                    kb.dve(lambda: nc.vector.tensor_tensor(out=dnb[i][:, 672:704], in0=tm[:, 64:96], in1=tm[:, 96:128],
                                                           op=ALU.add), reads=[tmp_b[i]], writes=[dnb_b[i]])
                    pbank = 4 + i
                    pvw = ps_bf(pbank)
                    def ft():
                        ins = None
                        for c in range(5):
                            ins = nc.tensor.transpose(pvw[:, c * 128:(c + 1) * 128], dnb[i][:, c * 128:(c + 1) * 128], ident)
                        ins = nc.tensor.transpose(pvw[0:64, 640:768], dnb[i][:, 640:704], ident)
                        return ins
                    kb.pe(ft, reads=[dnb_b[i], B_const], writes=[PB[pbank]])
                    off = 0 if seg == 0 else 0
                    kb.act(lambda: nc.scalar.activation(out=cqT[:, :, ts], in_=pvw[:, 0:384].rearrange("p (c n) -> p c n", c=3),
                                                        func=AF.Copy), reads=[PB[pbank]], writes=[B_cq])
                    if seg == 0:
                        kb.act(lambda: nc.scalar.activation(out=ckvT[:, :, ts], in_=pvw[:, 384:640].rearrange("p (c n) -> p c n", c=2),
                                                            func=AF.Copy), reads=[PB[pbank]], writes=[B_ckv])
                        kb.act(lambda: nc.scalar.activation(out=kropeT[:, ts], in_=pvw[0:64, 640:768], func=AF.Copy),
                               reads=[PB[pbank]], writes=[B_ckv])
                    else:
                        kb.act(lambda: nc.scalar.activation(out=ckvT[:, :, ts], in_=pvw[:, 384:640].rearrange("p (c n) -> p c n", c=2),
                                                            func=AF.Copy), reads=[PB[pbank]], writes=[B_ckv])
                        kb.act(lambda: nc.scalar.activation(out=kropeT[:, ts], in_=pvw[0:64, 640:768], func=AF.Copy),
                               reads=[PB[pbank]], writes=[B_ckv])
                if seg == 1:
                    kb.dma(POOL, cc_in_mla.rearrange("(c p) n -> p c n", p=128), ckvT[:, :, 0:S],
                           reads=[B_ckv], writes=[B_cc["in_mla"]])
                    kb.dma(POOL, cc_in_mlb, kropeT[:, 0:S], reads=[B_ckv], writes=[B_cc["in_mlb"]])
                    kb.collective([cc_in_mla], [cc_out_mla], reads=[B_cc["in_mla"]], writes=[B_cc["out_mla"]], groups=GROUPS)
                    kb.collective([cc_in_mlb], [cc_out_mlb], reads=[B_cc["in_mlb"]], writes=[B_cc["out_mlb"]], groups=GROUPS)
                    for r in range(4):
                        kb.dma(SP, ckvT[:, :, r * S:(r + 1) * S],
                               cc_out_mla[r * 256:(r + 1) * 256, :].rearrange("(c p) n -> p c n", p=128),
                               reads=[B_cc["out_mla"]], writes=[B_ckv])
                        kb.dma(SP, kropeT[:, r * S:(r + 1) * S], cc_out_mlb[r * 64:(r + 1) * 64, :],
                               reads=[B_cc["out_mlb"]], writes=[B_ckv])
            kb.barrier()
            with ExitStack() as es3:
                sb3 = lambda name, shape, dt=BF16: es3.enter_context(sbt(name, shape, dt))[:]
                qnT = sb3("qnT", [128, S])
                qrT = sb3("qrT", [64, S])
                KhT = sb3("KhT", [128, nk])
                Vh = sb3("Vh", [128, nk // 128, 128])
                B_q, B_K, B_V = Buf(), Buf(), Buf()
                t1 = sb3("t1", [64, 512], F32)
                t2 = sb3("t2", [64, 512], F32)
                B_t = Buf()
                at = alloc_attn_tiles(es3)
                scale = 192.0 ** -0.5
                for h in range(8):
                    for qb in range(S // 512):
                        qs = slice(qb * 512, (qb + 1) * 512)
                        bq = 4 + (qb % 2)
                        def f():
                            ins = None
                            for c in range(3):
                                ins = nc.tensor.matmul(PS[:, bq, :], lhsT=w_uq[:, c, h * 192:h * 192 + 128], rhs=cqT[:, c, qs],
                                                       start=(c == 0), stop=(c == 2))
                            return ins
                        kb.pe(f, reads=[B_w, B_cq], writes=[PB[bq]])
                        kb.act(lambda: nc.scalar.activation(out=qnT[:, qs], in_=PS[:, bq, :], func=AF.Copy, scale=scale),
                               reads=[PB[bq]], writes=[B_q])
                        def f3():
                            ins = None
                            for c in range(3):
                                ins = nc.tensor.matmul(PS[0:64, bq, :], lhsT=w_uq[:, c, h * 192 + 128:h * 192 + 192], rhs=cqT[:, c, qs],
                                                       start=(c == 0), stop=(c == 2))
                            return ins
                        kb.pe(f3, reads=[B_w, B_cq], writes=[PB[bq]])
                        kb.dve(lambda: nc.vector.tensor_tensor(out=t1, in0=PS[0:64, bq, :], in1=csf[:, 0, qs], op=ALU.mult),
                               reads=[PB[bq], B_csf], writes=[B_t])
                        def f4():
                            ins = None
                            for c in range(3):
                                ins = nc.tensor.matmul(PS[0:64, bq, :], lhsT=w_uqr[:, c, h * 64:(h + 1) * 64], rhs=cqT[:, c, qs],
                                                       start=(c == 0), stop=(c == 2))
                            return ins
                        kb.pe(f4, reads=[B_w, B_cq], writes=[PB[bq]])
                        kb.dve(lambda: nc.vector.tensor_tensor(out=t2, in0=PS[0:64, bq, :], in1=csf[:, 1, qs], op=ALU.mult),
                               reads=[PB[bq], B_csf], writes=[B_t])
                        kb.dve(lambda: nc.vector.tensor_tensor(out=qrT[:, qs], in0=t1, in1=t2, op=ALU.add),
                               reads=[B_t], writes=[B_q])
                    for kbk in range(nk // 512):
                        ks = slice(kbk * 512, (kbk + 1) * 512)
                        bq = 4 + (kbk % 2)
                        def f():
                            ins = None
                            for c in range(2):
                                ins = nc.tensor.matmul(PS[:, bq, :], lhsT=w_ukv[:, c, h * 256:h * 256 + 128], rhs=ckvT[:, c, ks],
                                                       start=(c == 0), stop=(c == 1))
                            return ins
                        kb.pe(f, reads=[B_w, B_ckv], writes=[PB[bq]])
                        kb.act(lambda: nc.scalar.activation(out=KhT[:, ks], in_=PS[:, bq, :], func=AF.Copy),
                               reads=[PB[bq]], writes=[B_K])
                    for kg in range(nk // 512):
                        bq = 4 + (kg % 2)
                        def f():
                            ins = None
                            for jj in range(4):
                                kt = kg * 4 + jj
                                for c in range(2):
                                    ins = nc.tensor.matmul(PS[:, bq, jj * 128:(jj + 1) * 128], lhsT=ckvT[:, c, kt * 128:(kt + 1) * 128],
                                                           rhs=w_ukv[:, c, h * 256 + 128:h * 256 + 256], start=(c == 0), stop=(c == 1))
                            return ins
                        kb.pe(f, reads=[B_w, B_ckv], writes=[PB[bq]])
                        kb.dve(lambda: nc.vector.tensor_copy(out=Vh[:, kg * 4:(kg + 1) * 4, :],
                                                             in_=PS[:, bq, :].rearrange("p (a n) -> p a n", a=4)),
                               reads=[PB[bq]], writes=[B_V])
                    attn_head(at, nk, [(qnT, 128), (qrT, 64)], [(KhT, 128), (kropeT, 64)],
                              lambda kt: Vh[:, kt, :], OT[:, h, :], B_OT, [B_q, B_K, B_V, B_ckv])
            kb.barrier()
            es.close()
            with ExitStack() as es4:
                stage_out(es4, layer, seg, OT, B_OT, mla_w_o[j], (128, 8))
        kb.barrier()

    def mixer_gqa(layer, seg, j):
        nk = S if seg == 0 else 4 * S
        with ExitStack() as es0, ExitStack() as es:
            OT = es0.enter_context(sbt("OTg", [128, 8, S], BF16))[:]
            B_OT = Buf()
            sb = lambda name, shape, dt=BF16: es.enter_context(sbt(name, shape, dt))[:]
            qT = sb("qT", [128, 8, S])
            kT = sb("kT", [128, 2, nk])
            Vall = sb("Vall", [128, nk // 128, 256])
            B_q, B_k, B_v = Buf(), Buf(), Buf()
            with ExitStack() as es2:
                sb2 = lambda name, shape, dt=BF16: es2.enter_context(sbt(name, shape, dt))[:]
                xnTb = [sb2("xnTg%d" % i, [128, 8, 512]) for i in range(2)]
                B_xnb = [Buf(), Buf()]
                wqkv = sb2("wqkv", [128, 8, 1536])
                B_w = Buf()
                kb.dma(POOL, wqkv, gqa_w_qkv[j].rearrange("(c p) n -> p c n", p=128), writes=[B_w])
                gq, gq_b = load_gain(es2, "ggq", gqa_q_norm[j:j + 1, :], 128)
                gk, gk_b = load_gain(es2, "ggk", gqa_k_norm[j:j + 1, :], 128)
                kb.act(lambda: nc.scalar.mul(out=gq[:], in_=gq[:], mul=128.0 ** -0.5), reads=[gq_b], writes=[gq_b])
                cst = sb2("cstg", [128, NT, 128], F32)
                B_cst = Buf()
                kb.dma(SP, cst, gqa_cs_tok[seg].rearrange("(t p) n -> p t n", p=128), writes=[B_cst])
                sq = sb2("sqg", [128, 1280], F32)
                B_sq = Buf()
                stat = sb2("statg", [128, 32], F32)
                stat_b = Buf()
                qn = sb2("qng", [128, 1280], F32)
                B_qn = Buf()
                tm = sb2("tmg", [128, 4, 320], F32)
                B_tm = Buf()
                qkb = [sb2("qkb%d" % i, [128, 1280]) for i in range(2)]
                qkb_b = [Buf(), Buf()]
                for t in range(NT):
                    i = t % 2
                    ts = slice(t * 128, (t + 1) * 128)
                    b0 = 3 * i
                    xb_i = (t // 4) % 2
                    if t % 4 == 0:
                        kb.dma(SP, xnTb[xb_i], xnT_d[seg, :, :, t * 128:t * 128 + 512].rearrange("c p n -> p c n"),
                               reads=B_xnT[seg][t:t + 4], writes=[B_xnb[xb_i]])
                    xnT = xnTb[xb_i]
                    tl_ = slice((t % 4) * 128, (t % 4 + 1) * 128)
                    def f():
                        ins = None
                        for pc in range(3):
                            for c in range(8):
                                ins = nc.tensor.matmul(PS[:, b0 + pc, :], lhsT=xnT[:, c, tl_], rhs=wqkv[:, c, pc * 512:(pc + 1) * 512],
                                                       start=(c == 0), stop=(c == 7))
                        return ins
                    kb.pe(f, reads=[B_xnb[xb_i], B_w], writes=[PB[b0], PB[b0 + 1], PB[b0 + 2]])
                    kb.act(lambda: nc.scalar.activation(out=Vall[:, t, :], in_=PS[:, b0 + 2, 256:512], func=AF.Copy),
                           reads=[PB[b0 + 2]], writes=[B_v])
                    kb.act(lambda: nc.scalar.activation(out=sq[:, 0:1024], in_=ps_f(b0, 2), func=AF.Square),
                           reads=[PB[b0], PB[b0 + 1]], writes=[B_sq])
                    kb.act(lambda: nc.scalar.activation(out=sq[:, 1024:1280], in_=PS[:, b0 + 2, 0:256], func=AF.Square),
                           reads=[PB[b0 + 2]], writes=[B_sq])
                    kb.dve(lambda: nc.vector.tensor_reduce(out=stat[:, 0:10], in_=sq.rearrange("p (h d) -> p h d", d=128),
                                                           axis=mybir.AxisListType.X, op=ALU.add),
                           reads=[B_sq], writes=[stat_b])
                    kb.act(lambda: nc.scalar.activation(out=stat[:, 10:20], in_=stat[:, 0:10], func=AF.Sqrt, scale=1.0 / 128, bias=EPS),
                           reads=[stat_b], writes=[stat_b])
                    kb.dve(lambda: nc.vector.reciprocal(out=stat[:, 20:30], in_=stat[:, 10:20]), reads=[stat_b], writes=[stat_b])
                    for h in range(10):
                        src = PS[:, b0 + h // 4, (h % 4) * 128:(h % 4 + 1) * 128]
                        g_ = gq if h < 8 else gk
                        g_b = gq_b if h < 8 else gk_b
                        kb.dve(lambda: nc.vector.scalar_tensor_tensor(out=qn[:, h * 128:(h + 1) * 128], in0=src, scalar=stat[:, 20 + h:21 + h],
                                                                      in1=g_[:], op0=ALU.mult, op1=ALU.mult),
                               reads=[PB[b0 + h // 4], stat_b, g_b], writes=[B_qn])
                    q3 = qn.rearrange("p (h d) -> p h d", d=128)
                    o3 = qkb[i].rearrange("p (h d) -> p h d", d=128)
                    for part in range(2):
                        o = part * 64
                        x1 = q3[:, :, o:o + 32]
                        x2 = q3[:, :, o + 32:o + 64]
                        cc_ = cst[:, t, o:o + 32].rearrange("p (o n) -> p o n", o=1).to_broadcast([128, 10, 32])
                        ss_ = cst[:, t, o + 32:o + 64].rearrange("p (o n) -> p o n", o=1).to_broadcast([128, 10, 32])
                        tv = [tm[:, k, :].rearrange("p (h n) -> p h n", n=32) for k in range(4)]
                        kb.dve(lambda: nc.vector.tensor_tensor(out=tv[0], in0=x1, in1=cc_, op=ALU.mult), reads=[B_qn, B_cst], writes=[B_tm])
                        kb.dve(lambda: nc.vector.tensor_tensor(out=tv[1], in0=x2, in1=ss_, op=ALU.mult), reads=[B_qn, B_cst], writes=[B_tm])
                        kb.dve(lambda: nc.vector.tensor_tensor(out=tv[2], in0=x2, in1=cc_, op=ALU.mult), reads=[B_qn, B_cst], writes=[B_tm])
                        kb.dve(lambda: nc.vector.tensor_tensor(out=tv[3], in0=x1, in1=ss_, op=ALU.mult), reads=[B_qn, B_cst], writes=[B_tm])
                        kb.dve(lambda: nc.vector.tensor_tensor(out=o3[:, :, o:o + 32], in0=tv[0], in1=tv[1], op=ALU.subtract),
                               reads=[B_tm], writes=[qkb_b[i]])
                        kb.dve(lambda: nc.vector.tensor_tensor(out=o3[:, :, o + 32:o + 64], in0=tv[2], in1=tv[3], op=ALU.add),
                               reads=[B_tm], writes=[qkb_b[i]])
                    def ft():
                        ins = None
                        for c in range(8):
                            ins = nc.tensor.transpose(ps_bf(6)[:, c * 128:(c + 1) * 128], qkb[i][:, c * 128:(c + 1) * 128], ident)
                        for c in range(2):
                            ins = nc.tensor.transpose(ps_bf(7)[:, c * 128:(c + 1) * 128], qkb[i][:, (8 + c) * 128:(9 + c) * 128], ident)
                        return ins
                    kb.pe(ft, reads=[qkb_b[i], B_const], writes=[PB[6], PB[7]])
                    kb.act(lambda: nc.scalar.activation(out=qT[:, :, ts], in_=ps_bf(6).rearrange("p (c n) -> p c n", c=8), func=AF.Copy),
                           reads=[PB[6]], writes=[B_q])
                    kb.act(lambda: nc.scalar.activation(out=kT[:, :, ts], in_=ps_bf(7)[:, 0:256].rearrange("p (c n) -> p c n", c=2), func=AF.Copy),
                           reads=[PB[7]], writes=[B_k])
                if seg == 1:
                    kb.dma(POOL, cc_in_gk.rearrange("(c p) n -> p c n", p=128), kT[:, :, 0:S], reads=[B_k], writes=[B_cc["in_gk"]])
                    kb.dma(POOL, cc_in_gv.rearrange("(t p) n -> p t n", p=128), Vall[:, 0:NT, :], reads=[B_v], writes=[B_cc["in_gv"]])
                    kb.collective([cc_in_gk], [cc_out_gk], reads=[B_cc["in_gk"]], writes=[B_cc["out_gk"]], groups=GROUPS)
                    kb.collective([cc_in_gv], [cc_out_gv], reads=[B_cc["in_gv"]], writes=[B_cc["out_gv"]], groups=GROUPS)
                    for r in range(4):
                        kb.dma(SP, kT[:, :, r * S:(r + 1) * S], cc_out_gk[r * 256:(r + 1) * 256, :].rearrange("(c p) n -> p c n", p=128),
                               reads=[B_cc["out_gk"]], writes=[B_k])
                        kb.dma(SP, Vall[:, r * NT:(r + 1) * NT, :], cc_out_gv[r * S:(r + 1) * S, :].rearrange("(t p) n -> p t n", p=128),
                               reads=[B_cc["out_gv"]], writes=[B_v])
            kb.barrier()
            with ExitStack() as es3:
                at = alloc_attn_tiles(es3)
                for h in range(8):
                    kvh = h // 4
                    attn_head(at, nk, [(qT[:, h, :], 128)], [(kT[:, kvh, :], 128)],
                              lambda kt, kvh=kvh: Vall[:, kt, kvh * 128:(kvh + 1) * 128], OT[:, h, :], B_OT, [B_q, B_k, B_v])
            kb.barrier()
            es.close()
            with ExitStack() as es4:
                stage_out(es4, layer, seg, OT, B_OT, gqa_w_o[j], (128, 8))
        kb.barrier()

    NBT = 2560
    qTn_d = dscr("qTn_d", [2, 8, 128, S])
    kTn_d = dscr("kTn_d", [2, 8, 128, NBT])
    vn_d = dscr("vn_d", [2, NBT, D])
    cc_in_nak = dscr("cc_in_nak", [8 * 128, 512])
    cc_out_nak = dscr("cc_out_nak", [4 * 8 * 128, 512])
    cc_in_nav = dscr("cc_in_nav", [512, D])
    cc_out_nav = dscr("cc_out_nav", [4 * 512, D])
    B_na = {k: Buf() for k in ["q0", "q1", "k0", "k1", "v0", "v1", "ink", "outk", "inv", "outv"]}

    def na_window(lr):
        if lr < 4:
            lo, hi = lr - 4, 7
        elif lr >= 28:
            lo, hi = 24, lr + 3
        else:
            lo, hi = lr - 4, lr + 3
        blo, bhi = lo + 4, hi + 4
        ws = blo - (blo % 2)
        nrows = bhi - ws + 1
        nch = (nrows + 1) // 2
        rho0 = ws - lr + 3
        layout = 0 if rho0 % 2 == 0 else 1
        pi0 = rho0 // 2
        return ws, nch, layout, pi0

    def mixer_na(layer, seg, j):
        Bq, Bk, Bv = B_na["q%d" % seg], B_na["k%d" % seg], B_na["v%d" % seg]
        with ExitStack() as es2:
            sb2 = lambda name, shape, dt=BF16: es2.enter_context(sbt(name, shape, dt))[:]
            wq = sb2("wna", [128, 8, 3072])
            B_w = Buf()
            for k3 in range(3):
                kb.dma(POOL, wq[:, :, k3 * 1024:(k3 + 1) * 1024],
                       na_w_qkv[j, :, k3 * 1024:(k3 + 1) * 1024].rearrange("(c p) n -> p c n", p=128), writes=[B_w])
            xnTb = [sb2("xnTn%d" % i, [128, 8, 512]) for i in range(2)]
            B_xnb = [Buf(), Buf()]
            stg = [sb2("stgn%d" % i, [128, 512]) for i in range(2)]
            stg_b = [Buf(), Buf()]
            vst = [sb2("vstn%d" % i, [128, D]) for i in range(2)]
            vst_b = [Buf(), Buf()]
            zt = sb2("zt", [128, 8, 256])
            B_z = Buf()
            kb.dve(lambda: nc.vector.memset(zt, 0.0), writes=[B_z])
            if seg == 0:
                kb.dma(POOL, kTn_d[seg, :, :, 0:256].rearrange("c p n -> p c n"), zt, reads=[B_z], writes=[Bk])
                kb.dma(POOL, kTn_d[seg, :, :, NBT - 256:NBT].rearrange("c p n -> p c n"), zt, reads=[B_z], writes=[Bk])
                kb.dma(POOL, vn_d[seg, 0:256, :].rearrange("(t p) n -> p t n", p=128), zt.rearrange("p a (b n) -> p (a b) n", b=2)[:, 0:2, :].rearrange("p t n -> p t n") if False else zt[:, 0:8, :].rearrange("p c n -> p (c n)")[:, 0:2048].rearrange("p (t n) -> p t n", t=2),
                       reads=[B_z], writes=[Bv])
                kb.dma(POOL, vn_d[seg, NBT - 256:NBT, :].rearrange("(t p) n -> p t n", p=128),
                       zt[:, 0:8, :].rearrange("p c n -> p (c n)")[:, 0:2048].rearrange("p (t n) -> p t n", t=2),
                       reads=[B_z], writes=[Bv])
            k_ = 0
            for blk in range(4):
                xi = blk % 2
                kb.dma(SP, xnTb[xi], xnT_d[seg, :, :, blk * 512:(blk + 1) * 512].rearrange("c p n -> p c n"),
                       reads=B_xnT[seg][blk * 4:blk * 4 + 4], writes=[B_xnb[xi]])
                xnT = xnTb[xi]
                for m in range(16):
                    i = k_ % 2
                    k_ += 1
                    bq = 4 + i
                    def f():
                        ins = None
                        for c in range(8):
                            ins = nc.tensor.matmul(PS[:, bq, :], lhsT=wq[:, c, m * 128:(m + 1) * 128], rhs=xnT[:, c, :],
                                                   start=(c == 0), stop=(c == 7))
                        return ins
                    kb.pe(f, reads=[B_w, B_xnb[xi]], writes=[PB[bq]])
                    sc_ = 0.125 if m < 8 else 1.0
                    kb.act(lambda: nc.scalar.activation(out=stg[i], in_=PS[:, bq, :], func=AF.Copy, scale=sc_),
                           reads=[PB[bq]], writes=[stg_b[i]])
                    if m < 8:
                        kb.dma(POOL, qTn_d[seg, m, :, blk * 512:(blk + 1) * 512], stg[i], reads=[stg_b[i]], writes=[Bq])
                    else:
                        kb.dma(POOL, kTn_d[seg, m - 8, :, 256 + blk * 512:256 + (blk + 1) * 512], stg[i], reads=[stg_b[i]], writes=[Bk])
                for tt in range(4):
                    t = blk * 4 + tt
                    i = t % 2
                    b0 = 6 if False else (0 + 2 * i)
                    def f():
                        ins = None
                        for half in range(2):
                            for c in range(8):
                                ins = nc.tensor.matmul(PS[:, b0 + half, :], lhsT=xnT[:, c, tt * 128:(tt + 1) * 128],
                                                       rhs=wq[:, c, 2048 + half * 512:2048 + (half + 1) * 512], start=(c == 0), stop=(c == 7))
                        return ins
                    kb.pe(f, reads=[B_w, B_xnb[xi]], writes=[PB[b0], PB[b0 + 1]])
                    kb.act(lambda: nc.scalar.activation(out=vst[i], in_=ps_f(b0, 2), func=AF.Copy), reads=[PB[b0], PB[b0 + 1]], writes=[vst_b[i]])
                    kb.dma(POOL, vn_d[seg, 256 + t * 128:256 + (t + 1) * 128, :], vst[i], reads=[vst_b[i]], writes=[Bv])
            if seg == 1:
                kb.dma(POOL, cc_in_nak[:, 0:256].rearrange("(c p) n -> c p n", p=128), kTn_d[seg, :, :, 256:512], reads=[Bk], writes=[B_na["ink"]])
                kb.dma(POOL, cc_in_nak[:, 256:512].rearrange("(c p) n -> c p n", p=128), kTn_d[seg, :, :, NBT - 512:NBT - 256], reads=[Bk], writes=[B_na["ink"]])
                kb.dma(POOL, cc_in_nav[0:256, :], vn_d[seg, 256:512, :], reads=[Bv], writes=[B_na["inv"]])
                kb.dma(POOL, cc_in_nav[256:512, :], vn_d[seg, NBT - 512:NBT - 256, :], reads=[Bv], writes=[B_na["inv"]])
                kb.collective([cc_in_nak], [cc_out_nak], reads=[B_na["ink"]], writes=[B_na["outk"]], groups=GROUPS)
                kb.collective([cc_in_nav], [cc_out_nav], reads=[B_na["inv"]], writes=[B_na["outv"]], groups=GROUPS)
                hsel = sb2("hsel", [128, 8], F32)
                B_hs = Buf()
                kb.dma(SP, hsel, na_hsel, writes=[B_hs])
                candk = sb2("candk", [128, 4, 8, 512])
                candv = sb2("candv", [128, 4, 4, D])
                B_ck, B_cv = Buf(), Buf()
                for r in range(4):
                    kb.dma(SP, candk[:, r], cc_out_nak[r * 1024:(r + 1) * 1024, :].rearrange("(c p) n -> p c n", p=128),
                           reads=[B_na["outk"]], writes=[B_ck])
                    kb.dma(SP, candv[:, r], cc_out_nav[r * 512:(r + 1) * 512, :].rearrange("(t p) n -> p t n", p=128),
                           reads=[B_na["outv"]], writes=[B_cv])
                hk = sb2("hk", [128, 2, 8, 256])
                hv = sb2("hv", [128, 2, 2, D])
                B_hk, B_hv = Buf(), Buf()
                for side in range(2):
                    ksl = slice(256, 512) if side == 0 else slice(0, 256)
                    vsl = slice(2, 4) if side == 0 else slice(0, 2)
                    for r in range(4):
                        msk = hsel[:, side * 4 + r:side * 4 + r + 1]
                        if r == 0:
                            kb.dve(lambda: nc.vector.tensor_scalar(out=hk[:, side], in0=candk[:, r, :, ksl], scalar1=msk, scalar2=None, op0=ALU.mult),
                                   reads=[B_ck, B_hs], writes=[B_hk])
                            kb.dve(lambda: nc.vector.tensor_scalar(out=hv[:, side], in0=candv[:, r, vsl, :], scalar1=msk, scalar2=None, op0=ALU.mult),
                                   reads=[B_cv, B_hs], writes=[B_hv])
                        else:
                            kb.dve(lambda: nc.vector.scalar_tensor_tensor(out=hk[:, side], in0=candk[:, r, :, ksl], scalar=msk, in1=hk[:, side],
                                                                          op0=ALU.mult, op1=ALU.add), reads=[B_ck, B_hs, B_hk], writes=[B_hk])
                            kb.dve(lambda: nc.vector.scalar_tensor_tensor(out=hv[:, side], in0=candv[:, r, vsl, :], scalar=msk, in1=hv[:, side],
                                                                          op0=ALU.mult, op1=ALU.add), reads=[B_cv, B_hs, B_hv], writes=[B_hv])
                kb.dma(POOL, kTn_d[seg, :, :, 0:256].rearrange("c p n -> p c n"), hk[:, 0], reads=[B_hk], writes=[Bk])
                kb.dma(POOL, kTn_d[seg, :, :, NBT - 256:NBT].rearrange("c p n -> p c n"), hk[:, 1], reads=[B_hk], writes=[Bk])
                kb.dma(POOL, vn_d[seg, 0:256, :].rearrange("(t p) n -> p t n", p=128), hv[:, 0], reads=[B_hv], writes=[Bv])
                kb.dma(POOL, vn_d[seg, NBT - 256:NBT, :].rearrange("(t p) n -> p t n", p=128), hv[:, 1], reads=[B_hv], writes=[Bv])
        kb.barrier()
        with ExitStack() as es0:
            Otok = es0.enter_context(sbt("Otok", [128, NT, D], BF16))[:]
            B_Ot = Buf()
            with ExitStack() as es:
                sb = lambda name, shape, dt=BF16: es.enter_context(sbt(name, shape, dt))[:]
                rmask = sb("rmask", [128, 32, 6], F32)
                B_rm = Buf()
                kb.dma(SP, rmask, na_rmask[seg], writes=[B_rm])
                qg = [sb("qg%d" % i, [64, 4, S]) for i in range(2)]
                kg = [sb("kg%d" % i, [64, 4, NBT]) for i in range(2)]
                vg = [sb("vg%d" % i, [128, 20, 4, 72]) for i in range(2)]
                tabs = [[sb("tab%d_%d" % (i, l), [128, 8, 4, 64], F32) for l in range(2)] for i in range(2)]
                B_g = [Buf(), Buf()]
                for i in range(2):
                    kb.dve(lambda: nc.vector.memset(vg[i][:, :, :, 64:65], 1.0), writes=[B_g[i]])
                scb = [sb("scb%d" % i, [128, 6, 256], F32) for i in range(2)]
                scb_b = [Buf(), Buf()]
                PTn = [sb("PTn%d" % i, [128, 6, 4, 64]) for i in range(2)]
                PTn_b = [Buf(), Buf()]
                rec = sb("recn", [64, 8], F32)
                rec_b = Buf()
                vstg = sb("vstg", [128, 20, 256])
                B_vs = Buf()
                if NA_DBG == 'A':
                    kb.dve(lambda: nc.vector.memset(Otok, 0.0), writes=[B_Ot])
                for G in range(4 if NA_DBG != 'A' else 0):
                    gi = G % 2
                    kb.dma(SP, qg[gi], qTn_d[seg].rearrange("c p n -> (c p) n")[4 * G * 64:(4 * G + 4) * 64, :].rearrange("(h p) n -> p h n", p=64),
                           reads=[Bq], writes=[B_g[gi]])
                    kb.dma(SP, kg[gi], kTn_d[seg].rearrange("c p n -> (c p) n")[4 * G * 64:(4 * G + 4) * 64, :].rearrange("(h p) n -> p h n", p=64),
                           reads=[Bk], writes=[B_g[gi]])
                    for q5 in range(4):
                        kb.dma(SP, vstg[:, q5 * 5:(q5 + 1) * 5, :],
                               vn_d[seg, q5 * 640:(q5 + 1) * 640, 4 * G * 64:(4 * G + 4) * 64].rearrange("(c p) n -> p c n", p=128),
                               reads=[Bv], writes=[B_vs])
                    kb.dve(lambda: nc.vector.tensor_copy(out=vg[gi][:, :, :, 0:64], in_=vstg.rearrange("p c (h d) -> p c h d", d=64)),
                           reads=[B_vs], writes=[B_g[gi]])
                    for l in range(2 if NA_DBG != 'L2' else 0):
                        kb.dma(SP, tabs[gi][l].rearrange("p a h c -> p (a h c)"), na_tab[l, G], writes=[B_g[gi]])
                    if NA_DBG in ('L', 'L2', 'S', 'S0', 'S1'):
                        kb.dve(lambda: nc.vector.memset(Otok, 0.0), writes=[B_Ot])
                    for lr in range(32 if NA_DBG not in ('L', 'L2') else 0):
                        ws, nch, layout, pi0 = na_window(lr)
                        i = lr % 2
                        sb0 = 3 * i
                        ob = 6 + i
                        scv = PS[:, sb0:sb0 + 3, :].rearrange("p b n -> p (b n)")
                        def f():
                            ins = None
                            for ci in range(nch):
                                for hh in range(4):
                                    col = (ci * 4 + hh) * 64
                                    ins = nc.tensor.matmul(scv[:, col:col + 64],
                                                           lhsT=kg[gi][0:64, hh, (ws + 2 * ci) * 64:(ws + 2 * ci) * 64 + 128],
                                                           rhs=qg[gi][0:64, hh, lr * 64:(lr + 1) * 64], start=True, stop=True)
                            return ins
                        kb.pe(f, reads=[B_g[gi]], writes=[PB[sb0], PB[sb0 + 1], PB[sb0 + 2]])
                        n_el = nch * 256
                        if NA_DBG == 'S0':
                            continue
                        kb.dve(lambda: nc.vector.tensor_tensor(out=scb[i].rearrange("p a n -> p (a n)")[:, 0:n_el], in0=scv[:, 0:n_el],
                                                               in1=tabs[gi][layout][:, pi0:pi0 + nch].rearrange("p a h c -> p (a h c)"),
                                                               op=ALU.add),
                               reads=[PB[sb0], PB[sb0 + 1], PB[sb0 + 2], B_g[gi]], writes=[scb_b[i]])
                        if NA_DBG == 'S1':
                            continue
                        for ci in range(nch):
                            kb.act(lambda: nc.scalar.activation(out=PTn[i][:, ci].rearrange("p h c -> p (h c)"), in_=scb[i][:, ci, :],
                                                                func=AF.Exp, bias=rmask[:, lr, ci:ci + 1]),
                                   reads=[scb_b[i], B_rm], writes=[PTn_b[i]])
                        if NA_DBG == 'S':
                            continue
                        ov = PS[0:64, ob, 0:260].rearrange("p (h n) -> p h n", n=65)
                        def f2():
                            ins = None
                            for hh in range(4):
                                for ci in range(nch):
                                    ins = nc.tensor.matmul(ov[:, hh, :], lhsT=PTn[i][:, ci, hh, :], rhs=vg[gi][:, ws // 2 + ci, hh, 0:65],
                                                           start=(ci == 0), stop=(ci == nch - 1))
                            return ins
                        kb.pe(f2, reads=[PTn_b[i], B_g[gi]], writes=[PB[ob]])
                        kb.dve(lambda: nc.vector.reciprocal(out=rec[:, 4 * i:4 * i + 4], in_=ov[:, :, 64]), reads=[PB[ob]], writes=[rec_b])
                        pr = (lr % 2) * 64
                        for hh in range(4):
                            kb.dve(lambda: nc.vector.tensor_scalar(out=Otok[pr:pr + 64, lr // 2, (4 * G + hh) * 64:(4 * G + hh + 1) * 64],
                                                                   in0=ov[:, hh, 0:64], scalar1=rec[:, 4 * i + hh:4 * i + hh + 1],
                                                                   scalar2=None, op0=ALU.mult),
                                   reads=[PB[ob], rec_b], writes=[B_Ot])
            kb.barrier()
            if NA_DBG == 'O':
                with ExitStack() as esd:
                    tcp = esd.enter_context(sbt("tcpo", [128, D], F32))[:]
                    bcp = Buf()
                    for t in range(NT):
                        kb.act(lambda: nc.scalar.activation(out=tcp, in_=Otok[:, t, :], func=AF.Copy), reads=[B_Ot], writes=[bcp])
                        kb.dma(SP, y_out[seg, t * 128:(t + 1) * 128, :], tcp, reads=[bcp], writes=[Buf()])
                kb.barrier()
            with ExitStack() as es4:
                OT = es4.enter_context(sbt("OTn", [128, 8, S], BF16))[:]
                B_OT = Buf()
                for t in range(NT):
                    pbank = t % 2
                    pv = transpose_to(Otok[:, t, :], B_Ot, 8, pbank, None, None)
                    kb.act(lambda: nc.scalar.activation(out=OT[:, :, t * 128:(t + 1) * 128], in_=pv.rearrange("p (c n) -> p c n", c=8),
                                                        func=AF.Copy), reads=[PB[pbank]], writes=[B_OT])
                kb.barrier()
                stage_out(es4, layer, seg, OT, B_OT, na_w_o[j], (128, 8))
        kb.barrier()

    def weights_to_bf16(layer):
        for r in range(8):
            kb.dma(POOL, w_in_bf[r * 128:(r + 1) * 128, :], ffn_w_in[layer, r * 128:(r + 1) * 128, :],
                   reads=[], writes=[B_win])
        for r in range(8):
            kb.dma(POOL, w_out_bf[r * 512:(r + 1) * 512, :], ffn_w_out[layer, r * 512:(r + 1) * 512, :],
                   reads=[], writes=[B_wout])

    def stage_ffn(layer, seg):
        TB = 512
        NB = S // TB
        with ExitStack() as es:
            sb = lambda name, shape, dt=BF16: es.enter_context(sbt(name, shape, dt))[:]
            halo = sb("halo", [128, 8, 2])
            B_halo = Buf()
            if seg == 0:
                kb.dve(lambda: nc.vector.memset(halo, 0.0), writes=[B_halo])
            else:
                cand = sb("cand", [8, D])
                selm = sb("selm", [8, 2])
                B_cand = Buf()
                kb.dma(SP, cand, cc_out_h, reads=[B_cc["out_h"]], writes=[B_cand])
                kb.dma(POOL, selm, halo_sel, writes=[B_cand])
                def f():
                    ins = None
                    for c in range(8):
                        ins = nc.tensor.matmul(PS[:, 0, 2 * c:2 * c + 2], lhsT=cand[:, c * 128:(c + 1) * 128], rhs=selm,
                                               start=True, stop=True)
                    return ins
                kb.pe(f, reads=[B_cand], writes=[PB[0]])
                kb.dve(lambda: nc.vector.tensor_copy(out=halo, in_=PS[:, 0, 0:16].rearrange("p (c n) -> p c n", n=2)),
                       reads=[PB[0]], writes=[B_halo])
            cw = sb("cw", [128, 3, 32], F32)
            cbias = sb("cbias", [128, 32], F32)
            B_cw = Buf()
            kb.dma(SP, cw, ffn_conv_w[layer], writes=[B_cw])
            kb.dma(SP, cbias, ffn_conv_b[layer], writes=[B_cw])
            w_gate = sb("w_gate", [128, 8, D])
            w_proj = sb("w_proj", [128, 2, D])
            B_wg = Buf()
            kb.dma(POOL, w_gate, ple_w_gate[layer].rearrange("(c p) n -> p c n", p=128), writes=[B_wg])
            kb.dma(POOL, w_proj, ple_w_proj[layer].rearrange("(c p) n -> p c n", p=128), writes=[B_wg])
            gfpost, gfpost_b = load_gain(es, "gfpost", g_ffn_post[layer:layer + 1, :])
            gple, gple_b = load_gain(es, "gple", g_ple[layer:layer + 1, :])
            if layer + 1 < nlayers:
                gnext, gnext_b = load_gain(es, "gnext", g_mix_pre[layer + 1:layer + 2, :])
            xfb = [sb("xfb%d" % i, [128, 8, TB + 2]) for i in range(2)]
            xfb_b = [Buf(), Buf()]
            hT = sb("hT", [128, 32, TB])
            B_hT = Buf()
            win = [sb("win%d" % i, [128, 8, 256]) for i in range(3)]
            win_b = [Buf() for _ in range(3)]
            wout = [sb("wout%d" % i, [128, 32, 256]) for i in range(2)]
            wout_b = [Buf(), Buf()]
            gs = [sb("gs%d" % i, [128, TB + 2], F32) for i in range(2)]
            gs_b = [Buf(), Buf()]
            ta = [sb("ta%d" % i, [128, TB], F32) for i in range(2)]
            ta_b = [Buf(), Buf()]
            tb_ = [sb("tb%d" % i, [128, TB], F32) for i in range(2)]
            tb_b = [Buf(), Buf()]
            yblk = [sb("yblk%d" % i, [128, D], F32) for i in range(4)]
            yblk_b = [Buf() for _ in range(4)]
            ESET = []
            for e_ in range(2):
                ESET.append(dict(tl=alloc_norm_tiles(es, "f%d" % e_), xs=sb("xf%d" % e_, [128, D], F32), xs_b=Buf(),
                                 x2=sb("x2f%d" % e_, [128, D], F32), x2_b=Buf(), pt=sb("pt%d" % e_, [128, PLE]), pt_b=Buf(),
                                 pT=sb("pT%d" % e_, [128, 2, 128]), pT_b=Buf(), gate=sb("gate%d" % e_, [128, D], F32), gate_b=Buf(),
                                 st=sb("st3%d" % e_, [128, 32], F32), st_b=Buf()))
            def rms_steps(src_ap, src_bufs, n, scr, scr_b, col, st3, st3_b):
                kb.act(lambda: nc.scalar.activation(out=scr[:, 0:n], in_=src_ap, func=AF.Square, accum_out=st3[:, col:col + 1]),
                       reads=src_bufs, writes=[st3_b, scr_b])
                yield
                kb.act(lambda: nc.scalar.activation(out=st3[:, col + 1:col + 2], in_=st3[:, col:col + 1], func=AF.Sqrt, scale=1.0 / n, bias=EPS),
                       reads=[st3_b], writes=[st3_b])
                yield
                kb.dve(lambda: nc.vector.reciprocal(out=st3[:, col + 2:col + 3], in_=st3[:, col + 1:col + 2]), reads=[st3_b], writes=[st3_b])
                yield

            def epi(blk, tt):
                t = blk * (TB // 128) + tt
                ts = slice(t * 128, (t + 1) * 128)
                E_ = ESET[tt % 2]
                xs_, xs_b_, x2_, x2_b_, pt_, pt_b_, pT_, pT_b_, gate_, gate_b_ = (E_["xs"], E_["xs_b"], E_["x2"], E_["x2_b"], E_["pt"], E_["pt_b"],
                                                                              E_["pT"], E_["pT_b"], E_["gate"], E_["gate_b"])
                tl = E_["tl"]
                st3, st3_b = E_["st"], E_["st_b"]
                BK = 4 * (tt % 2)
                scr, scr_b = tl[0], tl[1]
                xn, xn_b = tl[4], tl[5]
                xT, xT_b = tl[6], tl[7]
                y = yblk[tt]
                kb.dma(SP, xs_, xres[seg, ts, :], reads=[B_xres[seg][t]], writes=[xs_b_])
                yield
                kb.dma(POOL, pt_, p_in[layer, seg, ts, :], writes=[pt_b_])
                yield
                yield from rms_steps(y, [yblk_b[tt]], D, scr, scr_b, 0, st3, st3_b)
                kb.dve(lambda: nc.vector.scalar_tensor_tensor(out=scr, in0=y, scalar=st3[:, 2:3], in1=gfpost[:], op0=ALU.mult, op1=ALU.mult),
                       reads=[yblk_b[tt], st3_b, gfpost_b], writes=[scr_b])
                yield
                kb.dve(lambda: nc.vector.tensor_tensor(out=x2_, in0=scr, in1=xs_, op=ALU.add), reads=[scr_b, xs_b_], writes=[x2_b_])
                yield
                yield from rms_steps(x2_, [x2_b_], D, scr, scr_b, 4, st3, st3_b)
                kb.dve(lambda: nc.vector.tensor_scalar(out=xn, in0=x2_, scalar1=st3[:, 6:7], scalar2=None, op0=ALU.mult),
                       reads=[x2_b_, st3_b], writes=[xn_b])
                yield
                yield 'pe'
                pv = transpose_to(xn, xn_b, 8, BK, None, None)
                yield
                kb.act(lambda: nc.scalar.activation(out=xT, in_=pv, func=AF.Copy), reads=[PB[BK]], writes=[xT_b])
                yield
                def fp():
                    ins = None
                    for c in range(2):
                        ins = nc.tensor.transpose(ps_bf(BK + 1)[:, c * 128:(c + 1) * 128], pt_[:, c * 128:(c + 1) * 128], ident)
                    return ins
                kb.pe(fp, reads=[pt_b_, B_const], writes=[PB[BK + 1]])
                yield
                kb.act(lambda: nc.scalar.activation(out=pT_, in_=ps_bf(BK + 1)[:, 0:256].rearrange("p (c n) -> p c n", c=2), func=AF.Copy),
                       reads=[PB[BK + 1]], writes=[pT_b_])
                yield
                for half in range(2):
                    bkx = BK + 2 + half
                    hs_ = slice(half * 512, (half + 1) * 512)
                    def fg():
                        ins = None
                        for c in range(8):
                            ins = nc.tensor.matmul(PS[:, bkx, :], lhsT=xT[:, c * 128:(c + 1) * 128], rhs=w_gate[:, c, hs_],
                                                   start=(c == 0), stop=(c == 7))
                        return ins
                    if half == 0:
                        yield 'pe'
                    kb.pe(fg, reads=[xT_b, B_wg], writes=[PB[bkx]])
                    yield
                    kb.act(lambda: nc.scalar.activation(out=gate_[:, hs_], in_=PS[:, bkx, :], func=AF.Sigmoid),
                           reads=[PB[bkx]], writes=[gate_b_])
                    yield
                for half in range(2):
                    bkx = BK + 2 + half
                    hs_ = slice(half * 512, (half + 1) * 512)
                    def fe():
                        ins = None
                        for c in range(2):
                            ins = nc.tensor.matmul(PS[:, bkx, :], lhsT=pT_[:, c, :], rhs=w_proj[:, c, hs_], start=(c == 0), stop=(c == 1))
                        return ins
                    if half == 0:
                        yield 'pe'
                    kb.pe(fe, reads=[pT_b_, B_wg], writes=[PB[bkx]])
                    yield
                    kb.dve(lambda: nc.vector.tensor_tensor(out=gate_[:, hs_], in0=gate_[:, hs_], in1=PS[:, bkx, :], op=ALU.mult),
                           reads=[PB[bkx], gate_b_], writes=[gate_b_])
                    yield
                yield from rms_steps(gate_, [gate_b_], D, scr, scr_b, 8, st3, st3_b)
                kb.dve(lambda: nc.vector.scalar_tensor_tensor(out=scr, in0=gate_, scalar=st3[:, 10:11], in1=gple[:], op0=ALU.mult, op1=ALU.mult),
                       reads=[gate_b_, st3_b, gple_b], writes=[scr_b])
                yield
                kb.dve(lambda: nc.vector.tensor_tensor(out=xs_, in0=scr, in1=x2_, op=ALU.add), reads=[scr_b, x2_b_], writes=[xs_b_])
                yield
                if layer + 1 < nlayers:
                    kb.dma(POOL, xres[seg, ts, :], xs_, reads=[xs_b_], writes=[B_xres[seg][t]])
                    yield
                    yield from rms_steps(xs_, [xs_b_], D, scr, scr_b, 12, st3, st3_b)
                    kb.dve(lambda: nc.vector.scalar_tensor_tensor(out=xn, in0=xs_, scalar=st3[:, 14:15], in1=gnext[:], op0=ALU.mult, op1=ALU.mult),
                           reads=[xs_b_, st3_b, gnext_b], writes=[xn_b])
                    yield
                    yield 'pe'
                    pv2 = transpose_to(xn, xn_b, 8, BK, None, None)
                    yield
                    kb.act(lambda: nc.scalar.activation(out=xT, in_=pv2, func=AF.Copy), reads=[PB[BK]], writes=[xT_b])
                    yield
                    kb.dma(POOL, xnT_d[seg, :, :, t * 128:(t + 1) * 128].rearrange("c p n -> p c n"),
                           xT.rearrange("p (c n) -> p c n", c=8), reads=[xT_b], writes=[B_xnT[seg][t]])
                    yield
                else:
                    kb.dma(POOL, y_out[seg, ts, :], xs_, reads=[xs_b_], writes=[B_xres[seg][t]])
                    yield

            pending = iter(())
            wk = 0
            wo_k = 0
            for blk in range(NB):
                c0 = blk * TB
                xi = blk % 2
                xfT = xfb[xi]
                B_xf = xfb_b[xi]
                lo = c0 - 1 if blk > 0 else c0
                hi = c0 + TB + 1 if blk < NB - 1 else c0 + TB
                kb.dma(SP, xfT[:, :, (lo - (c0 - 1)):(hi - (c0 - 1))], xfT_d[seg, :, :, lo:hi].rearrange("c p n -> p c n"),
                       reads=B_xfT[seg], writes=[B_xf])
                if blk == 0:
                    kb.dve(lambda: nc.vector.tensor_copy(out=xfT[:, :, 0:1], in_=halo[:, :, 0:1]), reads=[B_halo], writes=[B_xf])
                if blk == NB - 1:
                    kb.dve(lambda: nc.vector.tensor_copy(out=xfT[:, :, TB + 1:TB + 2], in_=halo[:, :, 1:2]), reads=[B_halo], writes=[B_xf])
                for ch in range(32):
                    wi = wk % 3
                    wk += 1
                    kb.dma(SP, win[wi][:, :, 0:128], w_in_bf[:, ch * 128:(ch + 1) * 128].rearrange("(c p) n -> p c n", p=128),
                           reads=[B_win], writes=[win_b[wi]])
                    kb.dma(SP, win[wi][:, :, 128:256],
                           w_in_bf[:, DFF + ch * 128:DFF + (ch + 1) * 128].rearrange("(c p) n -> p c n", p=128),
                           reads=[B_win], writes=[win_b[wi]])
                    i = ch % 2
                    bg, bh, bu = 3 * i, 3 * i + 1, 3 * i + 2
                    def f():
                        ins = None
                        for c in range(8):
                            nc.tensor.matmul(PS[:, bg, :], lhsT=win[wi][:, c, 0:128], rhs=xfT[:, c, 0:TB],
                                             start=(c == 0), stop=(c == 7))
                        for c in range(8):
                            nc.tensor.matmul(PS[:, bh, 0:2], lhsT=win[wi][:, c, 0:128], rhs=xfT[:, c, TB:TB + 2],
                                             start=(c == 0), stop=(c == 7))
                        for c in range(8):
                            ins = nc.tensor.matmul(PS[:, bu, :], lhsT=win[wi][:, c, 128:256], rhs=xfT[:, c, 1:1 + TB],
                                                   start=(c == 0), stop=(c == 7))
                        return ins
                    kb.pe(f, reads=[win_b[wi], B_xf], writes=[PB[bg], PB[bh], PB[bu]])
                    kb.act(lambda: nc.scalar.activation(out=gs[i][:, 0:TB], in_=PS[:, bg, :], func=AF.Copy),
                           reads=[PB[bg]], writes=[gs_b[i]])
                    kb.act(lambda: nc.scalar.activation(out=gs[i][:, TB:TB + 2], in_=PS[:, bh, 0:2], func=AF.Copy),
                           reads=[PB[bh]], writes=[gs_b[i]])
                    kb.dve(lambda: nc.vector.tensor_scalar(out=ta[i], in0=gs[i][:, 1:TB + 1], scalar1=cw[:, 1, ch:ch + 1],
                                                           scalar2=cbias[:, ch:ch + 1], op0=ALU.mult, op1=ALU.add),
                           reads=[gs_b[i], B_cw], writes=[ta_b[i]])
                    kb.dve(lambda: nc.vector.scalar_tensor_tensor(out=tb_[i], in0=gs[i][:, 0:TB], scalar=cw[:, 0, ch:ch + 1],
                                                                  in1=ta[i], op0=ALU.mult, op1=ALU.add),
                           reads=[gs_b[i], B_cw, ta_b[i]], writes=[tb_b[i]])
                    kb.dve(lambda: nc.vector.scalar_tensor_tensor(out=ta[i], in0=gs[i][:, 2:TB + 2], scalar=cw[:, 2, ch:ch + 1],
                                                                  in1=tb_[i], op0=ALU.mult, op1=ALU.add),
                           reads=[gs_b[i], B_cw, tb_b[i]], writes=[ta_b[i]])
                    kb.act(lambda: nc.scalar.activation(out=tb_[i], in_=ta[i], func=AF.Gelu_apprx_tanh),
                           reads=[ta_b[i]], writes=[tb_b[i]])
                    kb.dve(lambda: nc.vector.tensor_tensor(out=hT[:, ch, :], in0=tb_[i], in1=PS[:, bu, :], op=ALU.mult),
                           reads=[tb_b[i], PB[bu]], writes=[B_hT])
                for q4 in range(4):
                    wo = wo_k % 2
                    wo_k += 1
                    kb.dma(SP, wout[wo], w_out_bf[:, q4 * 256:(q4 + 1) * 256].rearrange("(c p) n -> p c n", p=128),
                           reads=[B_wout], writes=[wout_b[wo]])
                    for tt in range(TB // 128):
                        by = 6 + (tt % 2)
                        def f():
                            ins = None
                            for c in range(32):
                                ins = nc.tensor.matmul(PS[:, by, 0:256], lhsT=hT[:, c, tt * 128:(tt + 1) * 128], rhs=wout[wo][:, c, :],
                                                       start=(c == 0), stop=(c == 31))
                            return ins
                        kb.pe(f, reads=[B_hT, wout_b[wo]], writes=[PB[by]])
                        kb.act(lambda: nc.scalar.activation(out=yblk[tt][:, q4 * 256:(q4 + 1) * 256], in_=PS[:, by, 0:256], func=AF.Copy),
                               reads=[PB[by]], writes=[yblk_b[tt]])
                for pair in range(TB // 256):
                    gens = [epi(blk, 2 * pair), epi(blk, 2 * pair + 1)]
                    alive = [True, True]
                    while any(alive):
                        for gi_ in range(2):
                            if alive[gi_]:
                                v_ = next(gens[gi_], 'end')
                                while v_ == 'pe':
                                    v_ = next(gens[gi_], 'end')
                                if v_ == 'end':
                                    alive[gi_] = False
        kb.barrier()

    stage_initial()
    for layer in range(nlayers):
        kind, j = layer % 3, layer // 3
        for seg in (1, 0):
            if kind == 0:
                mixer_mla(layer, seg, j)
            elif kind == 1:
                mixer_gqa(layer, seg, j)
            else:
                mixer_na(layer, seg, j)
            if seg == 1:
                weights_to_bf16(layer)
        if STOP_MIX and layer == nlayers - 1 and NA_DBG == 'O':
            break
        if STOP_MIX and layer == nlayers - 1:
            with ExitStack() as esd:
                tcp = esd.enter_context(sbt("tcp", [128, D], F32))[:]
                bcp = Buf()
                for seg in (0, 1):
                    for t in range(NT):
                        kb.dma(SP, tcp, xres[seg, t * 128:(t + 1) * 128, :], reads=[B_xres[seg][t]], writes=[bcp])
                        kb.dma(SP, y_out[seg, t * 128:(t + 1) * 128, :], tcp, reads=[bcp], writes=[B_xres[seg][t]])
            break
        for seg in (0, 1):
            stage_ffn(layer, seg)
    kb.barrier()
    return nc


def _na_window(lr):
    if lr < 4:
        lo, hi = lr - 4, 7
    elif lr >= 28:
        lo, hi = 24, lr + 3
    else:
        lo, hi = lr - 4, lr + 3
    blo, bhi = lo + 4, hi + 4
    ws = blo - (blo % 2)
    nrows = bhi - ws + 1
    nch = (nrows + 1) // 2
    rho0 = ws - lr + 3
    layout = 0 if rho0 % 2 == 0 else 1
    pi0 = rho0 // 2
    return ws, nch, layout, pi0


def _rope_tables(pos, dim):
    inv = 1.0 / (10000.0 ** (np.arange(0, dim, 2, dtype=np.float32) / np.float32(dim)))
    ang = pos.astype(np.float32)[:, None] * inv[None, :].astype(np.float32)
    return np.cos(ang).astype(np.float32), np.sin(ang).astype(np.float32)


_NC_CACHE = {}


def kernel(_nlayers=DEPTH, **inputs):
    inp = {k: np.ascontiguousarray(np.asarray(v)) for k, v in inputs.items()}
    if _nlayers not in _NC_CACHE:
        _NC_CACHE[_nlayers] = build_program(_nlayers)
    nc = _NC_CACHE[_nlayers]
    f32 = np.float32
    w_uq = inp["mla_w_uq"]
    rp = np.empty((2, 384, 512), f32)
    for h in range(8):
        base = h * 192 + 128
        idx = base + (np.arange(64) + 32) % 64
        rp[:, :, h * 64:(h + 1) * 64] = w_uq[:, :, idx]
    shared = {k: inp[k] for k in ["norm_mix_pre", "norm_mix_post", "norm_ffn_pre", "norm_ffn_post", "ple_norm",
                                  "mla_w_down", "mla_q_norm", "mla_kv_norm", "mla_w_uq", "mla_w_ukv", "mla_w_o",
                                  "gqa_w_qkv", "gqa_q_norm", "gqa_k_norm", "gqa_w_o", "na_w_qkv", "na_w_o",
                                  "ffn_w_in", "ffn_w_out", "ple_w_proj", "ple_w_gate"]}
    shared["ffn_conv_wT"] = np.ascontiguousarray(inp["ffn_conv_w"].reshape(DEPTH, 3, 32, 128).transpose(0, 3, 1, 2))
    shared["ffn_conv_bT"] = np.ascontiguousarray(inp["ffn_conv_b"].reshape(DEPTH, 32, 128).transpose(0, 2, 1))
    shared["mla_w_uq_rp"] = rp
    shared["ident"] = np.eye(128, dtype=f32)
    rpb = inp["na_rpb"][0]
    cq = np.arange(64)
    c0 = np.clip(cq - 8, 0, 48)
    kcs = np.arange(64)
    valid = (kcs[:, None] >= c0[None, :]) & (kcs[:, None] < c0[None, :] + 16)
    relc = np.clip(kcs[:, None] - cq[None, :] + 15, 0, 30)
    T2 = np.full((16, 17, 64, 64), NEG, f32)
    for rho in range(15):
        T2[:, rho] = np.where(valid[None], rpb[:, rho][:, relc], f32(NEG))
    na_tab = np.empty((2, 128, 8, 16, 64), f32)
    for layout in range(2):
        for k in range(8):
            for jj in range(2):
                rho = 2 * k + jj + layout
                na_tab[layout, jj * 64:(jj + 1) * 64, k] = T2[:, rho].transpose(1, 0, 2)
    shared["na_tab"] = np.ascontiguousarray(
        na_tab.reshape(2, 128, 8, 4, 4, 64).transpose(0, 3, 1, 2, 4, 5).reshape(2, 4, 128, 8 * 4 * 64))
    in_maps = []
    scale = f32(192.0 ** -0.5)
    for core in range(8):
        grp, qtr = core // 4, core % 4
        m = dict(shared)
        m["x_in"] = np.stack([inp["x_prompt"][core], inp["x_sample"][grp, qtr * S:(qtr + 1) * S]], 0)
        m["p_in"] = np.stack([inp["p_prompt"][:, core], inp["p_sample"][:, grp, qtr * S:(qtr + 1) * S]], 1)
        cs_tok = np.zeros((2, S, 64), f32)
        cs_feat = np.zeros((2, 2, 64, S), f32)
        gq_tok = np.zeros((2, S, 128), f32)
        for seg in range(2):
            pos = np.arange(S) + (0 if seg == 0 else qtr * S)
            c, s = _rope_tables(pos, 64)
            cs_tok[seg, :, 0:32] = c
            cs_tok[seg, :, 32:64] = s
            cs_feat[seg, 0] = np.concatenate([c, c], 1).T * scale
            cs_feat[seg, 1] = np.concatenate([-s, s], 1).T * scale
            rc, rs = _rope_tables(pos // 64, 64)
            cc, cs_ = _rope_tables(pos % 64, 64)
            gq_tok[seg] = np.concatenate([rc, rs, cc, cs_], 1)
        m["mla_cs_tok"] = cs_tok
        m["mla_cs_feat"] = cs_feat
        m["gqa_cs_tok"] = gq_tok
        hs = np.zeros((8, 2), f32)
        if qtr > 0:
            hs[2 * (qtr - 1) + 1, 0] = 1.0
        if qtr < 3:
            hs[2 * (qtr + 1), 1] = 1.0
        m["halo_sel"] = hs
        rm = np.full((2, 128, 32, 6), NEG, f32)
        for seg in range(2):
            base_row = 0 if seg == 0 else qtr * 32
            R = 32 if seg == 0 else 128
            for lr in range(32):
                ws, nch, layout, pi0 = _na_window(lr)
                rq = base_row + lr
                r0 = min(max(rq - 4, 0), R - 8)
                for ci in range(nch):
                    for jj in range(2):
                        rg = base_row + (ws + 2 * ci + jj - 4)
                        if r0 <= rg <= r0 + 7:
                            rm[seg, jj * 64:(jj + 1) * 64, lr, ci] = 0.0
        m["na_rmask"] = rm
        hsel = np.zeros((128, 8), f32)
        if qtr > 0:
            hsel[:, qtr - 1] = 1.0
        if qtr < 3:
            hsel[:, 4 + qtr + 1] = 1.0
        m["na_hsel"] = hsel
        in_maps.append(m)
    res = run_bass_kernel_spmd(nc, in_maps, core_ids=list(range(8)))
    outs = [r["y_out"] for r in res.results]
    y_prompt = np.stack([outs[c][0] for c in range(8)], 0).astype(f32)
    y_sample = np.stack([np.concatenate([outs[g * 4 + q][1] for q in range(4)], 0) for g in range(2)], 0).astype(f32)
    return (y_prompt, y_sample)
```
